# Optimizing a Trainium2 kernel written in Bass

```python
import math
import jax, jax.numpy as jnp
from jax import lax
import numpy as np


D_MODEL = 2048
BATCH = 4
SEQ = 8192
DEPTH = 2

GRID_W = 64
HEAD_DIM = 128
N_HEADS = D_MODEL // HEAD_DIM
N_KV_HEADS = N_HEADS // 4
GQA_GROUP = N_HEADS // N_KV_HEADS
ATTN_WIDTH = N_HEADS * HEAD_DIM
KV_WIDTH = N_KV_HEADS * HEAD_DIM
ATTN_IN_WIDTH = ATTN_WIDTH + 2 * KV_WIDTH + ATTN_WIDTH
ROPE_AXIS_DIM = HEAD_DIM // 2
ROPE_THETA = 10000.0
Q_BLOCK = 128
FOURIER_WIDTH = D_MODEL
FOURIER_GROUPS = 8
FOURIER_GROUP_W = FOURIER_WIDTH // FOURIER_GROUPS
FOURIER_IN_WIDTH = 2 * FOURIER_WIDTH
N_MIXERS = 2
N_ATTN_LAYERS = (DEPTH + 1) // 2
N_FOURIER_LAYERS = DEPTH // 2
EPS = 1e-6

kernel_name = "hybrid_gqa_axial_rope_fnet_adaln_encoder"


def rms_norm(x, gain):
    x32 = x.astype(jnp.float32)
    y = x32 * lax.rsqrt(jnp.mean(x32 * x32, axis=-1, keepdims=True) + EPS)
    return y.astype(x.dtype) * gain


def axial_rope_tables(seq_len):
    rows = seq_len // GRID_W
    row_ids = jnp.repeat(jnp.arange(rows), GRID_W).astype(jnp.float32)
    col_ids = jnp.tile(jnp.arange(GRID_W), rows).astype(jnp.float32)
    inv_freq = ROPE_THETA ** (-jnp.arange(0, ROPE_AXIS_DIM, 2, dtype=jnp.float32) / ROPE_AXIS_DIM)
    ang = jnp.concatenate([row_ids[:, None] * inv_freq[None, :],
                           col_ids[:, None] * inv_freq[None, :]], axis=-1)
    return jnp.cos(ang), jnp.sin(ang)


def apply_rope(x, cos, sin):
    xf = x.astype(jnp.float32).reshape(*x.shape[:-1], HEAD_DIM // 2, 2)
    x1, x2 = xf[..., 0], xf[..., 1]
    c = cos[None, :, None, :]
    s = sin[None, :, None, :]
    out = jnp.stack([x1 * c - x2 * s, x1 * s + x2 * c], axis=-1)
    return out.reshape(x.shape).astype(x.dtype)


def attention_mixer(h, w_in, q_gain, k_gain, w_out):
    B, S, _ = h.shape
    proj = h @ w_in
    q = proj[..., :ATTN_WIDTH].reshape(B, S, N_HEADS, HEAD_DIM)
    k = proj[..., ATTN_WIDTH:ATTN_WIDTH + KV_WIDTH].reshape(B, S, N_KV_HEADS, HEAD_DIM)
    v = proj[..., ATTN_WIDTH + KV_WIDTH:ATTN_WIDTH + 2 * KV_WIDTH].reshape(B, S, N_KV_HEADS, HEAD_DIM)
    gate = proj[..., ATTN_WIDTH + 2 * KV_WIDTH:]
    q = rms_norm(q, q_gain)
    k = rms_norm(k, k_gain)
    cos, sin = axial_rope_tables(S)
    q = apply_rope(q, cos, sin)
    k = apply_rope(k, cos, sin)
    scale = 1.0 / math.sqrt(HEAD_DIM)
    k32 = k.astype(jnp.float32)
    n_blocks = S // Q_BLOCK
    qb = q.reshape(B, n_blocks, Q_BLOCK, N_KV_HEADS, GQA_GROUP, HEAD_DIM).transpose(1, 0, 2, 3, 4, 5)

    def attend(q_blk):
        s = jnp.einsum('bqkgd,bskd->bkgqs', q_blk.astype(jnp.float32), k32) * scale
        p = jax.nn.softmax(s, axis=-1)
        return jnp.einsum('bkgqs,bskd->bqkgd', p.astype(v.dtype), v)

    o = lax.map(attend, qb)
    o = o.transpose(1, 0, 2, 3, 4, 5).reshape(B, S, ATTN_WIDTH)
    return (o * jax.nn.silu(gate)) @ w_out


def fourier_mixer(h, w_in, w_out):
    B, S, _ = h.shape
    proj = h @ w_in
    u = proj[..., :FOURIER_WIDTH]
    gate = proj[..., FOURIER_WIDTH:]
    ug = u.astype(jnp.float32).reshape(B, S, FOURIER_GROUPS, FOURIER_GROUP_W)
    f = jnp.fft.fft2(ug, axes=(1, 3), norm="ortho").real
    f = f.reshape(B, S, FOURIER_WIDTH).astype(h.dtype)
    return (f * jax.nn.silu(gate)) @ w_out


def setup_inputs(seed: int = 0) -> dict:
    key = jax.random.key(seed)
    ks = jax.random.split(key, 12)
    f32 = jnp.float32
    D = D_MODEL
    x = jax.random.normal(ks[0], (BATCH, SEQ, D), f32)
    c = jax.random.normal(ks[1], (BATCH, D), f32)
    norm_g = 1.0 + 0.02 * jax.random.normal(ks[2], (DEPTH, D), f32)
    ada_w = jax.random.normal(ks[3], (DEPTH, D, 3 * D), f32) * D ** -0.5
    ada_b = 0.02 * jax.random.normal(ks[4], (DEPTH, 3 * D), f32)
    attn_w_in = jax.random.normal(ks[5], (N_ATTN_LAYERS, D, ATTN_IN_WIDTH), f32) * D ** -0.5
    attn_q_gain = 1.0 + 0.02 * jax.random.normal(ks[6], (N_ATTN_LAYERS, HEAD_DIM), f32)
    attn_k_gain = 1.0 + 0.02 * jax.random.normal(ks[7], (N_ATTN_LAYERS, HEAD_DIM), f32)
    attn_w_out = jax.random.normal(ks[8], (N_ATTN_LAYERS, ATTN_WIDTH, D), f32) * ATTN_WIDTH ** -0.5
    fourier_w_in = jax.random.normal(ks[9], (N_FOURIER_LAYERS, D, FOURIER_IN_WIDTH), f32) * D ** -0.5
    fourier_w_out = jax.random.normal(ks[10], (N_FOURIER_LAYERS, FOURIER_WIDTH, D), f32) * FOURIER_WIDTH ** -0.5
    final_g = 1.0 + 0.02 * jax.random.normal(ks[11], (D,), f32)
    return {"x": x, "c": c, "norm_g": norm_g, "ada_w": ada_w, "ada_b": ada_b,
            "attn_w_in": attn_w_in, "attn_q_gain": attn_q_gain, "attn_k_gain": attn_k_gain,
            "attn_w_out": attn_w_out, "fourier_w_in": fourier_w_in, "fourier_w_out": fourier_w_out,
            "final_g": final_g}


def reference(x, c, norm_g, ada_w, ada_b, attn_w_in, attn_q_gain, attn_k_gain,
              attn_w_out, fourier_w_in, fourier_w_out, final_g):
    D = D_MODEL
    c_act = jax.nn.silu(c)
    for i in range(DEPTH):
        mod = c_act @ ada_w[i] + ada_b[i]
        shift = mod[:, None, :D]
        scale = mod[:, None, D:2 * D]
        gate = mod[:, None, 2 * D:]
        h = rms_norm(x, norm_g[i]) * (1.0 + scale) + shift
        j = i // N_MIXERS
        if i % N_MIXERS == 0:
            y = attention_mixer(h, attn_w_in[j], attn_q_gain[j], attn_k_gain[j], attn_w_out[j])
        else:
            y = fourier_mixer(h, fourier_w_in[j], fourier_w_out[j])
        x = x + gate * y
    return rms_norm(x, final_g)
```

```python
import math
from contextlib import ExitStack
import numpy as np
import ml_dtypes
import concourse.bass as bass
import concourse.mybir as mybir
from concourse.bass_utils import run_bass_kernel_spmd

F32 = mybir.dt.float32
BF16 = mybir.dt.bfloat16
AF = mybir.ActivationFunctionType
ALU = mybir.AluOpType
NPBF = ml_dtypes.bfloat16

D = 2048
S = 8192
NB = 4
NCORES = 8
TOWN = 4096
HD = 128
EPS = 1e-6
NCH = D // 128
GROUPS = [[0, 4], [1, 5], [2, 6], [3, 7]]


class Buf:
    __slots__ = ("name", "w", "r", "lsem", "ssem")

    def __init__(self, name=""):
        self.name = name
        self.w = None
        self.r = []
        self.lsem = None
        self.ssem = None


class Prog:
    ENG = ("pe", "act", "dve", "pool", "sp")

    def __init__(self):
        self.nc = bass.Bass("TRN2", target_bir_lowering=False)
        self.es = ExitStack()
        self.ops = {e: [] for e in self.ENG}
        self.N = {e: 0 for e in self.ENG}
        self.S = {e: self.es.enter_context(self.nc.semaphore("cnt_" + e)) for e in self.ENG}
        self.waited = {e: {} for e in self.ENG}
        self.semval = {}
        self.nsem = 5
        self.uid = 0
        self.dram = {}
        self.scope = self.es
        self.phase_sems = []
        self.free_sems = []
        self.ccsems = {}

    def name(self, base):
        self.uid += 1
        return f"{base}_{self.uid}"

    def sb(self, name, shape, dt, es=None):
        return (es or self.scope).enter_context(self.nc.sbuf_tensor(self.name(name), list(shape), dt))

    def ps(self, name, shape, dt, es=None):
        return (es or self.scope).enter_context(self.nc.psum_tensor(self.name(name), list(shape), dt))

    def newsem(self, name):
        if self.free_sems:
            s = self.free_sems.pop()
        else:
            self.nsem += 1
            s = self.es.enter_context(self.nc.semaphore(self.name(name)))
            self.semval[s] = 0
        self.phase_sems.append(s)
        return s

    def begin_phase(self):
        self.scope = ExitStack()
        self.phase_sems = []

    def end_phase(self):
        self.barrier()
        self.scope.close()
        self.scope = self.es
        self.free_sems.extend(self.phase_sems)
        self.phase_sems = []

    def dint(self, name, shape, dt):
        return self.nc.dram_tensor(name, list(shape), dt).ap()

    def allgather(self, in_ap, out_ap, reads=(), writes=()):
        q = "pool"
        self._deps(q, reads, writes)
        self.nsem += 1
        sem = self.es.enter_context(self.nc.semaphore(self.name("cc")))
        self.ccsems[sem] = 1
        self.ops[q].append(lambda e: e.collective_compute("AllGather", ALU.bypass, replica_groups=GROUPS,
                                                          ins=[in_ap.opt()], outs=[out_ap.opt()]).then_inc(sem, 1))
        tok = ("dma", sem, 1)
        for b in reads:
            b.r.append(tok)
        for b in writes:
            b.w = tok
            b.r = []
        return tok

    def din(self, name, shape, dt):
        t = self.nc.dram_tensor(name, list(shape), dt, kind="ExternalInput").ap()
        self.dram[name] = t
        return t

    def dout(self, name, shape, dt):
        t = self.nc.dram_tensor(name, list(shape), dt, kind="ExternalOutput").ap()
        self.dram[name] = t
        return t

    def _need(self, eng, tok):
        if tok[0] == "eng":
            _, f, n = tok
            if f == eng and eng == "pe":
                return
            sem, key, val = self.S[f], ("e", f), n
        else:
            _, sem, val = tok
            key = ("d", sem)
        if self.waited[eng].get(key, 0) >= val:
            return
        self.waited[eng][key] = val
        self.ops[eng].append(lambda e, sem=sem, val=val: e.wait_ge(sem, val))

    def _deps(self, eng, reads, writes):
        for b in reads:
            if b.w is not None:
                self._need(eng, b.w)
        for b in writes:
            if b.w is not None:
                self._need(eng, b.w)
            for t in b.r:
                self._need(eng, t)

    def op(self, eng, fn, reads=(), writes=()):
        self._deps(eng, reads, writes)
        self.N[eng] += 1
        tok = ("eng", eng, self.N[eng])
        sem = self.S[eng]
        self.ops[eng].append(lambda e, fn=fn, sem=sem: fn(e).then_inc(sem, 1))
        for b in reads:
            b.r.append(tok)
        for b in writes:
            b.w = tok
            b.r = []
        return tok

    def pe(self, fn, reads=(), writes=()):
        return self.op("pe", fn, reads, writes)

    def act(self, fn, reads=(), writes=()):
        return self.op("act", fn, reads, writes)

    def dve(self, fn, reads=(), writes=()):
        return self.op("dve", fn, reads, writes)

    def pool(self, fn, reads=(), writes=()):
        return self.op("pool", fn, reads, writes)

    def dma(self, out_ap, in_ap, reads=(), writes=(), q="sp", sem=None):
        self._deps(q, reads, writes)
        if sem is None:
            if writes and not reads:
                b = writes[0]
                if b.lsem is None:
                    b.lsem = self.newsem("l")
                sem = b.lsem
            else:
                b = reads[0]
                if b.ssem is None:
                    b.ssem = self.newsem("s")
                sem = b.ssem
        self.semval[sem] += 16
        val = self.semval[sem]
        self.ops[q].append(lambda e, o=out_ap, i=in_ap, sem=sem: e.dma_start(out=o, in_=i).then_inc(sem, 16))
        tok = ("dma", sem, val)
        for b in reads:
            b.r.append(tok)
        for b in writes:
            b.w = tok
            b.r = []
        return tok

    def barrier(self):
        for e in self.ENG:
            for f in self.ENG:
                if f != e and self.N[f] > 0:
                    self._need(e, ("eng", f, self.N[f]))
            for s, v in self.semval.items():
                if v > 0:
                    self._need(e, ("dma", s, v))
            for s, v in self.ccsems.items():
                self._need(e, ("dma", s, v))

    def finish(self):
        for s, v in self.semval.items():
            if v > 0:
                self._need("sp", ("dma", s, v))
        ops = self.ops
        with self.nc.Block() as block:
            @block.tensor
            def _(e):
                for o in ops["pe"]:
                    o(e)

            @block.scalar
            def _(e):
                for o in ops["act"]:
                    o(e)

            @block.vector
            def _(e):
                for o in ops["dve"]:
                    o(e)

            @block.gpsimd
            def _(e):
                for o in ops["pool"]:
                    o(e)

            @block.sync
            def _(e):
                for o in ops["sp"]:
                    o(e)
        self.es.close()
        return self.nc


class Ring:
    def __init__(self, P, name, shape, dt, n, es=None, psum=False):
        self.n = n
        self.i = 0
        alloc = P.ps if psum else P.sb
        self.t = [alloc(name, [128] + list(shape), dt, es) for _ in range(n)]
        self.b = [Buf(f"{name}{k}") for k in range(n)]

    def next(self):
        k = self.i % self.n
        self.i += 1
        return self.t[k], self.b[k]


def emit_consts(P):
    c = {}
    ident_d = P.din("c_ident", [128, 128], BF16)
    ones_d = P.din("c_ones", [128, 128], BF16)
    onesf_d = P.din("c_onesf", [128, 128], F32)
    c["ident"] = P.sb("ident", [128, 128], BF16)
    c["ones"] = P.sb("ones", [128, 128], BF16)
    c["onesf"] = P.sb("onesf", [128, 128], F32)
    c["eps"] = P.sb("eps", [128, 1], F32)
    cb = Buf("consts")
    c["buf"] = cb
    P.dma(c["ident"][:], ident_d, writes=[cb])
    P.dma(c["ones"][:], ones_d, writes=[cb])
    P.dma(c["onesf"][:], onesf_d, writes=[cb])
    eb = Buf("eps")
    P.pool(lambda e: e.memset(c["eps"][:], EPS), writes=[eb])
    c["epsb"] = eb
    return c


def emit_mod(P, C, cvec_d, adaw_d, adab_d, psring, dst_fn):
    with ExitStack() as es:
        cv = P.sb("cv", [128, NCH], F32, es)
        crep = P.sb("crep", [128, NCH, 128], F32, es)
        abrow = P.sb("abrow", [1, 3 * D], F32, es)
        cvb, crb, abb = Buf(), Buf(), Buf()
        P.dma(cv[:], cvec_d, writes=[cvb])
        P.dma(abrow[:], adab_d, writes=[abb])
        P.act(lambda e: e.activation(out=cv[:], in_=cv[:], func=AF.Silu), reads=[cvb], writes=[cvb])
        P.dve(lambda e: e.tensor_copy(crep[:], cv[:].unsqueeze(2).to_broadcast([128, NCH, 128])), reads=[cvb], writes=[crb])
        awr = Ring(P, "aw", [NCH, 512], F32, 2, es)
        awv = adaw_d.rearrange("(j p) n -> p j n", p=128)
        for nt in range(12):
            aw, awb = awr.next()
            P.dma(aw[:], awv[:, :, nt * 512:(nt + 1) * 512], writes=[awb])
            pst, psb = psring.next()
            for j in range(NCH):
                P.pe(lambda e, j=j, aw=aw, pst=pst: e.matmul(pst[:], crep[:, j, :], aw[:, j, :], start=(j == 0), stop=False),
                     reads=[crb, awb], writes=[psb])
            P.pe(lambda e, nt=nt, pst=pst: e.matmul(pst[:], C["onesf"][0:1, :], abrow[0:1, nt * 512:(nt + 1) * 512], start=False, stop=True),
                 reads=[C["buf"], abb], writes=[psb])
            dst, dstb = dst_fn(nt)
            P.act(lambda e, dst=dst, pst=pst: e.activation(out=dst, in_=pst[:], func=AF.Copy), reads=[psb], writes=[dstb])
        P.barrier()


class HBuilder:
    def __init__(self, P, C, shift_bc, gs_bc, modb, ptring):
        self.P, self.C = P, C
        self.shift_bc, self.gs_bc, self.modb = shift_bc, gs_bc, modb
        self.xr = Ring(P, "xt", [D], F32, 2)
        self.hr = Ring(P, "hb", [D], BF16, 2)
        self.ssr = Ring(P, "ss", [1], F32, 2)
        self.sdr = Ring(P, "sd", [1], F32, 2)
        self.ptr = ptring
        self.k = 0

    def tile(self, srcs, hT, hTb, col0):
        P, C = self.P, self.C
        xt, xb = self.xr.next()
        hb, hbb = self.hr.next()
        ss, ssb = self.ssr.next()
        sd, sdb = self.sdr.next()
        for (p0, p1, ap) in srcs:
            P.dma(xt[p0:p1, :], ap, writes=[xb])
        P.pool(lambda e: e.memset(ss[:], 0.0), writes=[ssb])
        P.act(lambda e: e.activation(out=hb[:], in_=xt[:], func=AF.Square, accum_out=ss[:]), reads=[xb], writes=[hbb, ssb])
        P.act(lambda e: e.activation(out=sd[:], in_=ss[:], func=AF.Sqrt, scale=1.0 / D, bias=C["eps"][:, 0:1]),
              reads=[ssb, C["epsb"]], writes=[sdb])
        P.dve(lambda e: e.reciprocal(out=sd[:], in_=sd[:]), reads=[sdb], writes=[sdb])
        P.dve(lambda e: e.scalar_tensor_tensor(out=xt[:], in0=xt[:], scalar=sd[:, 0:1], in1=self.gs_bc, op0=ALU.mult, op1=ALU.mult),
              reads=[xb, sdb, self.modb], writes=[xb])
        P.pool(lambda e: e.tensor_tensor(out=hb[:], in0=xt[:], in1=self.shift_bc, op=ALU.add), reads=[xb, self.modb], writes=[hbb])
        for half in range(2):
            pt, ptb = self.ptr.next()
            for c in range(8):
                ch = half * 8 + c
                P.pe(lambda e, pt=pt, c=c, ch=ch: e.transpose(pt[:, c, :], hb[:, ch * 128:(ch + 1) * 128], C["ident"][:]),
                     reads=[hbb, C["buf"]], writes=[ptb])
            dst = hT[:, half * 8:half * 8 + 8, col0:col0 + 128]
            if (self.k + half) % 2 == 0:
                P.act(lambda e, dst=dst, pt=pt: e.activation(out=dst, in_=pt[:], func=AF.Copy), reads=[ptb], writes=[hTb])
            else:
                P.dve(lambda e, dst=dst, pt=pt: e.tensor_copy(dst, pt[:]), reads=[ptb], writes=[hTb])
        self.k += 1


def emit_A(P, C, T):
    P.begin_phase()
    x_d, cvec_d, g_d, adaw_d, adab_d, win_d = T["x"], T["cvec"], T["norm_g0"], T["ada_w0"], T["ada_b0"], T["w_in0"]
    qg_d, kg_d, rc_d, rs_d = T["qg"], T["kg"], T["ropeC"], T["ropeS"]
    QT_d, SG_d, GB_d = T["QT"], T["SG"], T["gate_bc0"]

    psA = Ring(P, "psA", [512], F32, 3, psum=True)
    psB = Ring(P, "psB", [512], F32, 2, psum=True)
    psT = Ring(P, "psT", [8, 128], BF16, 2, psum=True)

    modp = P.sb("modp", [128, 2 * D], F32)
    modb = Buf("modp")
    gains = P.sb("gains", [128, 2], F32)
    gainb = Buf("gains")
    P.dma(gains[:, 0:1], qg_d, writes=[gainb])
    P.dma(gains[:, 1:2], kg_d, writes=[gainb])

    with ExitStack() as es:
        gtmp = P.sb("gtmp", [128, D], F32, es)
        gbc = P.sb("gbc", [128, D], F32, es)
        gtb, gbb = Buf(), Buf()
        P.dma(gbc[:], g_d.partition_broadcast(128).rearrange("p o n -> p (o n)"), writes=[gbb])

        def dst_fn(nt):
            if nt < 8:
                return modp[:, nt * 512:(nt + 1) * 512], modb
            return gtmp[:, (nt - 8) * 512:(nt - 7) * 512], gtb
        emit_mod(P, C, cvec_d, adaw_d, adab_d, psA, dst_fn)
        P.dma(GB_d, gtmp[:], reads=[gtb])
        P.dve(lambda e: e.scalar_tensor_tensor(out=modp[:, D:2 * D], in0=modp[:, D:2 * D], scalar=1.0, in1=gbc[:], op0=ALU.add, op1=ALU.mult),
              reads=[modb, gbb], writes=[modb])
        P.barrier()

    hb_ = HBuilder(P, C, modp[:, 0:D], modp[:, D:2 * D], modb, psT)
    TH = 2048
    hT = P.sb("hT", [128, NCH, TH], BF16)
    hTb = Buf("hT")
    rC = P.sb("rC", [128, TH], F32)
    rS = P.sb("rS", [128, TH], F32)
    ropeb = Buf("rope")
    wfr = Ring(P, "wf", [NCH, 128], F32, 3)
    wbr = Ring(P, "wb", [NCH, 128], BF16, 2)
    sqr = Ring(P, "sq", [512], BF16, 2)
    sdr = Ring(P, "sdq", [512], F32, 2)
    qnr = Ring(P, "qn", [512], F32, 2)
    t1r = Ring(P, "t1", [512], F32, 2)
    t2r = Ring(P, "t2", [512], F32, 2)
    qor = Ring(P, "qo", [512], BF16, 3)
    vor = Ring(P, "vo", [4, 128], BF16, 2)
    winv = win_d.rearrange("(j p) n -> p j n", p=128)

    for th in range(2):
        tok0 = th * TH
        Vv = T["Vown"][th].rearrange("(n p) c -> p n c", p=128)
        for tt in range(TH // 128):
            r0 = tok0 + tt * 128
            hb_.tile([(0, 128, x_d[r0:r0 + 128, :])], hT, hTb, tt * 128)
        P.dma(rC[:], rc_d[:, tok0:tok0 + TH], writes=[ropeb])
        P.dma(rS[:], rs_d[:, tok0:tok0 + TH], writes=[ropeb])
        for cc in range(40):
            wf, wfb = wfr.next()
            wb, wbb = wbr.next()
            P.dma(wf[:], winv[:, :, cc * 128:(cc + 1) * 128], writes=[wfb])
            P.pool(lambda e, wb=wb, wf=wf: e.tensor_copy(wb[:], wf[:]), reads=[wfb], writes=[wbb])
            if 20 <= cc < 24:
                vc = cc - 20
                for t4 in range(TH // 512):
                    pst, psb = psA.next()
                    for ti in range(4):
                        c0 = t4 * 512 + ti * 128
                        for j in range(NCH):
                            P.pe(lambda e, pst=pst, ti=ti, j=j, c0=c0, wb=wb: e.matmul(pst[:, ti * 128:(ti + 1) * 128], hT[:, j, c0:c0 + 128], wb[:, j, :], start=(j == 0), stop=(j == NCH - 1)),
                                 reads=[hTb, wbb], writes=[psb])
                    vo, vob = vor.next()
                    P.dve(lambda e, vo=vo, pst=pst: e.tensor_copy(vo[:].rearrange("p a b -> p (a b)"), pst[:]), reads=[psb], writes=[vob])
                    n0 = (t4 * 512) // 128
                    P.dma(Vv[:, n0:n0 + 4, vc * 128:(vc + 1) * 128], vo[:], reads=[vob])
                continue
            isqk = cc < 20
            pend = None

            def stage1(st):
                (pst, psb, sq, sqb, t5) = st
                gcol = gains[:, 0:1] if cc < 16 else gains[:, 1:2]
                pB, pBb = psB.next()
                P.pe(lambda e: e.matmul(pB[:], C["ones"][:], sq[:], start=True, stop=True), reads=[C["buf"], sqb], writes=[pBb])
                sd, sdb = sdr.next()
                qn, qnb = qnr.next()
                t1, t1b = t1r.next()
                t2, t2b = t2r.next()
                qo, qob = qor.next()
                P.act(lambda e: e.activation(out=sd[:], in_=pB[:], func=AF.Sqrt, scale=1.0 / HD, bias=C["eps"][:, 0:1]),
                      reads=[pBb, C["epsb"]], writes=[sdb])
                P.dve(lambda e: e.reciprocal(out=sd[:], in_=sd[:]), reads=[sdb], writes=[sdb])
                P.dve(lambda e: e.scalar_tensor_tensor(out=qn[:], in0=pst[:], scalar=gcol, in1=sd[:], op0=ALU.mult, op1=ALU.mult),
                      reads=[psb, sdb, gainb], writes=[qnb])
                cs = slice(t5 * 512, (t5 + 1) * 512)
                P.pool(lambda e: e.tensor_tensor(out=t1[:], in0=qn[:], in1=rC[:, cs], op=ALU.mult), reads=[qnb, ropeb], writes=[t1b])
                P.pool(lambda e: e.tensor_tensor(out=t2[0:64, :], in0=qn[64:128, :], in1=rS[64:128, cs], op=ALU.mult), reads=[qnb, ropeb], writes=[t2b])
                P.pool(lambda e: e.tensor_tensor(out=t2[64:128, :], in0=qn[0:64, :], in1=rS[0:64, cs], op=ALU.mult), reads=[qnb, ropeb], writes=[t2b])
                P.dve(lambda e: e.tensor_tensor(out=qo[:], in0=t1[:], in1=t2[:], op=ALU.add), reads=[t1b, t2b], writes=[qob])
                dstd = QT_d[cc, :, tok0 + t5 * 512: tok0 + (t5 + 1) * 512] if cc < 16 else T["KTown"][(cc - 16) // 2][((cc - 16) % 2) * 128:((cc - 16) % 2) * 128 + 128, tok0 + t5 * 512: tok0 + (t5 + 1) * 512]
                P.dma(dstd, qo[:], reads=[qob])

            for t5 in range(TH // 512):
                pst, psb = psA.next()
                for j in range(NCH):
                    P.pe(lambda e, pst=pst, j=j, t5=t5, wb=wb: e.matmul(pst[:], wb[:, j, :], hT[:, j, t5 * 512:(t5 + 1) * 512], start=(j == 0), stop=(j == NCH - 1)),
                         reads=[hTb, wbb], writes=[psb])
                if isqk:
                    sq, sqb = sqr.next()
                    P.act(lambda e, sq=sq, pst=pst: e.activation(out=sq[:], in_=pst[:], func=AF.Square), reads=[psb], writes=[sqb])
                    if pend is not None:
                        stage1(pend)
                    pend = (pst, psb, sq, sqb, t5)
                else:
                    qo, qob = qor.next()
                    P.act(lambda e, qo=qo, pst=pst: e.activation(out=qo[:], in_=pst[:], func=AF.Silu), reads=[psb], writes=[qob])
                    P.dma(SG_d[cc - 24, :, tok0 + t5 * 512: tok0 + (t5 + 1) * 512], qo[:], reads=[qob])
            if pend is not None:
                stage1(pend)
    P.barrier()
    agb = Buf("ag")
    for j in range(2):
        P.allgather(T["KTown"][j], T["KTag"][j], writes=[agb])
        P.allgather(T["Vown"][j], T["Vag"][j], writes=[agb])
    P.end_phase()


def host_consts():
    return {
        "c_ident": np.eye(128, dtype=np.float32).astype(NPBF),
        "c_ones": np.ones((128, 128), np.float32).astype(NPBF),
        "c_onesf": np.ones((128, 128), np.float32),
    }


def rope_tables():
    t = np.arange(S)
    rows = (t // 64).astype(np.float32)
    cols = (t % 64).astype(np.float32)
    inv = (np.float32(10000.0) ** (-np.arange(0, 64, 2, dtype=np.float32) / np.float32(64))).astype(np.float32)
    ang = np.concatenate([rows[:, None] * inv[None, :], cols[:, None] * inv[None, :]], axis=-1)
    c = np.cos(ang).T.astype(np.float32)
    s = np.sin(ang).T.astype(np.float32)
    return np.concatenate([c, c], 0), np.concatenate([s, -s], 0)


PERM = np.concatenate([np.arange(0, 128, 2), np.arange(1, 128, 2)])


def emit_B(P, C, T):
    P.begin_phase()
    QT_d, SG_d, OG_d = T["QT"], T["SG"], T["OGT"]
    psS = Ring(P, "psS", [512], F32, 4, psum=True)
    psO = Ring(P, "psO", [512], F32, 2, psum=True)
    psL = Ring(P, "psL", [512], F32, 2, psum=True)
    kr = Ring(P, "kt", [S], BF16, 2)
    vr = Ring(P, "vt", [64, 128], BF16, 2)
    qr = Ring(P, "qt", [TOWN], BF16, 2)
    sgr = Ring(P, "sg", [TOWN], BF16, 2)
    pr = Ring(P, "p", [512], BF16, 4)
    rlr = Ring(P, "rl", [512], F32, 2)
    o1r = Ring(P, "o1", [512], F32, 2)
    ogr = Ring(P, "og", [512], BF16, 2)
    scale = 1.0 / math.sqrt(HD)
    NKB = S // 128
    pend = []

    def front(st):
        pS, pSb = psS.next()
        p, pb = pr.next()
        kt, ktb, qt_, qb, qi, kb = st["kt"], st["ktb"], st["qt"], st["qb"], st["qi"], st["kb"]
        P.pe(lambda e: e.matmul(pS[:], kt[:, kb * 128:(kb + 1) * 128], qt_[:, qi * 512:(qi + 1) * 512], start=True, stop=True),
             reads=[ktb, qb], writes=[pSb])
        P.act(lambda e: e.activation(out=p[:], in_=pS[:], func=AF.Exp, scale=scale), reads=[pSb], writes=[pb])
        st["p"], st["pb"] = p, pb

    def back(st):
        p, pb, vt, vtb, kb = st["p"], st["pb"], st["vt"], st["vtb"], st["kb"]
        pO, pOb, pL, pLb = st["pO"], st["pOb"], st["pL"], st["pLb"]
        P.pe(lambda e: e.matmul(pO[:], vt[:, kb, :], p[:], start=(kb == 0), stop=(kb == NKB - 1)), reads=[vtb, pb], writes=[pOb])
        P.pe(lambda e: e.matmul(pL[:], C["ones"][:], p[:], start=(kb == 0), stop=(kb == NKB - 1)), reads=[C["buf"], pb], writes=[pLb])
        if kb == NKB - 1:
            h, qi, sg, sgb = st["h"], st["qi"], st["sg"], st["sgb"]
            rl, rlb = rlr.next()
            o1, o1b = o1r.next()
            og, ogb = ogr.next()
            P.dve(lambda e: e.reciprocal(out=rl[:], in_=pL[:]), reads=[pLb], writes=[rlb])
            P.dve(lambda e: e.tensor_tensor(out=o1[:], in0=pO[:], in1=rl[:], op=ALU.mult), reads=[pOb, rlb], writes=[o1b])
            P.pool(lambda e: e.tensor_tensor(out=og[:], in0=o1[:], in1=sg[:, qi * 512:(qi + 1) * 512], op=ALU.mult), reads=[o1b, sgb], writes=[ogb])
            P.dma(OG_d[h, :, qi * 512:(qi + 1) * 512], og[:], reads=[ogb])

    for g in range(4):
        kt, ktb = kr.next()
        vt, vtb = vr.next()
        P.dma(kt[:].rearrange("p (r t) -> p r t", r=2),
              T["KTag"][g // 2].rearrange("(r k p) t -> p k r t", r=2, k=2)[:, g % 2], writes=[ktb])
        for j in range(2):
            for r in range(2):
                P.dma(vt[:, r * 32 + j * 16:r * 32 + j * 16 + 16, :],
                      T["Vag"][j][r * 2048:(r + 1) * 2048, g * 128:(g + 1) * 128].rearrange("(kk p) d -> p kk d", p=128), writes=[vtb])
        for hh in range(4):
            h = 4 * g + hh
            qt_, qb = qr.next()
            sg, sgb = sgr.next()
            P.dma(qt_[:], QT_d[h], writes=[qb])
            P.dma(sg[:], SG_d[h], writes=[sgb])
            for qi in range(TOWN // 512):
                pO, pOb = psO.next()
                pL, pLb = psL.next()
                for kb in range(NKB):
                    st = dict(kt=kt, ktb=ktb, vt=vt, vtb=vtb, qt=qt_, qb=qb, sg=sg, sgb=sgb, h=h, qi=qi, kb=kb,
                              pO=pO, pOb=pOb, pL=pL, pLb=pLb)
                    front(st)
                    pend.append(st)
                    if len(pend) > 2:
                        back(pend.pop(0))
    while pend:
        back(pend.pop(0))
    P.end_phase()


def emit_CF(P, C, T, final):
    P.begin_phase()
    if final:
        w_d, x_d, gb_d = T["w_out1"], T["x1"], T["gate_bc1"]
        sg_d, fg_d, out_d = T["SG1"], T["final_g"], T["out"]
        xv = x_d.rearrange("(a b) m -> b a m", b=64)
        ov = out_d.rearrange("(a b) m -> b a m", b=64)
        sel = P.sb("sel", [128, 2], F32)
        selb = Buf("sel")
        P.dma(sel[:], T["sel"], writes=[selb])
    else:
        w_d, x_d, gb_d = T["w_out0"], T["x"], T["gate_bc0"]
        OG_d, out_d = T["OGT"], T["x1"]
    psY = Ring(P, "psY", [512], F32, 6 if final else 8, psum=True)
    wbf = P.sb("wbf", [128, NCH, D], BF16)
    wbb = Buf("wbf")
    gbc = P.sb("gbc", [128, D], F32)
    gbb = Buf("gbc")
    P.dma(gbc[:], gb_d, writes=[gbb])
    wfr = Ring(P, "wf", [NCH, 128], F32, 3)
    wv = w_d.rearrange("(j p) n -> p j n", p=128)
    for cc in range(16):
        wf, wfb = wfr.next()
        P.dma(wf[:], wv[:, :, cc * 128:(cc + 1) * 128], writes=[wfb])
        P.pool(lambda e, wf=wf, cc=cc: e.tensor_copy(wbf[:, :, cc * 128:(cc + 1) * 128], wf[:]), reads=[wfb], writes=[wbb])
    xr = Ring(P, "xt", [D], F32, 2)
    tr = Ring(P, "tmp", [D], F32, 2)
    if final:
        psT = Ring(P, "psT", [8, 128], BF16, 2, psum=True)
        fgb_t = P.sb("fgbc", [128, D], F32)
        fgb = Buf("fg")
        P.dma(fgb_t[:], fg_d.partition_broadcast(128).rearrange("p o n -> p (o n)"), writes=[fgb])
        fr = Ring(P, "ft", [D], BF16, 2)
        b0r = Ring(P, "b0", [1024], BF16, 2)
        b1r = Ring(P, "b1", [1024], BF16, 2)
        sgr = Ring(P, "sgt", [D], BF16, 2)
        obr = Ring(P, "ogb", [D], BF16, 2)
        otr = Ring(P, "ogT", [NCH, 128], BF16, 2)
        ssr = Ring(P, "ss", [1], F32, 2)
        sdr = Ring(P, "sd", [1], F32, 2)
        jr = Ring(P, "junk", [D], BF16, 1)
    else:
        ogr = Ring(P, "og", [NCH, 512], BF16, 2)
        OGv = OG_d.rearrange("c p t -> p c t")

    for tt in range(TOWN // 128):
        r0 = tt * 128
        if final:
            ft, ftb = fr.next()
            sg, sgb = sgr.next()
            ob, obb = obr.next()
            oT, oTb = otr.next()
            b0, b0b = b0r.next()
            b1, b1b = b1r.next()
            fob = Buf("fo")
            P.dma(ft[:, 0:1024], T["f_own"][r0:r0 + 128, :], writes=[fob])
            fj, fr0 = tt // 8, (tt % 8) * 128
            P.dma(b0[:], T["fag"][fj][fr0:fr0 + 128, :], writes=[b0b])
            P.dma(b1[:], T["fag"][fj][1024 + fr0:1024 + fr0 + 128, :], writes=[b1b])
            P.dma(sg[:], sg_d[r0:r0 + 128, :], writes=[sgb])
            P.pool(lambda e, b1=b1: e.tensor_scalar(out=b1[:], in0=b1[:], scalar1=sel[:, 1:2], scalar2=None, op0=ALU.mult), reads=[b1b, selb], writes=[b1b])
            P.dve(lambda e, ft=ft, b0=b0, b1=b1: e.scalar_tensor_tensor(out=ft[:, 1024:2048], in0=b0[:], scalar=sel[:, 0:1], in1=b1[:], op0=ALU.mult, op1=ALU.add),
                  reads=[b0b, b1b, selb, fob], writes=[ftb])
            P.pool(lambda e, ob=ob, ft=ft, sg=sg: e.tensor_tensor(out=ob[:], in0=ft[:], in1=sg[:], op=ALU.mult), reads=[ftb, fob, sgb], writes=[obb])
            for half in range(2):
                pt, ptb = psT.next()
                for c in range(8):
                    ch = half * 8 + c
                    P.pe(lambda e, pt=pt, c=c, ch=ch, ob=ob: e.transpose(pt[:, c, :], ob[:, ch * 128:(ch + 1) * 128], C["ident"][:]),
                         reads=[obb, C["buf"]], writes=[ptb])
                P.act(lambda e, pt=pt, oT=oT, half=half: e.activation(out=oT[:, half * 8:half * 8 + 8, :], in_=pt[:], func=AF.Copy), reads=[ptb], writes=[oTb])
            lhs = lambda c, oT=oT: oT[:, c, :]
            lb = oTb
        else:
            if tt % 4 == 0:
                og, ogb = ogr.next()
                P.dma(og[:], OGv[:, :, r0:r0 + 512], writes=[ogb])
            ti = tt % 4
            lhs = lambda c, og=og, ti=ti: og[:, c, ti * 128:(ti + 1) * 128]
            lb = ogb
        xt, xb = xr.next()
        tmp, tmpb = tr.next()
        if final:
            P.dma(xt[0:64, :], xv[2 * tt], writes=[xb])
            P.dma(xt[64:128, :], xv[2 * tt + 1], writes=[xb])
        else:
            P.dma(xt[:], x_d[r0:r0 + 128, :], writes=[xb])
        for n in range(4):
            ps, psb = psY.next()
            for c in range(NCH):
                P.pe(lambda e, ps=ps, c=c, n=n, lhs=lhs: e.matmul(ps[:], lhs(c), wbf[:, c, n * 512:(n + 1) * 512], start=(c == 0), stop=(c == NCH - 1)),
                     reads=[lb, wbb], writes=[psb])
            P.dve(lambda e, ps=ps, n=n, tmp=tmp: e.tensor_tensor(out=tmp[:, n * 512:(n + 1) * 512], in0=ps[:], in1=gbc[:, n * 512:(n + 1) * 512], op=ALU.mult),
                  reads=[psb, gbb], writes=[tmpb])
        P.pool(lambda e, xt=xt, tmp=tmp: e.tensor_tensor(out=xt[:], in0=xt[:], in1=tmp[:], op=ALU.add), reads=[xb, tmpb], writes=[xb])
        if final:
            ss, ssb = ssr.next()
            sd, sdb = sdr.next()
            jk, jkb = jr.next()
            P.pool(lambda e, ss=ss: e.memset(ss[:], 0.0), writes=[ssb])
            P.act(lambda e, jk=jk, xt=xt, ss=ss: e.activation(out=jk[:], in_=xt[:], func=AF.Square, accum_out=ss[:]), reads=[xb], writes=[jkb, ssb])
            P.act(lambda e, sd=sd, ss=ss: e.activation(out=sd[:], in_=ss[:], func=AF.Sqrt, scale=1.0 / D, bias=C["eps"][:, 0:1]),
                  reads=[ssb, C["epsb"]], writes=[sdb])
            P.dve(lambda e, sd=sd: e.reciprocal(out=sd[:], in_=sd[:]), reads=[sdb], writes=[sdb])
            P.dve(lambda e, xt=xt, sd=sd: e.scalar_tensor_tensor(out=xt[:], in0=xt[:], scalar=sd[:, 0:1], in1=fgb_t[:], op0=ALU.mult, op1=ALU.mult),
                  reads=[xb, sdb, fgb], writes=[xb])
        if final:
            P.dma(ov[2 * tt], xt[0:64, :], reads=[xb])
            P.dma(ov[2 * tt + 1], xt[64:128, :], reads=[xb])
        else:
            P.dma(out_d[r0:r0 + 128, :], xt[:], reads=[xb])
    P.end_phase()


def emit_D(P, C, T):
    P.begin_phase()
    x_d, cvec_d, g_d, adaw_d, adab_d, win_d = T["x1"], T["cvec"], T["norm_g1"], T["ada_w1"], T["ada_b1"], T["w_in1"]
    SG_d, GB_d = T["SG1"], T["gate_bc1"]
    psA = Ring(P, "psA", [512], F32, 4, psum=True)
    psT = Ring(P, "psT", [8, 128], BF16, 2, psum=True)
    modp = P.sb("modp", [128, 2 * D], F32)
    modb = Buf("modp")
    with ExitStack() as es:
        gtmp = P.sb("gtmp", [128, D], F32, es)
        gbc = P.sb("gbc", [128, D], F32, es)
        gtb, gbb = Buf(), Buf()
        P.dma(gbc[:], g_d.partition_broadcast(128).rearrange("p o n -> p (o n)"), writes=[gbb])

        def dst_fn(nt):
            if nt < 8:
                return modp[:, nt * 512:(nt + 1) * 512], modb
            return gtmp[:, (nt - 8) * 512:(nt - 7) * 512], gtb
        emit_mod(P, C, cvec_d, adaw_d, adab_d, psA, dst_fn)
        P.dma(GB_d, gtmp[:], reads=[gtb])
        P.dve(lambda e: e.scalar_tensor_tensor(out=modp[:, D:2 * D], in0=modp[:, D:2 * D], scalar=1.0, in1=gbc[:], op0=ALU.add, op1=ALU.mult),
              reads=[modb, gbb], writes=[modb])
        P.barrier()
    hb_ = HBuilder(P, C, modp[:, 0:D], modp[:, D:2 * D], modb, psT)
    TH = 2048
    hT = P.sb("hT", [128, NCH, TH], BF16)
    hTb = Buf("hT")
    wfr = Ring(P, "wf", [NCH, 128], F32, 3)
    wbr = Ring(P, "wb", [NCH, 128], BF16, 2)
    uor = Ring(P, "uo", [512], BF16, 3)
    gor = Ring(P, "go", [4, 128], BF16, 3)
    winv = win_d.rearrange("(j p) n -> p j n", p=128)
    xv = x_d.rearrange("(a b) m -> b a m", b=64)
    SGv = SG_d.rearrange("(n p) c -> p n c", p=128)
    for th in range(2):
        tok0 = th * TH
        for tt in range(TH // 128):
            t2a = (tok0 + tt * 128) // 64
            hb_.tile([(0, 64, xv[t2a]), (64, 128, xv[t2a + 1])], hT, hTb, tt * 128)
        for cc in range(32):
            wf, wfb = wfr.next()
            wb, wbb = wbr.next()
            P.dma(wf[:], winv[:, :, cc * 128:(cc + 1) * 128], writes=[wfb])
            P.pool(lambda e, wb=wb, wf=wf: e.tensor_copy(wb[:], wf[:]), reads=[wfb], writes=[wbb])
            for t5 in range(TH // 512):
                pst, psb = psA.next()
                if cc < 16:
                    for j in range(NCH):
                        P.pe(lambda e, pst=pst, j=j, t5=t5, wb=wb: e.matmul(pst[:], wb[:, j, :], hT[:, j, t5 * 512:(t5 + 1) * 512], start=(j == 0), stop=(j == NCH - 1)),
                             reads=[hTb, wbb], writes=[psb])
                    uo, uob = uor.next()
                    if t5 % 2 == 0:
                        P.act(lambda e, uo=uo, pst=pst: e.activation(out=uo[:], in_=pst[:], func=AF.Copy), reads=[psb], writes=[uob])
                    else:
                        P.dve(lambda e, uo=uo, pst=pst: e.tensor_copy(uo[:], pst[:]), reads=[psb], writes=[uob])
                    if cc < 8:
                        udst = T["UTown"][cc, :, tok0 + t5 * 512: tok0 + (t5 + 1) * 512]
                    else:
                        udst = T["UTsend"][(cc - 8) // 2][((cc - 8) % 2) * 128:((cc - 8) % 2) * 128 + 128, tok0 + t5 * 512: tok0 + (t5 + 1) * 512]
                    P.dma(udst, uo[:], reads=[uob])
                else:
                    for ti in range(4):
                        c0 = t5 * 512 + ti * 128
                        for j in range(NCH):
                            P.pe(lambda e, pst=pst, ti=ti, j=j, c0=c0, wb=wb: e.matmul(pst[:, ti * 128:(ti + 1) * 128], hT[:, j, c0:c0 + 128], wb[:, j, :], start=(j == 0), stop=(j == NCH - 1)),
                                 reads=[hTb, wbb], writes=[psb])
                    go, gob = gor.next()
                    P.act(lambda e, go=go, pst=pst: e.activation(out=go[:].rearrange("p a b -> p (a b)"), in_=pst[:], func=AF.Silu), reads=[psb], writes=[gob])
                    n0 = (tok0 + t5 * 512) // 128
                    P.dma(SGv[:, n0:n0 + 4, (cc - 16) * 128:(cc - 15) * 128], go[:], reads=[gob])
    P.barrier()
    agb = Buf("ag")
    for j in range(4):
        P.allgather(T["UTsend"][j], T["UTag"][j], writes=[agb])
    P.end_phase()


def fourier_tables():
    w = np.arange(256)
    aw = 2 * np.pi * np.outer(w, w) / 256
    csw = np.concatenate([np.cos(aw), np.sin(aw)], 1) / 16.0
    csw = csw.reshape(2, 128, 512).transpose(1, 0, 2)
    t1 = np.arange(128)
    a1 = 2 * np.pi * np.outer(t1, t1) / 128
    nrm = 1.0 / math.sqrt(8192.0)
    ra = np.concatenate([np.cos(a1), np.sin(a1)], 1) * nrm
    rb = np.concatenate([-np.sin(a1), np.cos(a1)], 1) * nrm
    m = np.arange(128)
    t2 = m % 64
    c2 = m // 64
    atw = 2 * np.pi * np.outer(t2, np.arange(128)) / 8192
    ct, st = np.cos(atw), np.sin(atw)
    n = np.arange(128)
    c2n, k2 = n // 64, n % 64
    a2 = 2 * np.pi * np.outer(t2, k2) / 64
    delta = (c2[:, None] == c2n[None, :]).astype(np.float64)
    re = delta * np.cos(a2)
    rf = -delta * np.sin(a2)
    return {
        "t_csw": np.ascontiguousarray(csw).astype(np.float32).astype(NPBF),
        "t_ra": ra.astype(np.float32).astype(NPBF),
        "t_rb": rb.astype(np.float32).astype(NPBF),
        "t_ct": ct.astype(np.float32),
        "t_st": st.astype(np.float32),
        "t_re": re.astype(np.float32).astype(NPBF),
        "t_rf": rf.astype(np.float32).astype(NPBF),
    }


def emit_E(P, C, T):
    P.begin_phase()
    csw_d, ra_d, rb_d, ct_d, st_d, re_d, rf_d = T["t_csw"], T["t_ra"], T["t_rb"], T["t_ct"], T["t_st"], T["t_re"], T["t_rf"]
    sel = P.sb("sel", [128, 2], F32)
    selb = Buf("sel")
    P.dma(sel[:], T["sel"], writes=[selb])
    csw = P.sb("csw", [128, 2, 512], BF16)
    ra = P.sb("ra", [128, 256], BF16)
    rb = P.sb("rb", [128, 256], BF16)
    ct = P.sb("ct", [128, 128], F32)
    st = P.sb("st", [128, 128], F32)
    re = P.sb("re", [128, 128], BF16)
    rf = P.sb("rf", [128, 128], BF16)
    tb = Buf("tables")
    for dst, src in ((csw, csw_d), (ra, ra_d), (rb, rb_d), (ct, ct_d), (st, st_d), (re, re_d), (rf, rf_d)):
        P.dma(dst[:], src, writes=[tb])
    ps1 = Ring(P, "ps1", [512], F32, 2, psum=True)
    ps2 = Ring(P, "ps2", [512], F32, 2, psum=True)
    ps3 = Ring(P, "ps3", [512], F32, 2, psum=True)
    AB = P.sb("AB", [128, 512, 64], BF16)
    ABb = Buf("AB")
    G2 = P.sb("G2", [128, 64, 256], BF16)
    G2b = Buf("G2")
    fS = P.sb("fS", [128, 64, 128], BF16)
    fSb = Buf("fS")
    ur = Ring(P, "u", [2, 8, 128], BF16, 3)
    upb_ = [Buf("up0"), Buf("up1"), Buf("up2")]
    b0r = Ring(P, "ub0", [2, 512], BF16, 2)
    b1r = Ring(P, "ub1", [2, 512], BF16, 2)
    efr = Ring(P, "ef", [512], F32, 2)
    m1r = Ring(P, "m1", [2, 128], F32, 2)
    m2r = Ring(P, "m2", [2, 128], F32, 2)
    m3r = Ring(P, "m3", [2, 128], F32, 2)
    m4r = Ring(P, "m4", [2, 128], F32, 2)
    fov = T["f_own"].rearrange("(t a e) c -> e t a c", t=64, a=32, e=2)
    fsv = [T["fsend"][j].rearrange("(t a e) c -> e t a c", t=16, a=32, e=2) for j in range(4)]
    uk = 0
    ctb = ct[:].unsqueeze(1).to_broadcast([128, 2, 128])
    stb = st[:].unsqueeze(1).to_broadcast([128, 2, 128])
    k = 0
    for g in range(4):
        for t8 in range(8):
            u, ub = ur.next()
            upb = upb_[uk % 3]
            uk += 1
            b0, b0b = b0r.next()
            b1, b1b = b1r.next()
            c0 = t8 * 512
            for kc in range(2):
                P.dma(u[:, kc, :, 0:64], T["UTown"][2 * g + kc, :, c0:c0 + 512].rearrange("p (i t) -> p i t", t=64), writes=[ub])
            agv = T["UTag"][g].rearrange("(r k p) t -> p r k t", r=2, k=2)
            P.dma(b0[:], agv[:, 0, :, c0:c0 + 512], writes=[b0b])
            P.dma(b1[:], agv[:, 1, :, c0:c0 + 512], writes=[b1b])
            P.pool(lambda e, b1=b1: e.tensor_scalar(out=b1[:], in0=b1[:], scalar1=sel[:, 1:2], scalar2=None, op0=ALU.mult), reads=[b1b, selb], writes=[b1b])
            P.dve(lambda e, u=u, b0=b0, b1=b1: e.scalar_tensor_tensor(out=u[:, :, :, 64:128], in0=b0[:].rearrange("p k (i t) -> p k i t", t=64), scalar=sel[:, 0:1],
                                                                     in1=b1[:].rearrange("p k (i t) -> p k i t", t=64), op0=ALU.mult, op1=ALU.add),
                  reads=[b0b, b1b, selb], writes=[upb])
            for ti in range(8):
                t2 = t8 * 8 + ti
                ps, psb = ps1.next()
                for kc in range(2):
                    P.pe(lambda e, ps=ps, u=u, kc=kc, ti=ti: e.matmul(ps[:], u[:, kc, ti, :], csw[:, kc, :], start=(kc == 0), stop=(kc == 1)),
                         reads=[ub, upb, tb], writes=[psb])
                k += 1
                if k % 2 == 0:
                    P.act(lambda e, ps=ps, t2=t2: e.activation(out=AB[:, :, t2], in_=ps[:], func=AF.Copy), reads=[psb], writes=[ABb])
                else:
                    P.dve(lambda e, ps=ps, t2=t2: e.tensor_copy(AB[:, :, t2], ps[:]), reads=[psb], writes=[ABb])
        for half in range(2):
            for pb in range(32):
                ps, psb = ps2.next()
                for pi in range(2):
                    pl = 2 * pb + pi
                    ch = half * 128 + 2 * pl
                    P.pe(lambda e, ps=ps, pi=pi, ch=ch: e.matmul(ps[:, pi * 256:(pi + 1) * 256], AB[:, ch:ch + 2, :].rearrange("p a b -> p (a b)"), ra[:], start=True, stop=False),
                         reads=[ABb, tb], writes=[psb])
                    P.pe(lambda e, ps=ps, pi=pi, ch=ch: e.matmul(ps[:, pi * 256:(pi + 1) * 256], AB[:, 256 + ch:256 + ch + 2, :].rearrange("p a b -> p (a b)"), rb[:], start=False, stop=True),
                         reads=[ABb, tb], writes=[psb])
                ef, efb = efr.next()
                P.act(lambda e, ef=ef, ps=ps: e.activation(out=ef[:], in_=ps[:], func=AF.Copy), reads=[psb], writes=[efb])
                efv = ef[:].rearrange("p (a b) -> p a b", a=2)
                Ev, Fv = efv[:, :, 0:128], efv[:, :, 128:256]
                m1, m1b = m1r.next()
                m2, m2b = m2r.next()
                m3, m3b = m3r.next()
                m4, m4b = m4r.next()
                P.pool(lambda e, m1=m1, Ev=Ev: e.tensor_tensor(out=m1[:], in0=Ev, in1=ctb, op=ALU.mult), reads=[efb, tb], writes=[m1b])
                P.pool(lambda e, m2=m2, Fv=Fv: e.tensor_tensor(out=m2[:], in0=Fv, in1=stb, op=ALU.mult), reads=[efb, tb], writes=[m2b])
                P.dve(lambda e, m3=m3, Ev=Ev: e.tensor_tensor(out=m3[:], in0=Ev, in1=stb, op=ALU.mult), reads=[efb, tb], writes=[m3b])
                P.pool(lambda e, m4=m4, Fv=Fv: e.tensor_tensor(out=m4[:], in0=Fv, in1=ctb, op=ALU.mult), reads=[efb, tb], writes=[m4b])
                P.dve(lambda e, m1=m1, m2=m2, pb=pb: e.tensor_tensor(out=G2[:, 2 * pb:2 * pb + 2, 0:128], in0=m1[:], in1=m2[:], op=ALU.subtract),
                      reads=[m1b, m2b], writes=[G2b])
                P.dve(lambda e, m3=m3, m4=m4, pb=pb: e.tensor_tensor(out=G2[:, 2 * pb:2 * pb + 2, 128:256], in0=m3[:], in1=m4[:], op=ALU.add),
                      reads=[m3b, m4b], writes=[G2b])
            for qb in range(16):
                ps, psb = ps3.next()
                for pi in range(4):
                    pl = 4 * qb + pi
                    P.pe(lambda e, ps=ps, pi=pi, pl=pl: e.matmul(ps[:, pi * 128:(pi + 1) * 128], G2[:, pl, 0:128], re[:], start=True, stop=False),
                         reads=[G2b, tb], writes=[psb])
                    P.pe(lambda e, ps=ps, pi=pi, pl=pl: e.matmul(ps[:, pi * 128:(pi + 1) * 128], G2[:, pl, 128:256], rf[:], start=False, stop=True),
                         reads=[G2b, tb], writes=[psb])
                dst = fS[:, :, 8 * qb:8 * qb + 8].rearrange("p k (a c) -> p a c k", a=4, c=2)
                src_fn = lambda ps=ps: ps[:].rearrange("p (a c k) -> p a c k", a=4, c=2)
                if qb % 2 == 0:
                    P.act(lambda e, dst=dst, src_fn=src_fn: e.activation(out=dst, in_=src_fn(), func=AF.Copy), reads=[psb], writes=[fSb])
                else:
                    P.dve(lambda e, dst=dst, src_fn=src_fn: e.tensor_copy(dst, src_fn()), reads=[psb], writes=[fSb])
            c0 = g * 256 + half * 128
            for e_ in range(2):
                P.dma(fov[e_][:, :, c0:c0 + 128], fS[e_ * 64:(e_ + 1) * 64, 0:32, :], reads=[fSb])
                for j in range(4):
                    P.dma(fsv[j][e_][:, :, c0:c0 + 128], fS[e_ * 64 + 16 * j:e_ * 64 + 16 * j + 16, 32:64, :], reads=[fSb])
    P.barrier()
    agb = Buf("ag")
    for j in range(4):
        P.allgather(T["fsend"][j], T["fag"][j], writes=[agb])
    P.end_phase()


def assemble_UTf(ut0, ut1, hf):
    a = np.asarray(ut0)[8 * hf:8 * hf + 8].reshape(8, 128, 64, 64)
    b = np.asarray(ut1)[8 * hf:8 * hf + 8].reshape(8, 128, 64, 64)
    return np.ascontiguousarray(np.concatenate([a, b], axis=3).reshape(8, 128, S))


def build_fused():
    P = Prog()
    T = {}
    I32 = mybir.dt.int32
    T["x"] = P.din("x", [TOWN, D], F32)
    T["cvec"] = P.din("cvec", [128, NCH], F32)
    T["norm_g0"] = P.din("norm_g0", [1, D], F32)
    T["norm_g1"] = P.din("norm_g1", [1, D], F32)
    T["ada_w0"] = P.din("ada_w0", [D, 3 * D], F32)
    T["ada_w1"] = P.din("ada_w1", [D, 3 * D], F32)
    T["ada_b0"] = P.din("ada_b0", [1, 3 * D], F32)
    T["ada_b1"] = P.din("ada_b1", [1, 3 * D], F32)
    T["w_in0"] = P.din("w_in0", [D, 5120], F32)
    T["qg"] = P.din("qg", [128, 1], F32)
    T["kg"] = P.din("kg", [128, 1], F32)
    T["ropeC"] = P.din("ropeC", [128, TOWN], F32)
    T["ropeS"] = P.din("ropeS", [128, TOWN], F32)
    T["w_out0"] = P.din("w_out0", [D, D], F32)
    T["w_in1"] = P.din("w_in1", [D, 4096], F32)
    T["w_out1"] = P.din("w_out1", [D, D], F32)
    T["final_g"] = P.din("final_g", [1, D], F32)
    T["sel"] = P.din("sel", [128, 2], F32)
    T["t_csw"] = P.din("t_csw", [128, 2, 512], BF16)
    T["t_ra"] = P.din("t_ra", [128, 256], BF16)
    T["t_rb"] = P.din("t_rb", [128, 256], BF16)
    T["t_ct"] = P.din("t_ct", [128, 128], F32)
    T["t_st"] = P.din("t_st", [128, 128], F32)
    T["t_re"] = P.din("t_re", [128, 128], BF16)
    T["t_rf"] = P.din("t_rf", [128, 128], BF16)
    T["out"] = P.dout("out", [TOWN, D], F32)
    T["QT"] = P.dint("QT", [16, 128, TOWN], BF16)
    T["SG"] = P.dint("SG", [16, 128, TOWN], BF16)
    T["gate_bc0"] = P.dint("gate_bc0", [128, D], F32)
    T["gate_bc1"] = P.dint("gate_bc1", [128, D], F32)
    T["KTown"] = [P.dint(f"KTown{j}", [256, TOWN], BF16) for j in range(2)]
    T["KTag"] = [P.dint(f"KTag{j}", [512, TOWN], BF16) for j in range(2)]
    T["Vown"] = [P.dint(f"Vown{j}", [2048, 512], BF16) for j in range(2)]
    T["Vag"] = [P.dint(f"Vag{j}", [4096, 512], BF16) for j in range(2)]
    T["OGT"] = P.dint("OGT", [16, 128, TOWN], BF16)
    T["x1"] = P.dint("x1", [TOWN, D], F32)
    T["UTown"] = P.dint("UTown", [8, 128, TOWN], BF16)
    T["UTsend"] = [P.dint(f"UTsend{j}", [256, TOWN], BF16) for j in range(4)]
    T["UTag"] = [P.dint(f"UTag{j}", [512, TOWN], BF16) for j in range(4)]
    T["SG1"] = P.dint("SG1", [TOWN, D], BF16)
    T["f_own"] = P.dint("f_own", [TOWN, 1024], BF16)
    T["fsend"] = [P.dint(f"fsend{j}", [1024, 1024], BF16) for j in range(4)]
    T["fag"] = [P.dint(f"fag{j}", [2048, 1024], BF16) for j in range(4)]
    C = emit_consts(P)
    emit_A(P, C, T)
    emit_B(P, C, T)
    emit_CF(P, C, T, False)
    emit_D(P, C, T)
    emit_E(P, C, T)
    emit_CF(P, C, T, True)
    return P.finish()


def core_tables(hf):
    tb = fourier_tables()
    q = (np.arange(128) + 64 * hf) % 128
    tb["t_ra"] = np.ascontiguousarray(tb["t_ra"][q])
    tb["t_rb"] = np.ascontiguousarray(tb["t_rb"][q])
    n = np.arange(128)
    k2 = ((n % 64) + 32 * hf) % 64
    col = (n // 64) * 64 + k2
    tb["t_re"] = np.ascontiguousarray(tb["t_re"][:, col])
    tb["t_rf"] = np.ascontiguousarray(tb["t_rf"][:, col])
    return tb


def kernel(x, c, norm_g, ada_w, ada_b, attn_w_in, attn_q_gain, attn_k_gain,
           attn_w_out, fourier_w_in, fourier_w_out, final_g):
    x = np.asarray(x)
    c = np.asarray(c)
    norm_g = np.asarray(norm_g)
    ada_w = np.asarray(ada_w)
    ada_b = np.asarray(ada_b)
    w_in = np.asarray(attn_w_in)[0]
    fw_in = np.asarray(fourier_w_in)[0]
    fw_out = np.asarray(fourier_w_out)[0]
    ropeC, ropeS = rope_tables()
    colperm = np.arange(5120)
    for h in range(20):
        colperm[h * 128:(h + 1) * 128] = h * 128 + PERM
    w_in_p = np.ascontiguousarray(w_in[:, colperm])
    consts = host_consts()
    qg = np.ascontiguousarray(np.asarray(attn_q_gain)[0][PERM].reshape(128, 1))
    kg = np.ascontiguousarray(np.asarray(attn_k_gain)[0][PERM].reshape(128, 1))
    per_hf = []
    for hf in range(2):
        ch = np.concatenate([np.arange(1024 * hf, 1024 * hf + 1024), np.arange(1024 * (1 - hf), 1024 * (1 - hf) + 1024)])
        d = dict(core_tables(hf))
        d["w_in1"] = np.ascontiguousarray(np.concatenate([fw_in[:, ch], fw_in[:, 2048 + ch]], axis=1))
        d["w_out1"] = np.ascontiguousarray(fw_out[ch, :])
        d["sel"] = np.ascontiguousarray(np.broadcast_to(np.array([float(hf), float(1 - hf)], np.float32), (128, 2)))
        d["ropeC"] = np.ascontiguousarray(ropeC[:, hf * TOWN:(hf + 1) * TOWN])
        d["ropeS"] = np.ascontiguousarray(ropeS[:, hf * TOWN:(hf + 1) * TOWN])
        per_hf.append(d)
    maps = []
    for core in range(NCORES):
        b, hf = core % 4, core // 4
        m = dict(consts)
        m.update(per_hf[hf])
        m["x"] = np.ascontiguousarray(x[b, hf * TOWN:(hf + 1) * TOWN])
        m["cvec"] = np.ascontiguousarray(c[b].reshape(NCH, 128).T)
        m["norm_g0"] = np.ascontiguousarray(norm_g[0:1])
        m["norm_g1"] = np.ascontiguousarray(norm_g[1:2])
        m["ada_w0"] = ada_w[0]
        m["ada_w1"] = ada_w[1]
        m["ada_b0"] = np.ascontiguousarray(ada_b[0:1])
        m["ada_b1"] = np.ascontiguousarray(ada_b[1:2])
        m["w_in0"] = w_in_p
        m["qg"] = qg
        m["kg"] = kg
        m["w_out0"] = np.asarray(attn_w_out)[0]
        m["final_g"] = np.ascontiguousarray(np.asarray(final_g).reshape(1, D))
        maps.append(m)
    res = run_bass_kernel_spmd(build_fused(), maps, core_ids=list(range(NCORES))).results
    out = np.empty((NB, S, D), np.float32)
    for core in range(NCORES):
        b, hf = core % 4, core // 4
        out[b, hf * TOWN:(hf + 1) * TOWN] = res[core]["out"]
    return out
```

```python
import math
from contextlib import ExitStack
import numpy as np
import ml_dtypes
import concourse.bass as bass
import concourse.mybir as mybir
from concourse.bass_utils import run_bass_kernel_spmd

F32 = mybir.dt.float32
BF16 = mybir.dt.bfloat16
AF = mybir.ActivationFunctionType
ALU = mybir.AluOpType
NPBF = ml_dtypes.bfloat16

D = 2048
S = 8192
NB = 4
NCORES = 8
TOWN = 4096
HD = 128
EPS = 1e-6
NCH = D // 128
GROUPS = [[0, 4], [1, 5], [2, 6], [3, 7]]


class Buf:
    __slots__ = ("name", "w", "r", "lsem", "ssem")

    def __init__(self, name=""):
        self.name = name
        self.w = None
        self.r = []
        self.lsem = None
        self.ssem = None


class Prog:
    ENG = ("pe", "act", "dve", "pool", "sp")

    def __init__(self):
        self.nc = bass.Bass("TRN2", target_bir_lowering=False)
        self.es = ExitStack()
        self.ops = {e: [] for e in self.ENG}
        self.N = {e: 0 for e in self.ENG}
        self.S = {e: self.es.enter_context(self.nc.semaphore("cnt_" + e)) for e in self.ENG}
        self.waited = {e: {} for e in self.ENG}
        self.semval = {}
        self.nsem = 5
        self.uid = 0
        self.dram = {}
        self.scope = self.es
        self.phase_sems = []
        self.free_sems = []
        self.ccsems = {}

    def name(self, base):
        self.uid += 1
        return f"{base}_{self.uid}"

    def sb(self, name, shape, dt, es=None):
        return (es or self.scope).enter_context(self.nc.sbuf_tensor(self.name(name), list(shape), dt))

    def ps(self, name, shape, dt, es=None):
        return (es or self.scope).enter_context(self.nc.psum_tensor(self.name(name), list(shape), dt))

    def newsem(self, name):
        if self.free_sems:
            s = self.free_sems.pop()
        else:
            self.nsem += 1
            s = self.es.enter_context(self.nc.semaphore(self.name(name)))
            self.semval[s] = 0
        self.phase_sems.append(s)
        return s

    def begin_phase(self):
        self.scope = ExitStack()
        self.phase_sems = []

    def end_phase(self):
        self.barrier()
        self.scope.close()
        self.scope = self.es
        self.free_sems.extend(self.phase_sems)
        self.phase_sems = []

    def dint(self, name, shape, dt):
        return self.nc.dram_tensor(name, list(shape), dt).ap()

    def allgather(self, in_ap, out_ap, reads=(), writes=()):
        q = "pool"
        self._deps(q, reads, writes)
        self.nsem += 1
        sem = self.es.enter_context(self.nc.semaphore(self.name("cc")))
        self.ccsems[sem] = 1
        self.ops[q].append(lambda e: e.collective_compute("AllGather", ALU.bypass, replica_groups=GROUPS,
                                                          ins=[in_ap.opt()], outs=[out_ap.opt()]).then_inc(sem, 1))
        tok = ("dma", sem, 1)
        for b in reads:
            b.r.append(tok)
        for b in writes:
            b.w = tok
            b.r = []
        return tok

    def din(self, name, shape, dt):
        t = self.nc.dram_tensor(name, list(shape), dt, kind="ExternalInput").ap()
        self.dram[name] = t
        return t

    def dout(self, name, shape, dt):
        t = self.nc.dram_tensor(name, list(shape), dt, kind="ExternalOutput").ap()
        self.dram[name] = t
        return t

    def _need(self, eng, tok):
        if tok[0] == "eng":
            _, f, n = tok
            if f == eng and eng == "pe":
                return
            sem, key, val = self.S[f], ("e", f), n
        else:
            _, sem, val = tok
            key = ("d", sem)
        if self.waited[eng].get(key, 0) >= val:
            return
        self.waited[eng][key] = val
        self.ops[eng].append(lambda e, sem=sem, val=val: e.wait_ge(sem, val))

    def _deps(self, eng, reads, writes):
        for b in reads:
            if b.w is not None:
                self._need(eng, b.w)
        for b in writes:
            if b.w is not None:
                self._need(eng, b.w)
            for t in b.r:
                self._need(eng, t)

    def op(self, eng, fn, reads=(), writes=()):
        self._deps(eng, reads, writes)
        self.N[eng] += 1
        tok = ("eng", eng, self.N[eng])
        sem = self.S[eng]
        self.ops[eng].append(lambda e, fn=fn, sem=sem: fn(e).then_inc(sem, 1))
        for b in reads:
            b.r.append(tok)
        for b in writes:
            b.w = tok
            b.r = []
        return tok

    def pe(self, fn, reads=(), writes=()):
        return self.op("pe", fn, reads, writes)

    def act(self, fn, reads=(), writes=()):
        return self.op("act", fn, reads, writes)

    def dve(self, fn, reads=(), writes=()):
        return self.op("dve", fn, reads, writes)

    def pool(self, fn, reads=(), writes=()):
        return self.op("pool", fn, reads, writes)

    def dma(self, out_ap, in_ap, reads=(), writes=(), q="sp", sem=None):
        self._deps(q, reads, writes)
        if sem is None:
            if writes and not reads:
                b = writes[0]
                if b.lsem is None:
                    b.lsem = self.newsem("l")
                sem = b.lsem
            else:
                b = reads[0]
                if b.ssem is None:
                    b.ssem = self.newsem("s")
                sem = b.ssem
        self.semval[sem] += 16
        val = self.semval[sem]
        self.ops[q].append(lambda e, o=out_ap, i=in_ap, sem=sem: e.dma_start(out=o, in_=i).then_inc(sem, 16))
        tok = ("dma", sem, val)
        for b in reads:
            b.r.append(tok)
        for b in writes:
            b.w = tok
            b.r = []
        return tok

    def barrier(self):
        for e in self.ENG:
            for f in self.ENG:
                if f != e and self.N[f] > 0:
                    self._need(e, ("eng", f, self.N[f]))
            for s, v in self.semval.items():
                if v > 0:
                    self._need(e, ("dma", s, v))
            for s, v in self.ccsems.items():
                self._need(e, ("dma", s, v))

    def finish(self):
        for s, v in self.semval.items():
            if v > 0:
                self._need("sp", ("dma", s, v))
        ops = self.ops
        with self.nc.Block() as block:
            @block.tensor
            def _(e):
                for o in ops["pe"]:
                    o(e)

            @block.scalar
            def _(e):
                for o in ops["act"]:
                    o(e)

            @block.vector
            def _(e):
                for o in ops["dve"]:
                    o(e)

            @block.gpsimd
            def _(e):
                for o in ops["pool"]:
                    o(e)

            @block.sync
            def _(e):
                for o in ops["sp"]:
                    o(e)
        self.es.close()
        return self.nc


class Ring:
    def __init__(self, P, name, shape, dt, n, es=None, psum=False):
        self.n = n
        self.i = 0
        alloc = P.ps if psum else P.sb
        self.t = [alloc(name, [128] + list(shape), dt, es) for _ in range(n)]
        self.b = [Buf(f"{name}{k}") for k in range(n)]

    def next(self):
        k = self.i % self.n
        self.i += 1
        return self.t[k], self.b[k]


def emit_consts(P):
    c = {}
    ident_d = P.din("c_ident", [128, 128], BF16)
    ones_d = P.din("c_ones", [128, 128], BF16)
    onesf_d = P.din("c_onesf", [128, 128], F32)
    c["ident"] = P.sb("ident", [128, 128], BF16)
    c["ones"] = P.sb("ones", [128, 128], BF16)
    c["onesf"] = P.sb("onesf", [128, 128], F32)
    c["eps"] = P.sb("eps", [128, 1], F32)
    cb = Buf("consts")
    c["buf"] = cb
    P.dma(c["ident"][:], ident_d, writes=[cb])
    P.dma(c["ones"][:], ones_d, writes=[cb])
    P.dma(c["onesf"][:], onesf_d, writes=[cb])
    eb = Buf("eps")
    P.pool(lambda e: e.memset(c["eps"][:], EPS), writes=[eb])
    c["epsb"] = eb
    return c


def emit_mod(P, C, cvec_d, adaw_d, adab_d, psring, dst_fn):
    with ExitStack() as es:
        cv = P.sb("cv", [128, NCH], F32, es)
        crep = P.sb("crep", [128, NCH, 128], F32, es)
        abrow = P.sb("abrow", [1, 3 * D], F32, es)
        cvb, crb, abb = Buf(), Buf(), Buf()
        P.dma(cv[:], cvec_d, writes=[cvb])
        P.dma(abrow[:], adab_d, writes=[abb])
        P.act(lambda e: e.activation(out=cv[:], in_=cv[:], func=AF.Silu), reads=[cvb], writes=[cvb])
        P.dve(lambda e: e.tensor_copy(crep[:], cv[:].unsqueeze(2).to_broadcast([128, NCH, 128])), reads=[cvb], writes=[crb])
        awr = Ring(P, "aw", [NCH, 512], F32, 2, es)
        awv = adaw_d.rearrange("(j p) n -> p j n", p=128)
        for nt in range(12):
            aw, awb = awr.next()
            P.dma(aw[:], awv[:, :, nt * 512:(nt + 1) * 512], writes=[awb])
            pst, psb = psring.next()
            for j in range(NCH):
                P.pe(lambda e, j=j, aw=aw, pst=pst: e.matmul(pst[:], crep[:, j, :], aw[:, j, :], start=(j == 0), stop=False),
                     reads=[crb, awb], writes=[psb])
            P.pe(lambda e, nt=nt, pst=pst: e.matmul(pst[:], C["onesf"][0:1, :], abrow[0:1, nt * 512:(nt + 1) * 512], start=False, stop=True),
                 reads=[C["buf"], abb], writes=[psb])
            dst, dstb = dst_fn(nt)
            P.act(lambda e, dst=dst, pst=pst: e.activation(out=dst, in_=pst[:], func=AF.Copy), reads=[psb], writes=[dstb])
        P.barrier()


class HBuilder:
    def __init__(self, P, C, shift_bc, gs_bc, modb, ptring):
        self.P, self.C = P, C
        self.shift_bc, self.gs_bc, self.modb = shift_bc, gs_bc, modb
        self.xr = Ring(P, "xt", [D], F32, 2)
        self.hr = Ring(P, "hb", [D], BF16, 2)
        self.ssr = Ring(P, "ss", [1], F32, 2)
        self.sdr = Ring(P, "sd", [1], F32, 2)
        self.ptr = ptring
        self.k = 0

    def tile(self, srcs, hT, hTb, col0):
        P, C = self.P, self.C
        xt, xb = self.xr.next()
        hb, hbb = self.hr.next()
        ss, ssb = self.ssr.next()
        sd, sdb = self.sdr.next()
        for (p0, p1, ap) in srcs:
            P.dma(xt[p0:p1, :], ap, writes=[xb])
        P.pool(lambda e: e.memset(ss[:], 0.0), writes=[ssb])
        P.act(lambda e: e.activation(out=hb[:], in_=xt[:], func=AF.Square, accum_out=ss[:]), reads=[xb], writes=[hbb, ssb])
        P.act(lambda e: e.activation(out=sd[:], in_=ss[:], func=AF.Sqrt, scale=1.0 / D, bias=C["eps"][:, 0:1]),
              reads=[ssb, C["epsb"]], writes=[sdb])
        P.dve(lambda e: e.reciprocal(out=sd[:], in_=sd[:]), reads=[sdb], writes=[sdb])
        P.dve(lambda e: e.scalar_tensor_tensor(out=xt[:], in0=xt[:], scalar=sd[:, 0:1], in1=self.gs_bc, op0=ALU.mult, op1=ALU.mult),
              reads=[xb, sdb, self.modb], writes=[xb])
        P.pool(lambda e: e.tensor_tensor(out=hb[:], in0=xt[:], in1=self.shift_bc, op=ALU.add), reads=[xb, self.modb], writes=[hbb])
        for half in range(2):
            pt, ptb = self.ptr.next()
            for c in range(8):
                ch = half * 8 + c
                P.pe(lambda e, pt=pt, c=c, ch=ch: e.transpose(pt[:, c, :], hb[:, ch * 128:(ch + 1) * 128], C["ident"][:]),
                     reads=[hbb, C["buf"]], writes=[ptb])
            dst = hT[:, half * 8:half * 8 + 8, col0:col0 + 128]
            if (self.k + half) % 2 == 0:
                P.act(lambda e, dst=dst, pt=pt: e.activation(out=dst, in_=pt[:], func=AF.Copy), reads=[ptb], writes=[hTb])
            else:
                P.dve(lambda e, dst=dst, pt=pt: e.tensor_copy(dst, pt[:]), reads=[ptb], writes=[hTb])
        self.k += 1


class WStream:
    def __init__(self, P, w_d):
        self.P = P
        self.wv = w_d.rearrange("(j p) n -> p j n", p=128)
        self.wfr = Ring(P, "wf", [4, 512], F32, 4)
        self.wbr = Ring(P, "wb", [NCH, 512], BF16, 2)
        self.pending = {}
        self.k = 0

    def dma(self, g):
        wb, wbb = self.wbr.next()
        st = []
        for jq in range(4):
            wf, wfb = self.wfr.next()
            self.P.dma(wf[:], self.wv[:, 4 * jq:4 * jq + 4, g * 512:(g + 1) * 512], writes=[wfb])
            st.append((wf, wfb))
        self.pending[g] = (wb, wbb, st)

    def casts(self, g):
        P = self.P
        wb, wbb, st = self.pending.pop(g)
        self.k += 1
        for jq, (wf, wfb) in enumerate(st):
            dst = wb[:, 4 * jq:4 * jq + 4, :]
            if self.k % 2 == 0:
                P.dve(lambda e, dst=dst, wf=wf: e.tensor_copy(dst, wf[:]), reads=[wfb], writes=[wbb])
            else:
                P.act(lambda e, dst=dst, wf=wf: e.activation(out=dst, in_=wf[:], func=AF.Copy), reads=[wfb], writes=[wbb])
        return wb, wbb


def emit_A(P, C, T):
    P.begin_phase()
    x_d, cvec_d, g_d, adaw_d, adab_d, win_d = T["x"], T["cvec"], T["norm_g0"], T["ada_w0"], T["ada_b0"], T["w_in0"]
    qg_d, kg_d, rc_d, rs_d = T["qg"], T["kg"], T["ropeC"], T["ropeS"]
    QT_d, SG_d, GB_d = T["QT"], T["SG"], T["gate_bc0"]

    psA = Ring(P, "psA", [512], F32, 3, psum=True)
    psB = Ring(P, "psB", [512], F32, 2, psum=True)
    psT = Ring(P, "psT", [8, 128], BF16, 2, psum=True)

    modp = P.sb("modp", [128, 2 * D], F32)
    modb = Buf("modp")
    gains = P.sb("gains", [128, 2], F32)
    gainb = Buf("gains")
    P.dma(gains[:, 0:1], qg_d, writes=[gainb])
    P.dma(gains[:, 1:2], kg_d, writes=[gainb])

    with ExitStack() as es:
        gtmp = P.sb("gtmp", [128, D], F32, es)
        gbc = P.sb("gbc", [128, D], F32, es)
        gtb, gbb = Buf(), Buf()
        P.dma(gbc[:], g_d.partition_broadcast(128).rearrange("p o n -> p (o n)"), writes=[gbb])

        def dst_fn(nt):
            if nt < 8:
                return modp[:, nt * 512:(nt + 1) * 512], modb
            return gtmp[:, (nt - 8) * 512:(nt - 7) * 512], gtb
        emit_mod(P, C, cvec_d, adaw_d, adab_d, psA, dst_fn)
        P.dma(GB_d, gtmp[:], reads=[gtb])
        P.dve(lambda e: e.scalar_tensor_tensor(out=modp[:, D:2 * D], in0=modp[:, D:2 * D], scalar=1.0, in1=gbc[:], op0=ALU.add, op1=ALU.mult),
              reads=[modb, gbb], writes=[modb])
        P.barrier()

    hb_ = HBuilder(P, C, modp[:, 0:D], modp[:, D:2 * D], modb, psT)
    TH = 2048
    hT = P.sb("hT", [128, NCH, TH], BF16)
    hTb = Buf("hT")
    rC = P.sb("rC", [128, TH], F32)
    rS = P.sb("rS", [128, TH], F32)
    ropeb = Buf("rope")
    W = WStream(P, win_d)
    sqr = Ring(P, "sq", [512], BF16, 2)
    sdr = Ring(P, "sdq", [512], F32, 2)
    qnr = Ring(P, "qn", [512], F32, 2)
    t1r = Ring(P, "t1", [512], F32, 2)
    t2r = Ring(P, "t2", [512], F32, 2)
    qor = Ring(P, "qo", [512], BF16, 3)
    vor = qor
    NG = 10

    for th in range(2):
        tok0 = th * TH
        Vv = T["Vown"][th]
        W.dma(0)
        for tt in range(TH // 128):
            r0 = tok0 + tt * 128
            hb_.tile([(0, 128, x_d[r0:r0 + 128, :])], hT, hTb, tt * 128)
        P.dma(rC[:], rc_d[:, tok0:tok0 + TH], writes=[ropeb])
        P.dma(rS[:], rs_d[:, tok0:tok0 + TH], writes=[ropeb])
        pend_a, pend_b = [], []

        def stage1a(st):
            pst, psb, sq, sqb, t5, cc = st["pst"], st["psb"], st["sq"], st["sqb"], st["t5"], st["cc"]
            gcol = gains[:, 0:1] if cc < 16 else gains[:, 1:2]
            pB, pBb = psB.next()
            P.pe(lambda e: e.matmul(pB[:], C["ones"][:], sq[:], start=True, stop=True), reads=[C["buf"], sqb], writes=[pBb])
            sd, sdb = sdr.next()
            qn, qnb = qnr.next()
            t1, t1b = t1r.next()
            t2, t2b = t2r.next()
            P.act(lambda e: e.activation(out=sd[:], in_=pB[:], func=AF.Ln, scale=1.0 / HD, bias=C["eps"][:, 0:1]),
                  reads=[pBb, C["epsb"]], writes=[sdb])
            P.act(lambda e: e.activation(out=sd[:], in_=sd[:], func=AF.Exp, scale=-0.5), reads=[sdb], writes=[sdb])
            P.dve(lambda e: e.scalar_tensor_tensor(out=qn[:], in0=pst[:], scalar=gcol, in1=sd[:], op0=ALU.mult, op1=ALU.mult),
                  reads=[psb, sdb, gainb], writes=[qnb])
            cs = slice(t5 * 512, (t5 + 1) * 512)
            P.dve(lambda e: e.tensor_tensor(out=t1[:], in0=qn[:], in1=rC[:, cs], op=ALU.mult), reads=[qnb, ropeb], writes=[t1b])
            P.pool(lambda e: e.tensor_tensor(out=t2[0:64, :], in0=qn[64:128, :], in1=rS[64:128, cs], op=ALU.mult), reads=[qnb, ropeb], writes=[t2b])
            P.pool(lambda e: e.tensor_tensor(out=t2[64:128, :], in0=qn[0:64, :], in1=rS[0:64, cs], op=ALU.mult), reads=[qnb, ropeb], writes=[t2b])
            st.update(t1=t1, t1b=t1b, t2=t2, t2b=t2b)

        def stage1b(st):
            t1, t1b, t2, t2b, t5, cc = st["t1"], st["t1b"], st["t2"], st["t2b"], st["t5"], st["cc"]
            qo, qob = qor.next()
            P.dve(lambda e: e.tensor_tensor(out=qo[:], in0=t1[:], in1=t2[:], op=ALU.add), reads=[t1b, t2b], writes=[qob])
            if cc < 16:
                dstd = QT_d[cc, :, tok0 + t5 * 512: tok0 + (t5 + 1) * 512]
            else:
                dstd = T["KTown"][(cc - 16) // 2][((cc - 16) % 2) * 128:((cc - 16) % 2) * 128 + 128, tok0 + t5 * 512: tok0 + (t5 + 1) * 512]
            P.dma(dstd, qo[:], reads=[qob])

        def flush():
            while pend_a:
                st = pend_a.pop(0)
                stage1a(st)
                pend_b.append(st)
            while pend_b:
                stage1b(pend_b.pop(0))

        wb, wbb = W.casts(0)
        for g in range(NG):
            if g + 1 < NG:
                W.dma(g + 1)
            nxt = None
            if g == 5:
                flush()
                for tt in range(TH // 128):
                    pst, psb = psA.next()
                    for j in range(NCH):
                        P.pe(lambda e, pst=pst, j=j, tt=tt, wb=wb: e.matmul(pst[:], hT[:, j, tt * 128:(tt + 1) * 128], wb[:, j, :], start=(j == 0), stop=(j == NCH - 1)),
                             reads=[hTb, wbb], writes=[psb])
                    vo, vob = vor.next()
                    if tt % 2 == 0:
                        P.dve(lambda e, vo=vo, pst=pst: e.tensor_copy(vo[:], pst[:]), reads=[psb], writes=[vob])
                    else:
                        P.act(lambda e, vo=vo, pst=pst: e.activation(out=vo[:], in_=pst[:], func=AF.Copy), reads=[psb], writes=[vob])
                    P.dma(Vv[tt * 128:(tt + 1) * 128, :], vo[:], reads=[vob])
                    if tt == 8:
                        nxt = W.casts(g + 1)
                wb, wbb = nxt
                continue
            for ci in range(4):
                cc = 4 * g + ci if g < 5 else 4 * g + ci
                if ci == 2 and g + 1 < NG:
                    nxt = W.casts(g + 1)
                for t5 in range(TH // 512):
                    pst, psb = psA.next()
                    for j in range(NCH):
                        P.pe(lambda e, pst=pst, j=j, t5=t5, wb=wb, ci=ci: e.matmul(pst[:], wb[:, j, ci * 128:(ci + 1) * 128], hT[:, j, t5 * 512:(t5 + 1) * 512], start=(j == 0), stop=(j == NCH - 1)),
                             reads=[hTb, wbb], writes=[psb])
                    if cc < 20:
                        sq, sqb = sqr.next()
                        P.act(lambda e, sq=sq, pst=pst: e.activation(out=sq[:], in_=pst[:], func=AF.Square), reads=[psb], writes=[sqb])
                        if pend_b:
                            stage1b(pend_b.pop(0))
                        if pend_a:
                            st = pend_a.pop(0)
                            stage1a(st)
                            pend_b.append(st)
                        pend_a.append(dict(pst=pst, psb=psb, sq=sq, sqb=sqb, t5=t5, cc=cc))
                    else:
                        qo, qob = qor.next()
                        P.act(lambda e, qo=qo, pst=pst: e.activation(out=qo[:], in_=pst[:], func=AF.Silu), reads=[psb], writes=[qob])
                        P.dma(SG_d[cc - 24, :, tok0 + t5 * 512: tok0 + (t5 + 1) * 512], qo[:], reads=[qob])
            if g == 4:
                flush()
            if nxt is not None:
                wb, wbb = nxt
    P.barrier()
    agb = Buf("ag")
    for j in range(2):
        P.allgather(T["KTown"][j], T["KTag"][j], writes=[agb])
        P.allgather(T["Vown"][j], T["Vag"][j], writes=[agb])
    P.end_phase()


def host_consts():
    return {
        "c_ident": np.eye(128, dtype=np.float32).astype(NPBF),
        "c_ones": np.ones((128, 128), np.float32).astype(NPBF),
        "c_onesf": np.ones((128, 128), np.float32),
    }


def rope_tables():
    t = np.arange(S)
    rows = (t // 64).astype(np.float32)
    cols = (t % 64).astype(np.float32)
    inv = (np.float32(10000.0) ** (-np.arange(0, 64, 2, dtype=np.float32) / np.float32(64))).astype(np.float32)
    ang = np.concatenate([rows[:, None] * inv[None, :], cols[:, None] * inv[None, :]], axis=-1)
    c = np.cos(ang).T.astype(np.float32)
    s = np.sin(ang).T.astype(np.float32)
    return np.concatenate([c, c], 0), np.concatenate([s, -s], 0)


PERM = np.concatenate([np.arange(0, 128, 2), np.arange(1, 128, 2)])


def emit_B(P, C, T):
    P.begin_phase()
    QT_d, SG_d, OG_d = T["QT"], T["SG"], T["OGT"]
    psS = Ring(P, "psS", [512], F32, 4, psum=True)
    psO = Ring(P, "psO", [512], F32, 2, psum=True)
    psL = Ring(P, "psL", [512], F32, 2, psum=True)
    kr = Ring(P, "kt", [S], BF16, 2)
    vr = Ring(P, "vt", [64, 128], BF16, 2)
    qr = Ring(P, "qt", [TOWN], BF16, 2)
    sgr = Ring(P, "sg", [TOWN], BF16, 2)
    pr = Ring(P, "p", [512], BF16, 4)
    rlr = Ring(P, "rl", [512], F32, 2)
    lnr = Ring(P, "lnl", [512], F32, 2)
    asr = Ring(P, "asum", [512], BF16, 2)
    aDr = Ring(P, "accD", [512], F32, 2)
    aPr = Ring(P, "accP", [512], F32, 2)
    o1r = Ring(P, "o1", [512], F32, 2)
    ogr = Ring(P, "og", [512], BF16, 2)
    scale = 1.0 / math.sqrt(HD)
    NKB = S // 128
    pend = []

    def front(st):
        pS, pSb = psS.next()
        p, pb = pr.next()
        kt, ktb, qt_, qb, qi, kb = st["kt"], st["ktb"], st["qt"], st["qb"], st["qi"], st["kb"]
        P.pe(lambda e: e.matmul(pS[:], kt[:, kb * 128:(kb + 1) * 128], qt_[:, qi * 512:(qi + 1) * 512], start=True, stop=True),
             reads=[ktb, qb], writes=[pSb])
        P.act(lambda e: e.activation(out=p[:], in_=pS[:], func=AF.Exp, scale=scale), reads=[pSb], writes=[pb])
        st["p"], st["pb"] = p, pb

    def back(st):
        p, pb, vt, vtb, kb = st["p"], st["pb"], st["vt"], st["vtb"], st["kb"]
        pO, pOb = st["pO"], st["pOb"]
        accs = st["accs"]
        P.pe(lambda e: e.matmul(pO[:], vt[:, kb, :], p[:], start=(kb == 0), stop=(kb == NKB - 1)), reads=[vtb, pb], writes=[pOb])
        which = "pool" if kb % 3 == 2 else "dve"
        acc, accb = accs[which]
        first = kb < 3 and (kb == 2 or kb == 0)
        if which == "dve":
            if kb == 0:
                P.dve(lambda e: e.tensor_copy(acc[:], p[:]), reads=[pb], writes=[accb])
            else:
                P.dve(lambda e: e.tensor_tensor(out=acc[:], in0=acc[:], in1=p[:], op=ALU.add), reads=[pb, accb], writes=[accb])
        else:
            if kb == 2:
                P.pool(lambda e: e.tensor_copy(acc[:], p[:]), reads=[pb], writes=[accb])
            else:
                P.pool(lambda e: e.tensor_tensor(out=acc[:], in0=acc[:], in1=p[:], op=ALU.add), reads=[pb, accb], writes=[accb])
        if kb == NKB - 1:
            h, qi, sg, sgb = st["h"], st["qi"], st["sg"], st["sgb"]
            pL, pLb = psL.next()
            asum, asumb = asr.next()
            ln, lnb = lnr.next()
            rl, rlb = rlr.next()
            o1, o1b = o1r.next()
            og, ogb = ogr.next()
            (aD, aDb), (aP, aPb) = accs["dve"], accs["pool"]
            P.pool(lambda e: e.tensor_tensor(out=asum[:], in0=aD[:], in1=aP[:], op=ALU.add), reads=[aDb, aPb], writes=[asumb])
            P.pe(lambda e: e.matmul(pL[:], C["ones"][:], asum[:], start=True, stop=True), reads=[C["buf"], asumb], writes=[pLb])
            P.act(lambda e: e.activation(out=ln[:], in_=pL[:], func=AF.Ln), reads=[pLb], writes=[lnb])
            P.act(lambda e: e.activation(out=rl[:], in_=ln[:], func=AF.Exp, scale=-1.0), reads=[lnb], writes=[rlb])
            P.dve(lambda e: e.tensor_tensor(out=o1[:], in0=pO[:], in1=rl[:], op=ALU.mult), reads=[pOb, rlb], writes=[o1b])
            P.pool(lambda e: e.tensor_tensor(out=og[:], in0=o1[:], in1=sg[:, qi * 512:(qi + 1) * 512], op=ALU.mult), reads=[o1b, sgb], writes=[ogb])
            P.dma(OG_d[h, :, qi * 512:(qi + 1) * 512], og[:], reads=[ogb])

    for g in range(4):
        kt, ktb = kr.next()
        vt, vtb = vr.next()
        P.dma(kt[:].rearrange("p (r t) -> p r t", r=2),
              T["KTag"][g // 2].rearrange("(r k p) t -> p k r t", r=2, k=2)[:, g % 2], writes=[ktb])
        for j in range(2):
            for r in range(2):
                P.dma(vt[:, r * 32 + j * 16:r * 32 + j * 16 + 16, :],
                      T["Vag"][j][r * 2048:(r + 1) * 2048, g * 128:(g + 1) * 128].rearrange("(kk p) d -> p kk d", p=128), writes=[vtb])
        for hh in range(4):
            h = 4 * g + hh
            qt_, qb = qr.next()
            sg, sgb = sgr.next()
            P.dma(qt_[:], QT_d[h], writes=[qb])
            P.dma(sg[:], SG_d[h], writes=[sgb])
            for qi in range(TOWN // 512):
                pO, pOb = psO.next()
                accs = {"dve": aDr.next(), "pool": aPr.next()}
                for kb in range(NKB):
                    st = dict(kt=kt, ktb=ktb, vt=vt, vtb=vtb, qt=qt_, qb=qb, sg=sg, sgb=sgb, h=h, qi=qi, kb=kb,
                              pO=pO, pOb=pOb, accs=accs)
                    front(st)
                    pend.append(st)
                    if len(pend) > 2:
                        back(pend.pop(0))
    while pend:
        back(pend.pop(0))
    P.end_phase()


def emit_CF(P, C, T, final):
    P.begin_phase()
    if final:
        w_d, x_d, gb_d = T["w_out1"], T["x1"], T["gate_bc1"]
        sg_d, fg_d, out_d = T["SG1"], T["final_g"], T["out"]
        xv = x_d.rearrange("(a b) m -> b a m", b=64)
        ov = out_d.rearrange("(a b) m -> b a m", b=64)
        sel = P.sb("sel", [128, 2], F32)
        selb = Buf("sel")
        P.dma(sel[:], T["sel"], writes=[selb])
    else:
        w_d, x_d, gb_d = T["w_out0"], T["x"], T["gate_bc0"]
        OG_d, out_d = T["OGT"], T["x1"]
    psY = Ring(P, "psY", [512], F32, 6 if final else 8, psum=True)
    wbf = P.sb("wbf", [128, NCH, D], BF16)
    wbb = Buf("wbf")
    gbc = P.sb("gbc", [128, D], F32)
    gbb = Buf("gbc")
    P.dma(gbc[:], gb_d, writes=[gbb])
    wfr = Ring(P, "wf", [4, 512], F32, 3)
    wv = w_d.rearrange("(j p) n -> p j n", p=128)
    wbn = [Buf(f"wbf{n}") for n in range(4)]
    for n in range(4):
        for jq in range(4):
            wf, wfb = wfr.next()
            P.dma(wf[:], wv[:, 4 * jq:4 * jq + 4, n * 512:(n + 1) * 512], writes=[wfb])
            dst = wbf[:, 4 * jq:4 * jq + 4, n * 512:(n + 1) * 512]
            if n % 2 == 0:
                P.dve(lambda e, wf=wf, dst=dst: e.tensor_copy(dst, wf[:]), reads=[wfb], writes=[wbn[n]])
            else:
                P.act(lambda e, wf=wf, dst=dst: e.activation(out=dst, in_=wf[:], func=AF.Copy), reads=[wfb], writes=[wbn[n]])
    xr = Ring(P, "xt", [D], F32, 2)
    tr = Ring(P, "tmp", [D], F32, 2)
    if final:
        psT = Ring(P, "psT", [8, 128], BF16, 2, psum=True)
        fgb_t = P.sb("fgbc", [128, D], F32)
        fgb = Buf("fg")
        P.dma(fgb_t[:], fg_d.partition_broadcast(128).rearrange("p o n -> p (o n)"), writes=[fgb])
        fr = Ring(P, "ft", [D], BF16, 2)
        b0r = Ring(P, "b0", [1024], BF16, 2)
        b1r = Ring(P, "b1", [1024], BF16, 2)
        sgr = Ring(P, "sgt", [D], BF16, 2)
        obr = Ring(P, "ogb", [D], BF16, 2)
        otr = Ring(P, "ogT", [NCH, 128], BF16, 2)
        ssr = Ring(P, "ss", [1], F32, 2)
        sdr = Ring(P, "sd", [1], F32, 2)
        jr = Ring(P, "junk", [D], BF16, 1)
    else:
        ogr = Ring(P, "og", [NCH, 512], BF16, 2)
        OGv = OG_d.rearrange("c p t -> p c t")

    for tt in range(TOWN // 128):
        r0 = tt * 128
        if final:
            ft, ftb = fr.next()
            sg, sgb = sgr.next()
            ob, obb = obr.next()
            oT, oTb = otr.next()
            b0, b0b = b0r.next()
            b1, b1b = b1r.next()
            fob = Buf("fo")
            P.dma(ft[:, 0:1024], T["f_own"][r0:r0 + 128, :], writes=[fob])
            fj, fr0 = tt // 8, (tt % 8) * 128
            P.dma(b0[:], T["fag"][fj][fr0:fr0 + 128, :], writes=[b0b])
            P.dma(b1[:], T["fag"][fj][1024 + fr0:1024 + fr0 + 128, :], writes=[b1b])
            P.dma(sg[:], sg_d[r0:r0 + 128, :], writes=[sgb])
            P.pool(lambda e, b1=b1: e.tensor_scalar(out=b1[:], in0=b1[:], scalar1=sel[:, 1:2], scalar2=None, op0=ALU.mult), reads=[b1b, selb], writes=[b1b])
            P.dve(lambda e, ft=ft, b0=b0, b1=b1: e.scalar_tensor_tensor(out=ft[:, 1024:2048], in0=b0[:], scalar=sel[:, 0:1], in1=b1[:], op0=ALU.mult, op1=ALU.add),
                  reads=[b0b, b1b, selb, fob], writes=[ftb])
            P.pool(lambda e, ob=ob, ft=ft, sg=sg: e.tensor_tensor(out=ob[:], in0=ft[:], in1=sg[:], op=ALU.mult), reads=[ftb, fob, sgb], writes=[obb])
            for half in range(2):
                pt, ptb = psT.next()
                for c in range(8):
                    ch = half * 8 + c
                    P.pe(lambda e, pt=pt, c=c, ch=ch, ob=ob: e.transpose(pt[:, c, :], ob[:, ch * 128:(ch + 1) * 128], C["ident"][:]),
                         reads=[obb, C["buf"]], writes=[ptb])
                P.act(lambda e, pt=pt, oT=oT, half=half: e.activation(out=oT[:, half * 8:half * 8 + 8, :], in_=pt[:], func=AF.Copy), reads=[ptb], writes=[oTb])
            lhs = lambda c, oT=oT: oT[:, c, :]
            lb = oTb
        else:
            if tt % 4 == 0:
                og, ogb = ogr.next()
                P.dma(og[:], OGv[:, :, r0:r0 + 512], writes=[ogb])
            ti = tt % 4
            lhs = lambda c, og=og, ti=ti: og[:, c, ti * 128:(ti + 1) * 128]
            lb = ogb
        xt, xb = xr.next()
        tmp, tmpb = tr.next()
        if final:
            P.dma(xt[0:64, :], xv[2 * tt], writes=[xb])
            P.dma(xt[64:128, :], xv[2 * tt + 1], writes=[xb])
        else:
            P.dma(xt[:], x_d[r0:r0 + 128, :], writes=[xb])
        for n in range(4):
            ps, psb = psY.next()
            for c in range(NCH):
                P.pe(lambda e, ps=ps, c=c, n=n, lhs=lhs: e.matmul(ps[:], lhs(c), wbf[:, c, n * 512:(n + 1) * 512], start=(c == 0), stop=(c == NCH - 1)),
                     reads=[lb, wbn[n]], writes=[psb])
            P.dve(lambda e, ps=ps, n=n, tmp=tmp: e.tensor_tensor(out=tmp[:, n * 512:(n + 1) * 512], in0=ps[:], in1=gbc[:, n * 512:(n + 1) * 512], op=ALU.mult),
                  reads=[psb, gbb], writes=[tmpb])
        P.pool(lambda e, xt=xt, tmp=tmp: e.tensor_tensor(out=xt[:], in0=xt[:], in1=tmp[:], op=ALU.add), reads=[xb, tmpb], writes=[xb])
        if final:
            ss, ssb = ssr.next()
            sd, sdb = sdr.next()
            jk, jkb = jr.next()
            P.pool(lambda e, ss=ss: e.memset(ss[:], 0.0), writes=[ssb])
            P.act(lambda e, jk=jk, xt=xt, ss=ss: e.activation(out=jk[:], in_=xt[:], func=AF.Square, accum_out=ss[:]), reads=[xb], writes=[jkb, ssb])
            P.act(lambda e, sd=sd, ss=ss: e.activation(out=sd[:], in_=ss[:], func=AF.Sqrt, scale=1.0 / D, bias=C["eps"][:, 0:1]),
                  reads=[ssb, C["epsb"]], writes=[sdb])
            P.dve(lambda e, sd=sd: e.reciprocal(out=sd[:], in_=sd[:]), reads=[sdb], writes=[sdb])
            P.dve(lambda e, xt=xt, sd=sd: e.scalar_tensor_tensor(out=xt[:], in0=xt[:], scalar=sd[:, 0:1], in1=fgb_t[:], op0=ALU.mult, op1=ALU.mult),
                  reads=[xb, sdb, fgb], writes=[xb])
        if final:
            P.dma(ov[2 * tt], xt[0:64, :], reads=[xb])
            P.dma(ov[2 * tt + 1], xt[64:128, :], reads=[xb])
        else:
            P.dma(out_d[r0:r0 + 128, :], xt[:], reads=[xb])
    P.end_phase()


def emit_D(P, C, T):
    P.begin_phase()
    x_d, cvec_d, g_d, adaw_d, adab_d, win_d = T["x1"], T["cvec"], T["norm_g1"], T["ada_w1"], T["ada_b1"], T["w_in1"]
    SG_d, GB_d = T["SG1"], T["gate_bc1"]
    psA = Ring(P, "psA", [512], F32, 4, psum=True)
    psT = Ring(P, "psT", [8, 128], BF16, 2, psum=True)
    modp = P.sb("modp", [128, 2 * D], F32)
    modb = Buf("modp")
    with ExitStack() as es:
        gtmp = P.sb("gtmp", [128, D], F32, es)
        gbc = P.sb("gbc", [128, D], F32, es)
        gtb, gbb = Buf(), Buf()
        P.dma(gbc[:], g_d.partition_broadcast(128).rearrange("p o n -> p (o n)"), writes=[gbb])

        def dst_fn(nt):
            if nt < 8:
                return modp[:, nt * 512:(nt + 1) * 512], modb
            return gtmp[:, (nt - 8) * 512:(nt - 7) * 512], gtb
        emit_mod(P, C, cvec_d, adaw_d, adab_d, psA, dst_fn)
        P.dma(GB_d, gtmp[:], reads=[gtb])
        P.dve(lambda e: e.scalar_tensor_tensor(out=modp[:, D:2 * D], in0=modp[:, D:2 * D], scalar=1.0, in1=gbc[:], op0=ALU.add, op1=ALU.mult),
              reads=[modb, gbb], writes=[modb])
        P.barrier()
    hb_ = HBuilder(P, C, modp[:, 0:D], modp[:, D:2 * D], modb, psT)
    TH = 2048
    hT = P.sb("hT", [128, NCH, TH], BF16)
    hTb = Buf("hT")
    W = WStream(P, win_d)
    uor = Ring(P, "uo", [512], BF16, 3)
    gor = Ring(P, "go", [512], BF16, 3)
    xv = x_d.rearrange("(a b) m -> b a m", b=64)
    NG = 8
    for th in range(2):
        tok0 = th * TH
        W.dma(0)
        for tt in range(TH // 128):
            t2a = (tok0 + tt * 128) // 64
            hb_.tile([(0, 64, xv[t2a]), (64, 128, xv[t2a + 1])], hT, hTb, tt * 128)
        wb, wbb = W.casts(0)
        for g in range(NG):
            if g + 1 < NG:
                W.dma(g + 1)
            nxt = None
            if g < 4:
                for ci in range(4):
                    cc = 4 * g + ci
                    if ci == 2 and g + 1 < NG:
                        nxt = W.casts(g + 1)
                    for t5 in range(TH // 512):
                        pst, psb = psA.next()
                        for j in range(NCH):
                            P.pe(lambda e, pst=pst, j=j, t5=t5, wb=wb, ci=ci: e.matmul(pst[:], wb[:, j, ci * 128:(ci + 1) * 128], hT[:, j, t5 * 512:(t5 + 1) * 512], start=(j == 0), stop=(j == NCH - 1)),
                                 reads=[hTb, wbb], writes=[psb])
                        uo, uob = uor.next()
                        if t5 % 2 == 0:
                            P.act(lambda e, uo=uo, pst=pst: e.activation(out=uo[:], in_=pst[:], func=AF.Copy), reads=[psb], writes=[uob])
                        else:
                            P.dve(lambda e, uo=uo, pst=pst: e.tensor_copy(uo[:], pst[:]), reads=[psb], writes=[uob])
                        if cc < 8:
                            udst = T["UTown"][cc, :, tok0 + t5 * 512: tok0 + (t5 + 1) * 512]
                        else:
                            udst = T["UTsend"][(cc - 8) // 2][((cc - 8) % 2) * 128:((cc - 8) % 2) * 128 + 128, tok0 + t5 * 512: tok0 + (t5 + 1) * 512]
                        P.dma(udst, uo[:], reads=[uob])
            else:
                for tt in range(TH // 128):
                    if tt == 8 and g + 1 < NG:
                        nxt = W.casts(g + 1)
                    pst, psb = psA.next()
                    for j in range(NCH):
                        P.pe(lambda e, pst=pst, j=j, tt=tt, wb=wb: e.matmul(pst[:], hT[:, j, tt * 128:(tt + 1) * 128], wb[:, j, :], start=(j == 0), stop=(j == NCH - 1)),
                             reads=[hTb, wbb], writes=[psb])
                    go, gob = gor.next()
                    P.act(lambda e, go=go, pst=pst: e.activation(out=go[:], in_=pst[:], func=AF.Silu), reads=[psb], writes=[gob])
                    r0 = tok0 + tt * 128
                    P.dma(SG_d[r0:r0 + 128, (g - 4) * 512:(g - 3) * 512], go[:], reads=[gob])
            if nxt is not None:
                wb, wbb = nxt
    P.barrier()
    agb = Buf("ag")
    for j in range(4):
        P.allgather(T["UTsend"][j], T["UTag"][j], writes=[agb])
    P.end_phase()


def fourier_tables():
    w = np.arange(256)
    aw = 2 * np.pi * np.outer(w, w) / 256
    csw = np.concatenate([np.cos(aw), np.sin(aw)], 1) / 16.0
    csw = csw.reshape(2, 128, 512).transpose(1, 0, 2)
    t1 = np.arange(128)
    a1 = 2 * np.pi * np.outer(t1, t1) / 128
    nrm = 1.0 / math.sqrt(8192.0)
    ra = np.concatenate([np.cos(a1), np.sin(a1)], 1) * nrm
    rb = np.concatenate([-np.sin(a1), np.cos(a1)], 1) * nrm
    m = np.arange(128)
    t2 = m % 64
    c2 = m // 64
    atw = 2 * np.pi * np.outer(t2, np.arange(128)) / 8192
    ct, st = np.cos(atw), np.sin(atw)
    n = np.arange(128)
    c2n, k2 = n // 64, n % 64
    a2 = 2 * np.pi * np.outer(t2, k2) / 64
    delta = (c2[:, None] == c2n[None, :]).astype(np.float64)
    re = delta * np.cos(a2)
    rf = -delta * np.sin(a2)
    return {
        "t_csw": np.ascontiguousarray(csw).astype(np.float32).astype(NPBF),
        "t_ra": ra.astype(np.float32).astype(NPBF),
        "t_rb": rb.astype(np.float32).astype(NPBF),
        "t_ct": ct.astype(np.float32),
        "t_st": st.astype(np.float32),
        "t_re": re.astype(np.float32).astype(NPBF),
        "t_rf": rf.astype(np.float32).astype(NPBF),
    }


def emit_E(P, C, T):
    P.begin_phase()
    csw_d, ra_d, rb_d, ct_d, st_d, re_d, rf_d = T["t_csw"], T["t_ra"], T["t_rb"], T["t_ct"], T["t_st"], T["t_re"], T["t_rf"]
    sel = P.sb("sel", [128, 2], F32)
    selb = Buf("sel")
    P.dma(sel[:], T["sel"], writes=[selb])
    csw = P.sb("csw", [128, 2, 512], BF16)
    ra = P.sb("ra", [128, 256], BF16)
    rb = P.sb("rb", [128, 256], BF16)
    ct = P.sb("ct", [128, 128], F32)
    st = P.sb("st", [128, 128], F32)
    re = P.sb("re", [128, 128], BF16)
    rf = P.sb("rf", [128, 128], BF16)
    tb = Buf("tables")
    for dst, src in ((csw, csw_d), (ra, ra_d), (rb, rb_d), (ct, ct_d), (st, st_d), (re, re_d), (rf, rf_d)):
        P.dma(dst[:], src, writes=[tb])
    ps1 = Ring(P, "ps1", [512], F32, 2, psum=True)
    ps2 = Ring(P, "ps2", [512], F32, 2, psum=True)
    ps3 = Ring(P, "ps3", [512], F32, 2, psum=True)
    AB = P.sb("AB", [128, 512, 64], BF16)
    ABb = Buf("AB")
    G2 = P.sb("G2", [128, 64, 256], BF16)
    G2b = Buf("G2")
    fS = P.sb("fS", [128, 64, 128], BF16)
    fSb = Buf("fS")
    ur = Ring(P, "u", [2, 8, 128], BF16, 3)
    upb_ = [Buf("up0"), Buf("up1"), Buf("up2")]
    b0r = Ring(P, "ub0", [2, 512], BF16, 2)
    b1r = Ring(P, "ub1", [2, 512], BF16, 2)
    efr = Ring(P, "ef", [512], F32, 2)
    m1r = Ring(P, "m1", [2, 128], F32, 2)
    m2r = Ring(P, "m2", [2, 128], F32, 2)
    m3r = Ring(P, "m3", [2, 128], F32, 2)
    m4r = Ring(P, "m4", [2, 128], F32, 2)
    fov = T["f_own"].rearrange("(t a e) c -> e t a c", t=64, a=32, e=2)
    fsv = [T["fsend"][j].rearrange("(t a e) c -> e t a c", t=16, a=32, e=2) for j in range(4)]
    uk = 0
    ctb = ct[:].unsqueeze(1).to_broadcast([128, 2, 128])
    stb = st[:].unsqueeze(1).to_broadcast([128, 2, 128])
    k = 0
    for g in range(4):
        for t8 in range(8):
            u, ub = ur.next()
            upb = upb_[uk % 3]
            uk += 1
            b0, b0b = b0r.next()
            b1, b1b = b1r.next()
            c0 = t8 * 512
            for kc in range(2):
                P.dma(u[:, kc, :, 0:64], T["UTown"][2 * g + kc, :, c0:c0 + 512].rearrange("p (i t) -> p i t", t=64), writes=[ub])
            agv = T["UTag"][g].rearrange("(r k p) t -> p r k t", r=2, k=2)
            P.dma(b0[:], agv[:, 0, :, c0:c0 + 512], writes=[b0b])
            P.dma(b1[:], agv[:, 1, :, c0:c0 + 512], writes=[b1b])
            P.pool(lambda e, b1=b1: e.tensor_scalar(out=b1[:], in0=b1[:], scalar1=sel[:, 1:2], scalar2=None, op0=ALU.mult), reads=[b1b, selb], writes=[b1b])
            P.dve(lambda e, u=u, b0=b0, b1=b1: e.scalar_tensor_tensor(out=u[:, :, :, 64:128], in0=b0[:].rearrange("p k (i t) -> p k i t", t=64), scalar=sel[:, 0:1],
                                                                     in1=b1[:].rearrange("p k (i t) -> p k i t", t=64), op0=ALU.mult, op1=ALU.add),
                  reads=[b0b, b1b, selb], writes=[upb])
            for ti in range(8):
                t2 = t8 * 8 + ti
                ps, psb = ps1.next()
                for kc in range(2):
                    P.pe(lambda e, ps=ps, u=u, kc=kc, ti=ti: e.matmul(ps[:], u[:, kc, ti, :], csw[:, kc, :], start=(kc == 0), stop=(kc == 1)),
                         reads=[ub, upb, tb], writes=[psb])
                k += 1
                if k % 2 == 0:
                    P.act(lambda e, ps=ps, t2=t2: e.activation(out=AB[:, :, t2], in_=ps[:], func=AF.Copy), reads=[psb], writes=[ABb])
                else:
                    P.dve(lambda e, ps=ps, t2=t2: e.tensor_copy(AB[:, :, t2], ps[:]), reads=[psb], writes=[ABb])
        for half in range(2):
            for pb in range(32):
                ps, psb = ps2.next()
                for pi in range(2):
                    pl = 2 * pb + pi
                    ch = half * 128 + 2 * pl
                    P.pe(lambda e, ps=ps, pi=pi, ch=ch: e.matmul(ps[:, pi * 256:(pi + 1) * 256], AB[:, ch:ch + 2, :].rearrange("p a b -> p (a b)"), ra[:], start=True, stop=False),
                         reads=[ABb, tb], writes=[psb])
                    P.pe(lambda e, ps=ps, pi=pi, ch=ch: e.matmul(ps[:, pi * 256:(pi + 1) * 256], AB[:, 256 + ch:256 + ch + 2, :].rearrange("p a b -> p (a b)"), rb[:], start=False, stop=True),
                         reads=[ABb, tb], writes=[psb])
                ef, efb = efr.next()
                P.act(lambda e, ef=ef, ps=ps: e.activation(out=ef[:], in_=ps[:], func=AF.Copy), reads=[psb], writes=[efb])
                efv = ef[:].rearrange("p (a b) -> p a b", a=2)
                Ev, Fv = efv[:, :, 0:128], efv[:, :, 128:256]
                m1, m1b = m1r.next()
                m2, m2b = m2r.next()
                m3, m3b = m3r.next()
                m4, m4b = m4r.next()
                P.pool(lambda e, m1=m1, Ev=Ev: e.tensor_tensor(out=m1[:], in0=Ev, in1=ctb, op=ALU.mult), reads=[efb, tb], writes=[m1b])
                P.pool(lambda e, m2=m2, Fv=Fv: e.tensor_tensor(out=m2[:], in0=Fv, in1=stb, op=ALU.mult), reads=[efb, tb], writes=[m2b])
                P.dve(lambda e, m3=m3, Ev=Ev: e.tensor_tensor(out=m3[:], in0=Ev, in1=stb, op=ALU.mult), reads=[efb, tb], writes=[m3b])
                P.pool(lambda e, m4=m4, Fv=Fv: e.tensor_tensor(out=m4[:], in0=Fv, in1=ctb, op=ALU.mult), reads=[efb, tb], writes=[m4b])
                P.dve(lambda e, m1=m1, m2=m2, pb=pb: e.tensor_tensor(out=G2[:, 2 * pb:2 * pb + 2, 0:128], in0=m1[:], in1=m2[:], op=ALU.subtract),
                      reads=[m1b, m2b], writes=[G2b])
                P.dve(lambda e, m3=m3, m4=m4, pb=pb: e.tensor_tensor(out=G2[:, 2 * pb:2 * pb + 2, 128:256], in0=m3[:], in1=m4[:], op=ALU.add),
                      reads=[m3b, m4b], writes=[G2b])
            for qb in range(16):
                ps, psb = ps3.next()
                for pi in range(4):
                    pl = 4 * qb + pi
                    P.pe(lambda e, ps=ps, pi=pi, pl=pl: e.matmul(ps[:, pi * 128:(pi + 1) * 128], G2[:, pl, 0:128], re[:], start=True, stop=False),
                         reads=[G2b, tb], writes=[psb])
                    P.pe(lambda e, ps=ps, pi=pi, pl=pl: e.matmul(ps[:, pi * 128:(pi + 1) * 128], G2[:, pl, 128:256], rf[:], start=False, stop=True),
                         reads=[G2b, tb], writes=[psb])
                dst = fS[:, :, 8 * qb:8 * qb + 8].rearrange("p k (a c) -> p a c k", a=4, c=2)
                src_fn = lambda ps=ps: ps[:].rearrange("p (a c k) -> p a c k", a=4, c=2)
                if qb % 2 == 0:
                    P.act(lambda e, dst=dst, src_fn=src_fn: e.activation(out=dst, in_=src_fn(), func=AF.Copy), reads=[psb], writes=[fSb])
                else:
                    P.dve(lambda e, dst=dst, src_fn=src_fn: e.tensor_copy(dst, src_fn()), reads=[psb], writes=[fSb])
            c0 = g * 256 + half * 128
            for e_ in range(2):
                P.dma(fov[e_][:, :, c0:c0 + 128], fS[e_ * 64:(e_ + 1) * 64, 0:32, :], reads=[fSb])
                for j in range(4):
                    P.dma(fsv[j][e_][:, :, c0:c0 + 128], fS[e_ * 64 + 16 * j:e_ * 64 + 16 * j + 16, 32:64, :], reads=[fSb])
    P.barrier()
    agb = Buf("ag")
    for j in range(4):
        P.allgather(T["fsend"][j], T["fag"][j], writes=[agb])
    P.end_phase()


def assemble_UTf(ut0, ut1, hf):
    a = np.asarray(ut0)[8 * hf:8 * hf + 8].reshape(8, 128, 64, 64)
    b = np.asarray(ut1)[8 * hf:8 * hf + 8].reshape(8, 128, 64, 64)
    return np.ascontiguousarray(np.concatenate([a, b], axis=3).reshape(8, 128, S))


def build_fused():
    P = Prog()
    T = {}
    I32 = mybir.dt.int32
    T["x"] = P.din("x", [TOWN, D], F32)
    T["cvec"] = P.din("cvec", [128, NCH], F32)
    T["norm_g0"] = P.din("norm_g0", [1, D], F32)
    T["norm_g1"] = P.din("norm_g1", [1, D], F32)
    T["ada_w0"] = P.din("ada_w0", [D, 3 * D], F32)
    T["ada_w1"] = P.din("ada_w1", [D, 3 * D], F32)
    T["ada_b0"] = P.din("ada_b0", [1, 3 * D], F32)
    T["ada_b1"] = P.din("ada_b1", [1, 3 * D], F32)
    T["w_in0"] = P.din("w_in0", [D, 5120], F32)
    T["qg"] = P.din("qg", [128, 1], F32)
    T["kg"] = P.din("kg", [128, 1], F32)
    T["ropeC"] = P.din("ropeC", [128, TOWN], F32)
    T["ropeS"] = P.din("ropeS", [128, TOWN], F32)
    T["w_out0"] = P.din("w_out0", [D, D], F32)
    T["w_in1"] = P.din("w_in1", [D, 4096], F32)
    T["w_out1"] = P.din("w_out1", [D, D], F32)
    T["final_g"] = P.din("final_g", [1, D], F32)
    T["sel"] = P.din("sel", [128, 2], F32)
    T["t_csw"] = P.din("t_csw", [128, 2, 512], BF16)
    T["t_ra"] = P.din("t_ra", [128, 256], BF16)
    T["t_rb"] = P.din("t_rb", [128, 256], BF16)
    T["t_ct"] = P.din("t_ct", [128, 128], F32)
    T["t_st"] = P.din("t_st", [128, 128], F32)
    T["t_re"] = P.din("t_re", [128, 128], BF16)
    T["t_rf"] = P.din("t_rf", [128, 128], BF16)
    T["out"] = P.dout("out", [TOWN, D], F32)
    T["QT"] = P.dint("QT", [16, 128, TOWN], BF16)
    T["SG"] = P.dint("SG", [16, 128, TOWN], BF16)
    T["gate_bc0"] = P.dint("gate_bc0", [128, D], F32)
    T["gate_bc1"] = P.dint("gate_bc1", [128, D], F32)
    T["KTown"] = [P.dint(f"KTown{j}", [256, TOWN], BF16) for j in range(2)]
    T["KTag"] = [P.dint(f"KTag{j}", [512, TOWN], BF16) for j in range(2)]
    T["Vown"] = [P.dint(f"Vown{j}", [2048, 512], BF16) for j in range(2)]
    T["Vag"] = [P.dint(f"Vag{j}", [4096, 512], BF16) for j in range(2)]
    T["OGT"] = P.dint("OGT", [16, 128, TOWN], BF16)
    T["x1"] = P.dint("x1", [TOWN, D], F32)
    T["UTown"] = P.dint("UTown", [8, 128, TOWN], BF16)
    T["UTsend"] = [P.dint(f"UTsend{j}", [256, TOWN], BF16) for j in range(4)]
    T["UTag"] = [P.dint(f"UTag{j}", [512, TOWN], BF16) for j in range(4)]
    T["SG1"] = P.dint("SG1", [TOWN, D], BF16)
    T["f_own"] = P.dint("f_own", [TOWN, 1024], BF16)
    T["fsend"] = [P.dint(f"fsend{j}", [1024, 1024], BF16) for j in range(4)]
    T["fag"] = [P.dint(f"fag{j}", [2048, 1024], BF16) for j in range(4)]
    C = emit_consts(P)
    emit_A(P, C, T)
    emit_B(P, C, T)
    emit_CF(P, C, T, False)
    emit_D(P, C, T)
    emit_E(P, C, T)
    emit_CF(P, C, T, True)
    return P.finish()


def core_tables(hf):
    tb = fourier_tables()
    q = (np.arange(128) + 64 * hf) % 128
    tb["t_ra"] = np.ascontiguousarray(tb["t_ra"][q])
    tb["t_rb"] = np.ascontiguousarray(tb["t_rb"][q])
    n = np.arange(128)
    k2 = ((n % 64) + 32 * hf) % 64
    col = (n // 64) * 64 + k2
    tb["t_re"] = np.ascontiguousarray(tb["t_re"][:, col])
    tb["t_rf"] = np.ascontiguousarray(tb["t_rf"][:, col])
    return tb


def kernel(x, c, norm_g, ada_w, ada_b, attn_w_in, attn_q_gain, attn_k_gain,
           attn_w_out, fourier_w_in, fourier_w_out, final_g):
    x = np.asarray(x)
    c = np.asarray(c)
    norm_g = np.asarray(norm_g)
    ada_w = np.asarray(ada_w)
    ada_b = np.asarray(ada_b)
    w_in = np.asarray(attn_w_in)[0]
    fw_in = np.asarray(fourier_w_in)[0]
    fw_out = np.asarray(fourier_w_out)[0]
    ropeC, ropeS = rope_tables()
    colperm = np.arange(5120)
    for h in range(20):
        colperm[h * 128:(h + 1) * 128] = h * 128 + PERM
    w_in_p = np.ascontiguousarray(w_in[:, colperm])
    consts = host_consts()
    qg = np.ascontiguousarray(np.asarray(attn_q_gain)[0][PERM].reshape(128, 1))
    kg = np.ascontiguousarray(np.asarray(attn_k_gain)[0][PERM].reshape(128, 1))
    per_hf = []
    for hf in range(2):
        ch = np.concatenate([np.arange(1024 * hf, 1024 * hf + 1024), np.arange(1024 * (1 - hf), 1024 * (1 - hf) + 1024)])
        d = dict(core_tables(hf))
        d["w_in1"] = np.ascontiguousarray(np.concatenate([fw_in[:, ch], fw_in[:, 2048 + ch]], axis=1))
        d["w_out1"] = np.ascontiguousarray(fw_out[ch, :])
        d["sel"] = np.ascontiguousarray(np.broadcast_to(np.array([float(hf), float(1 - hf)], np.float32), (128, 2)))
        d["ropeC"] = np.ascontiguousarray(ropeC[:, hf * TOWN:(hf + 1) * TOWN])
        d["ropeS"] = np.ascontiguousarray(ropeS[:, hf * TOWN:(hf + 1) * TOWN])
        per_hf.append(d)
    maps = []
    for core in range(NCORES):
        b, hf = core % 4, core // 4
        m = dict(consts)
        m.update(per_hf[hf])
        m["x"] = np.ascontiguousarray(x[b, hf * TOWN:(hf + 1) * TOWN])
        m["cvec"] = np.ascontiguousarray(c[b].reshape(NCH, 128).T)
        m["norm_g0"] = np.ascontiguousarray(norm_g[0:1])
        m["norm_g1"] = np.ascontiguousarray(norm_g[1:2])
        m["ada_w0"] = ada_w[0]
        m["ada_w1"] = ada_w[1]
        m["ada_b0"] = np.ascontiguousarray(ada_b[0:1])
        m["ada_b1"] = np.ascontiguousarray(ada_b[1:2])
        m["w_in0"] = w_in_p
        m["qg"] = qg
        m["kg"] = kg
        m["w_out0"] = np.asarray(attn_w_out)[0]
        m["final_g"] = np.ascontiguousarray(np.asarray(final_g).reshape(1, D))
        maps.append(m)
    res = run_bass_kernel_spmd(build_fused(), maps, core_ids=list(range(NCORES))).results
    out = np.empty((NB, S, D), np.float32)
    for core in range(NCORES):
        b, hf = core % 4, core // 4
        out[b, hf * TOWN:(hf + 1) * TOWN] = res[core]["out"]
    return out
```

```python
import math
from contextlib import ExitStack
import numpy as np
import ml_dtypes
import concourse.bass as bass
import concourse.mybir as mybir
from concourse.bass_utils import run_bass_kernel_spmd

F32 = mybir.dt.float32
BF16 = mybir.dt.bfloat16
AF = mybir.ActivationFunctionType
ALU = mybir.AluOpType
NPBF = ml_dtypes.bfloat16

D = 2048
S = 8192
NB = 4
NCORES = 8
TOWN = 4096
HD = 128
EPS = 1e-6
NCH = D // 128
GROUPS = [[0, 4], [1, 5], [2, 6], [3, 7]]


class Buf:
    __slots__ = ("name", "w", "r", "lsem", "ssem")

    def __init__(self, name=""):
        self.name = name
        self.w = None
        self.r = []
        self.lsem = None
        self.ssem = None


class Prog:
    ENG = ("pe", "act", "dve", "pool", "sp")

    def __init__(self):
        self.nc = bass.Bass("TRN2", target_bir_lowering=False)
        self.es = ExitStack()
        self.ops = {e: [] for e in self.ENG}
        self.N = {e: 0 for e in self.ENG}
        self.S = {e: self.es.enter_context(self.nc.semaphore("cnt_" + e)) for e in self.ENG}
        self.waited = {e: {} for e in self.ENG}
        self.semval = {}
        self.nsem = 5
        self.uid = 0
        self.dram = {}
        self.scope = self.es
        self.phase_sems = []
        self.free_sems = []
        self.ccsems = {}

    def name(self, base):
        self.uid += 1
        return f"{base}_{self.uid}"

    def sb(self, name, shape, dt, es=None):
        return (es or self.scope).enter_context(self.nc.sbuf_tensor(self.name(name), list(shape), dt))

    def ps(self, name, shape, dt, es=None):
        return (es or self.scope).enter_context(self.nc.psum_tensor(self.name(name), list(shape), dt))

    def newsem(self, name):
        if self.free_sems:
            s = self.free_sems.pop()
        else:
            self.nsem += 1
            s = self.es.enter_context(self.nc.semaphore(self.name(name)))
            self.semval[s] = 0
        self.phase_sems.append(s)
        return s

    def begin_phase(self):
        self.scope = ExitStack()
        self.phase_sems = []

    def end_phase(self):
        self.barrier()
        self.scope.close()
        self.scope = self.es
        self.free_sems.extend(self.phase_sems)
        self.phase_sems = []

    def dint(self, name, shape, dt):
        return self.nc.dram_tensor(name, list(shape), dt).ap()

    def allgather(self, in_ap, out_ap, reads=(), writes=()):
        q = "pool"
        self._deps(q, reads, writes)
        self.nsem += 1
        sem = self.es.enter_context(self.nc.semaphore(self.name("cc")))
        self.ccsems[sem] = 1
        self.ops[q].append(lambda e: e.collective_compute("AllGather", ALU.bypass, replica_groups=GROUPS,
                                                          ins=[in_ap.opt()], outs=[out_ap.opt()]).then_inc(sem, 1))
        tok = ("dma", sem, 1)
        for b in reads:
            b.r.append(tok)
        for b in writes:
            b.w = tok
            b.r = []
        return tok

    def din(self, name, shape, dt):
        t = self.nc.dram_tensor(name, list(shape), dt, kind="ExternalInput").ap()
        self.dram[name] = t
        return t

    def dout(self, name, shape, dt):
        t = self.nc.dram_tensor(name, list(shape), dt, kind="ExternalOutput").ap()
        self.dram[name] = t
        return t

    def _need(self, eng, tok):
        if tok[0] == "eng":
            _, f, n = tok
            if f == eng and eng == "pe":
                return
            sem, key, val = self.S[f], ("e", f), n
        else:
            _, sem, val = tok
            key = ("d", sem)
        if self.waited[eng].get(key, 0) >= val:
            return
        self.waited[eng][key] = val
        self.ops[eng].append(lambda e, sem=sem, val=val: e.wait_ge(sem, val))

    def _deps(self, eng, reads, writes):
        for b in reads:
            if b.w is not None:
                self._need(eng, b.w)
        for b in writes:
            if b.w is not None:
                self._need(eng, b.w)
            for t in b.r:
                self._need(eng, t)

    def op(self, eng, fn, reads=(), writes=()):
        self._deps(eng, reads, writes)
        self.N[eng] += 1
        tok = ("eng", eng, self.N[eng])
        sem = self.S[eng]
        self.ops[eng].append(lambda e, fn=fn, sem=sem: fn(e).then_inc(sem, 1))
        for b in reads:
            b.r.append(tok)
        for b in writes:
            b.w = tok
            b.r = []
        return tok

    def pe(self, fn, reads=(), writes=()):
        return self.op("pe", fn, reads, writes)

    def act(self, fn, reads=(), writes=()):
        return self.op("act", fn, reads, writes)

    def dve(self, fn, reads=(), writes=()):
        return self.op("dve", fn, reads, writes)

    def pool(self, fn, reads=(), writes=()):
        return self.op("pool", fn, reads, writes)

    def dma(self, out_ap, in_ap, reads=(), writes=(), q="sp", sem=None):
        self._deps(q, reads, writes)
        if sem is None:
            if writes and not reads:
                b = writes[0]
                if b.lsem is None:
                    b.lsem = self.newsem("l")
                sem = b.lsem
            else:
                b = reads[0]
                if b.ssem is None:
                    b.ssem = self.newsem("s")
                sem = b.ssem
        self.semval[sem] += 16
        val = self.semval[sem]
        self.ops[q].append(lambda e, o=out_ap, i=in_ap, sem=sem: e.dma_start(out=o, in_=i).then_inc(sem, 16))
        tok = ("dma", sem, val)
        for b in reads:
            b.r.append(tok)
        for b in writes:
            b.w = tok
            b.r = []
        return tok

    def barrier(self):
        for e in self.ENG:
            for f in self.ENG:
                if f != e and self.N[f] > 0:
                    self._need(e, ("eng", f, self.N[f]))
            for s, v in self.semval.items():
                if v > 0:
                    self._need(e, ("dma", s, v))
            for s, v in self.ccsems.items():
                self._need(e, ("dma", s, v))

    def finish(self):
        for s, v in self.semval.items():
            if v > 0:
                self._need("sp", ("dma", s, v))
        ops = self.ops
        with self.nc.Block() as block:
            @block.tensor
            def _(e):
                for o in ops["pe"]:
                    o(e)

            @block.scalar
            def _(e):
                for o in ops["act"]:
                    o(e)

            @block.vector
            def _(e):
                for o in ops["dve"]:
                    o(e)

            @block.gpsimd
            def _(e):
                for o in ops["pool"]:
                    o(e)

            @block.sync
            def _(e):
                for o in ops["sp"]:
                    o(e)
        self.es.close()
        return self.nc


class Ring:
    def __init__(self, P, name, shape, dt, n, es=None, psum=False):
        self.n = n
        self.i = 0
        alloc = P.ps if psum else P.sb
        self.t = [alloc(name, [128] + list(shape), dt, es) for _ in range(n)]
        self.b = [Buf(f"{name}{k}") for k in range(n)]

    def next(self):
        k = self.i % self.n
        self.i += 1
        return self.t[k], self.b[k]


def emit_consts(P):
    c = {}
    ident_d = P.din("c_ident", [128, 128], BF16)
    ones_d = P.din("c_ones", [128, 128], BF16)
    onesf_d = P.din("c_onesf", [128, 128], F32)
    c["ident"] = P.sb("ident", [128, 128], BF16)
    c["ones"] = P.sb("ones", [128, 128], BF16)
    c["onesf"] = P.sb("onesf", [128, 128], F32)
    c["eps"] = P.sb("eps", [128, 1], F32)
    cb = Buf("consts")
    c["buf"] = cb
    P.dma(c["ident"][:], ident_d, writes=[cb])
    P.dma(c["ones"][:], ones_d, writes=[cb])
    P.dma(c["onesf"][:], onesf_d, writes=[cb])
    eb = Buf("eps")
    P.pool(lambda e: e.memset(c["eps"][:], EPS), writes=[eb])
    c["epsb"] = eb
    return c


def emit_mod(P, C, cvec_d, adaw_d, adab_d, psring, dst_fn):
    with ExitStack() as es:
        cv = P.sb("cv", [128, NCH], F32, es)
        crep = P.sb("crep", [128, NCH, 128], F32, es)
        abrow = P.sb("abrow", [1, 3 * D], F32, es)
        cvb, crb, abb = Buf(), Buf(), Buf()
        P.dma(cv[:], cvec_d, writes=[cvb])
        P.dma(abrow[:], adab_d, writes=[abb])
        P.act(lambda e: e.activation(out=cv[:], in_=cv[:], func=AF.Silu), reads=[cvb], writes=[cvb])
        P.dve(lambda e: e.tensor_copy(crep[:], cv[:].unsqueeze(2).to_broadcast([128, NCH, 128])), reads=[cvb], writes=[crb])
        awr = Ring(P, "aw", [NCH, 512], F32, 2, es)
        awv = adaw_d.rearrange("(j p) n -> p j n", p=128)
        for nt in range(12):
            aw, awb = awr.next()
            P.dma(aw[:], awv[:, :, nt * 512:(nt + 1) * 512], writes=[awb])
            pst, psb = psring.next()
            for j in range(NCH):
                P.pe(lambda e, j=j, aw=aw, pst=pst: e.matmul(pst[:], crep[:, j, :], aw[:, j, :], start=(j == 0), stop=False),
                     reads=[crb, awb], writes=[psb])
            P.pe(lambda e, nt=nt, pst=pst: e.matmul(pst[:], C["onesf"][0:1, :], abrow[0:1, nt * 512:(nt + 1) * 512], start=False, stop=True),
                 reads=[C["buf"], abb], writes=[psb])
            dst, dstb = dst_fn(nt)
            P.act(lambda e, dst=dst, pst=pst: e.activation(out=dst, in_=pst[:], func=AF.Copy), reads=[psb], writes=[dstb])
        P.barrier()


class HBuilder:
    def __init__(self, P, C, shift_bc, gs_bc, modb, ptring):
        self.P, self.C = P, C
        self.shift_bc, self.gs_bc, self.modb = shift_bc, gs_bc, modb
        self.xr = Ring(P, "xt", [D], F32, 2)
        self.hr = Ring(P, "hb", [D], BF16, 2)
        self.ssr = Ring(P, "ss", [1], F32, 2)
        self.sdr = Ring(P, "sd", [1], F32, 2)
        self.ptr = ptring
        self.k = 0

    def tile(self, srcs, hT, hTb, col0):
        P, C = self.P, self.C
        xt, xb = self.xr.next()
        hb, hbb = self.hr.next()
        ss, ssb = self.ssr.next()
        sd, sdb = self.sdr.next()
        for (p0, p1, ap) in srcs:
            P.dma(xt[p0:p1, :], ap, writes=[xb])
        P.pool(lambda e: e.memset(ss[:], 0.0), writes=[ssb])
        P.act(lambda e: e.activation(out=hb[:], in_=xt[:], func=AF.Square, accum_out=ss[:]), reads=[xb], writes=[hbb, ssb])
        P.act(lambda e: e.activation(out=sd[:], in_=ss[:], func=AF.Sqrt, scale=1.0 / D, bias=C["eps"][:, 0:1]),
              reads=[ssb, C["epsb"]], writes=[sdb])
        P.dve(lambda e: e.reciprocal(out=sd[:], in_=sd[:]), reads=[sdb], writes=[sdb])
        P.dve(lambda e: e.scalar_tensor_tensor(out=xt[:], in0=xt[:], scalar=sd[:, 0:1], in1=self.gs_bc, op0=ALU.mult, op1=ALU.mult),
              reads=[xb, sdb, self.modb], writes=[xb])
        P.pool(lambda e: e.tensor_tensor(out=hb[:], in0=xt[:], in1=self.shift_bc, op=ALU.add), reads=[xb, self.modb], writes=[hbb])
        for half in range(2):
            pt, ptb = self.ptr.next()
            for c in range(8):
                ch = half * 8 + c
                P.pe(lambda e, pt=pt, c=c, ch=ch: e.transpose(pt[:, c, :], hb[:, ch * 128:(ch + 1) * 128], C["ident"][:]),
                     reads=[hbb, C["buf"]], writes=[ptb])
            dst = hT[:, half * 8:half * 8 + 8, col0:col0 + 128]
            if (self.k + half) % 2 == 0:
                P.act(lambda e, dst=dst, pt=pt: e.activation(out=dst, in_=pt[:], func=AF.Copy), reads=[ptb], writes=[hTb])
            else:
                P.dve(lambda e, dst=dst, pt=pt: e.tensor_copy(dst, pt[:]), reads=[ptb], writes=[hTb])
        self.k += 1


class WStream:
    def __init__(self, P, w_d):
        self.P = P
        self.wv = w_d.rearrange("(j p) n -> p j n", p=128)
        self.wfr = Ring(P, "wf", [4, 512], F32, 4)
        self.wbr = Ring(P, "wb", [NCH, 512], BF16, 2)
        self.pending = {}
        self.k = 0

    def dma(self, g):
        wb, wbb = self.wbr.next()
        st = []
        for jq in range(4):
            wf, wfb = self.wfr.next()
            self.P.dma(wf[:], self.wv[:, 4 * jq:4 * jq + 4, g * 512:(g + 1) * 512], writes=[wfb])
            st.append((wf, wfb))
        self.pending[g] = (wb, wbb, st)

    def casts(self, g):
        P = self.P
        wb, wbb, st = self.pending.pop(g)
        self.k += 1
        for jq, (wf, wfb) in enumerate(st):
            dst = wb[:, 4 * jq:4 * jq + 4, :]
            if self.k % 2 == 0:
                P.dve(lambda e, dst=dst, wf=wf: e.tensor_copy(dst, wf[:]), reads=[wfb], writes=[wbb])
            else:
                P.act(lambda e, dst=dst, wf=wf: e.activation(out=dst, in_=wf[:], func=AF.Copy), reads=[wfb], writes=[wbb])
        return wb, wbb


def emit_A(P, C, T):
    P.begin_phase()
    x_d, cvec_d, g_d, adaw_d, adab_d, win_d = T["x"], T["cvec"], T["norm_g0"], T["ada_w0"], T["ada_b0"], T["w_in0"]
    qg_d, kg_d, rc_d, rs_d = T["qg"], T["kg"], T["ropeC"], T["ropeS"]
    QT_d, SG_d, GB_d = T["QT"], T["SG"], T["gate_bc0"]

    psA = Ring(P, "psA", [512], F32, 3, psum=True)
    psB = Ring(P, "psB", [512], F32, 2, psum=True)
    psT = Ring(P, "psT", [8, 128], BF16, 2, psum=True)

    modp = P.sb("modp", [128, 2 * D], F32)
    modb = Buf("modp")
    gains = P.sb("gains", [128, 2], F32)
    gainb = Buf("gains")
    P.dma(gains[:, 0:1], qg_d, writes=[gainb])
    P.dma(gains[:, 1:2], kg_d, writes=[gainb])

    with ExitStack() as es:
        gtmp = P.sb("gtmp", [128, D], F32, es)
        gbc = P.sb("gbc", [128, D], F32, es)
        gtb, gbb = Buf(), Buf()
        P.dma(gbc[:], g_d.partition_broadcast(128).rearrange("p o n -> p (o n)"), writes=[gbb])

        def dst_fn(nt):
            if nt < 8:
                return modp[:, nt * 512:(nt + 1) * 512], modb
            return gtmp[:, (nt - 8) * 512:(nt - 7) * 512], gtb
        emit_mod(P, C, cvec_d, adaw_d, adab_d, psA, dst_fn)
        P.dma(GB_d, gtmp[:], reads=[gtb])
        P.dve(lambda e: e.scalar_tensor_tensor(out=modp[:, D:2 * D], in0=modp[:, D:2 * D], scalar=1.0, in1=gbc[:], op0=ALU.add, op1=ALU.mult),
              reads=[modb, gbb], writes=[modb])
        P.barrier()

    hb_ = HBuilder(P, C, modp[:, 0:D], modp[:, D:2 * D], modb, psT)
    TH = 2048
    hT = P.sb("hT", [128, NCH, TH], BF16)
    hTb = Buf("hT")
    rC = P.sb("rC", [128, TH], F32)
    rS = P.sb("rS", [128, TH], F32)
    ropeb = Buf("rope")
    W = WStream(P, win_d)
    sqr = Ring(P, "sq", [512], BF16, 2)
    sdr = Ring(P, "sdq", [512], F32, 2)
    qnr = Ring(P, "qn", [512], F32, 2)
    t1r = Ring(P, "t1", [512], F32, 2)
    t2r = Ring(P, "t2", [512], F32, 2)
    qor = Ring(P, "qo", [512], BF16, 3)
    vor = qor
    NG = 10

    for th in range(2):
        tok0 = th * TH
        Vv = T["Vown"][th]
        W.dma(0)
        for tt in range(TH // 128):
            r0 = tok0 + tt * 128
            hb_.tile([(0, 128, x_d[r0:r0 + 128, :])], hT, hTb, tt * 128)
        P.dma(rC[:], rc_d[:, tok0:tok0 + TH], writes=[ropeb])
        P.dma(rS[:], rs_d[:, tok0:tok0 + TH], writes=[ropeb])
        pend_a, pend_b = [], []

        def stage1a(st):
            pst, psb, sq, sqb, t5, cc = st["pst"], st["psb"], st["sq"], st["sqb"], st["t5"], st["cc"]
            gcol = gains[:, 0:1] if cc < 16 else gains[:, 1:2]
            pB, pBb = psB.next()
            P.pe(lambda e: e.matmul(pB[:], C["ones"][:], sq[:], start=True, stop=True), reads=[C["buf"], sqb], writes=[pBb])
            sd, sdb = sdr.next()
            qn, qnb = qnr.next()
            t1, t1b = t1r.next()
            t2, t2b = t2r.next()
            P.act(lambda e: e.activation(out=sd[:], in_=pB[:], func=AF.Ln, scale=1.0 / HD, bias=C["eps"][:, 0:1]),
                  reads=[pBb, C["epsb"]], writes=[sdb])
            P.act(lambda e: e.activation(out=sd[:], in_=sd[:], func=AF.Exp, scale=-0.5), reads=[sdb], writes=[sdb])
            P.dve(lambda e: e.scalar_tensor_tensor(out=qn[:], in0=pst[:], scalar=gcol, in1=sd[:], op0=ALU.mult, op1=ALU.mult),
                  reads=[psb, sdb, gainb], writes=[qnb])
            cs = slice(t5 * 512, (t5 + 1) * 512)
            P.dve(lambda e: e.tensor_tensor(out=t1[:], in0=qn[:], in1=rC[:, cs], op=ALU.mult), reads=[qnb, ropeb], writes=[t1b])
            P.pool(lambda e: e.tensor_tensor(out=t2[0:64, :], in0=qn[64:128, :], in1=rS[64:128, cs], op=ALU.mult), reads=[qnb, ropeb], writes=[t2b])
            P.pool(lambda e: e.tensor_tensor(out=t2[64:128, :], in0=qn[0:64, :], in1=rS[0:64, cs], op=ALU.mult), reads=[qnb, ropeb], writes=[t2b])
            st.update(t1=t1, t1b=t1b, t2=t2, t2b=t2b)

        def stage1b(st):
            t1, t1b, t2, t2b, t5, cc = st["t1"], st["t1b"], st["t2"], st["t2b"], st["t5"], st["cc"]
            qo, qob = qor.next()
            P.dve(lambda e: e.tensor_tensor(out=qo[:], in0=t1[:], in1=t2[:], op=ALU.add), reads=[t1b, t2b], writes=[qob])
            if cc < 16:
                dstd = QT_d[cc, :, tok0 + t5 * 512: tok0 + (t5 + 1) * 512]
            else:
                dstd = T["KTown"][(cc - 16) // 2][((cc - 16) % 2) * 128:((cc - 16) % 2) * 128 + 128, tok0 + t5 * 512: tok0 + (t5 + 1) * 512]
            P.dma(dstd, qo[:], reads=[qob])

        def flush():
            while pend_a:
                st = pend_a.pop(0)
                stage1a(st)
                pend_b.append(st)
            while pend_b:
                stage1b(pend_b.pop(0))

        wb, wbb = W.casts(0)
        for g in range(NG):
            if g + 1 < NG:
                W.dma(g + 1)
            nxt = None
            if g == 5:
                flush()
                for tt in range(TH // 128):
                    pst, psb = psA.next()
                    for j in range(NCH):
                        P.pe(lambda e, pst=pst, j=j, tt=tt, wb=wb: e.matmul(pst[:], hT[:, j, tt * 128:(tt + 1) * 128], wb[:, j, :], start=(j == 0), stop=(j == NCH - 1)),
                             reads=[hTb, wbb], writes=[psb])
                    vo, vob = vor.next()
                    if tt % 2 == 0:
                        P.dve(lambda e, vo=vo, pst=pst: e.tensor_copy(vo[:], pst[:]), reads=[psb], writes=[vob])
                    else:
                        P.act(lambda e, vo=vo, pst=pst: e.activation(out=vo[:], in_=pst[:], func=AF.Copy), reads=[psb], writes=[vob])
                    P.dma(Vv[tt * 128:(tt + 1) * 128, :], vo[:], reads=[vob])
                    if tt == 8:
                        nxt = W.casts(g + 1)
                wb, wbb = nxt
                continue
            for ci in range(4):
                cc = 4 * g + ci if g < 5 else 4 * g + ci
                if ci == 2 and g + 1 < NG:
                    nxt = W.casts(g + 1)
                for t5 in range(TH // 512):
                    pst, psb = psA.next()
                    for j in range(NCH):
                        P.pe(lambda e, pst=pst, j=j, t5=t5, wb=wb, ci=ci: e.matmul(pst[:], wb[:, j, ci * 128:(ci + 1) * 128], hT[:, j, t5 * 512:(t5 + 1) * 512], start=(j == 0), stop=(j == NCH - 1)),
                             reads=[hTb, wbb], writes=[psb])
                    if cc < 20:
                        sq, sqb = sqr.next()
                        P.act(lambda e, sq=sq, pst=pst: e.activation(out=sq[:], in_=pst[:], func=AF.Square), reads=[psb], writes=[sqb])
                        if pend_b:
                            stage1b(pend_b.pop(0))
                        if pend_a:
                            st = pend_a.pop(0)
                            stage1a(st)
                            pend_b.append(st)
                        pend_a.append(dict(pst=pst, psb=psb, sq=sq, sqb=sqb, t5=t5, cc=cc))
                    else:
                        qo, qob = qor.next()
                        P.act(lambda e, qo=qo, pst=pst: e.activation(out=qo[:], in_=pst[:], func=AF.Silu), reads=[psb], writes=[qob])
                        P.dma(SG_d[cc - 24, :, tok0 + t5 * 512: tok0 + (t5 + 1) * 512], qo[:], reads=[qob])
            if g == 4:
                flush()
            if nxt is not None:
                wb, wbb = nxt
    P.barrier()
    agb = Buf("ag")
    for j in range(2):
        P.allgather(T["KTown"][j], T["KTag"][j], writes=[agb])
        P.allgather(T["Vown"][j], T["Vag"][j], writes=[agb])
    P.end_phase()


def host_consts():
    return {
        "c_ident": np.eye(128, dtype=np.float32).astype(NPBF),
        "c_ones": np.ones((128, 128), np.float32).astype(NPBF),
        "c_onesf": np.ones((128, 128), np.float32),
    }


def rope_tables():
    t = np.arange(S)
    rows = (t // 64).astype(np.float32)
    cols = (t % 64).astype(np.float32)
    inv = (np.float32(10000.0) ** (-np.arange(0, 64, 2, dtype=np.float32) / np.float32(64))).astype(np.float32)
    ang = np.concatenate([rows[:, None] * inv[None, :], cols[:, None] * inv[None, :]], axis=-1)
    c = np.cos(ang).T.astype(np.float32)
    s = np.sin(ang).T.astype(np.float32)
    return np.concatenate([c, c], 0), np.concatenate([s, -s], 0)


PERM = np.concatenate([np.arange(0, 128, 2), np.arange(1, 128, 2)])


def emit_B(P, C, T):
    P.begin_phase()
    QT_d, SG_d, OG_d = T["QT"], T["SG"], T["OGT"]
    psS = Ring(P, "psS", [512], F32, 4, psum=True)
    psO = Ring(P, "psO", [512], F32, 2, psum=True)
    psL = Ring(P, "psL", [512], F32, 2, psum=True)
    kr = Ring(P, "kt", [S], BF16, 2)
    vr = Ring(P, "vt", [64, 128], BF16, 2)
    qr = Ring(P, "qt", [TOWN], BF16, 2)
    sgr = Ring(P, "sg", [TOWN], BF16, 2)
    pr = Ring(P, "p", [512], BF16, 4)
    rlr = Ring(P, "rl", [512], F32, 2)
    lnr = Ring(P, "lnl", [512], F32, 2)
    asr = Ring(P, "asum", [512], BF16, 2)
    aDr = Ring(P, "accD", [512], F32, 2)
    aPr = Ring(P, "accP", [512], F32, 2)
    o1r = Ring(P, "o1", [512], F32, 2)
    ogr = Ring(P, "og", [512], BF16, 2)
    scale = 1.0 / math.sqrt(HD)
    NKB = S // 128
    pend = []

    def front(st):
        pS, pSb = psS.next()
        p, pb = pr.next()
        kt, ktb, qt_, qb, qi, kb = st["kt"], st["ktb"], st["qt"], st["qb"], st["qi"], st["kb"]
        P.pe(lambda e: e.matmul(pS[:], kt[:, kb * 128:(kb + 1) * 128], qt_[:, qi * 512:(qi + 1) * 512], start=True, stop=True),
             reads=[ktb, qb], writes=[pSb])
        P.act(lambda e: e.activation(out=p[:], in_=pS[:], func=AF.Exp, scale=scale), reads=[pSb], writes=[pb])
        st["p"], st["pb"] = p, pb

    def back(st):
        p, pb, vt, vtb, kb = st["p"], st["pb"], st["vt"], st["vtb"], st["kb"]
        pO, pOb = st["pO"], st["pOb"]
        accs = st["accs"]
        P.pe(lambda e: e.matmul(pO[:], vt[:, kb, :], p[:], start=(kb == 0), stop=(kb == NKB - 1)), reads=[vtb, pb], writes=[pOb])
        which = "pool" if kb % 2 == 1 else "dve"
        acc, accb = accs[which]
        first = kb < 3 and (kb == 2 or kb == 0)
        if which == "dve":
            if kb == 0:
                P.dve(lambda e: e.tensor_copy(acc[:], p[:]), reads=[pb], writes=[accb])
            else:
                P.dve(lambda e: e.tensor_tensor(out=acc[:], in0=acc[:], in1=p[:], op=ALU.add), reads=[pb, accb], writes=[accb])
        else:
            if kb == 1:
                P.pool(lambda e: e.tensor_copy(acc[:], p[:]), reads=[pb], writes=[accb])
            else:
                P.pool(lambda e: e.tensor_tensor(out=acc[:], in0=acc[:], in1=p[:], op=ALU.add), reads=[pb, accb], writes=[accb])
        if kb == NKB - 1:
            h, qi, sg, sgb = st["h"], st["qi"], st["sg"], st["sgb"]
            pL, pLb = psL.next()
            asum, asumb = asr.next()
            ln, lnb = lnr.next()
            rl, rlb = rlr.next()
            o1, o1b = o1r.next()
            og, ogb = ogr.next()
            (aD, aDb), (aP, aPb) = accs["dve"], accs["pool"]
            P.pool(lambda e: e.tensor_tensor(out=asum[:], in0=aD[:], in1=aP[:], op=ALU.add), reads=[aDb, aPb], writes=[asumb])
            P.pe(lambda e: e.matmul(pL[:], C["ones"][:], asum[:], start=True, stop=True), reads=[C["buf"], asumb], writes=[pLb])
            P.act(lambda e: e.activation(out=ln[:], in_=pL[:], func=AF.Ln), reads=[pLb], writes=[lnb])
            P.act(lambda e: e.activation(out=rl[:], in_=ln[:], func=AF.Exp, scale=-1.0), reads=[lnb], writes=[rlb])
            P.dve(lambda e: e.tensor_tensor(out=o1[:], in0=pO[:], in1=rl[:], op=ALU.mult), reads=[pOb, rlb], writes=[o1b])
            P.pool(lambda e: e.tensor_tensor(out=og[:], in0=o1[:], in1=sg[:, qi * 512:(qi + 1) * 512], op=ALU.mult), reads=[o1b, sgb], writes=[ogb])
            P.dma(OG_d[h, :, qi * 512:(qi + 1) * 512], og[:], reads=[ogb])

    for g in range(4):
        kt, ktb = kr.next()
        vt, vtb = vr.next()
        P.dma(kt[:].rearrange("p (r t) -> p r t", r=2),
              T["KTag"][g // 2].rearrange("(r k p) t -> p k r t", r=2, k=2)[:, g % 2], writes=[ktb])
        for j in range(2):
            for r in range(2):
                P.dma(vt[:, r * 32 + j * 16:r * 32 + j * 16 + 16, :],
                      T["Vag"][j][r * 2048:(r + 1) * 2048, g * 128:(g + 1) * 128].rearrange("(kk p) d -> p kk d", p=128), writes=[vtb])
        for hh in range(4):
            h = 4 * g + hh
            qt_, qb = qr.next()
            sg, sgb = sgr.next()
            P.dma(qt_[:], QT_d[h], writes=[qb])
            P.dma(sg[:], SG_d[h], writes=[sgb])
            for qi in range(TOWN // 512):
                pO, pOb = psO.next()
                accs = {"dve": aDr.next(), "pool": aPr.next()}
                for kb in range(NKB):
                    st = dict(kt=kt, ktb=ktb, vt=vt, vtb=vtb, qt=qt_, qb=qb, sg=sg, sgb=sgb, h=h, qi=qi, kb=kb,
                              pO=pO, pOb=pOb, accs=accs)
                    front(st)
                    pend.append(st)
                    if len(pend) > 2:
                        back(pend.pop(0))
    while pend:
        back(pend.pop(0))
    P.end_phase()


def emit_CF(P, C, T, final):
    P.begin_phase()
    if final:
        w_d, x_d, gb_d = T["w_out1"], T["x1"], T["gate_bc1"]
        sg_d, fg_d, out_d = T["SG1"], T["final_g"], T["out"]
        xv = x_d.rearrange("(a b) m -> b a m", b=64)
        ov = out_d.rearrange("(a b) m -> b a m", b=64)
        sel = P.sb("sel", [128, 2], F32)
        selb = Buf("sel")
        P.dma(sel[:], T["sel"], writes=[selb])
    else:
        w_d, x_d, gb_d = T["w_out0"], T["x"], T["gate_bc0"]
        OG_d, out_d = T["OGT"], T["x1"]
    psY = Ring(P, "psY", [512], F32, 6 if final else 8, psum=True)
    wbf = P.sb("wbf", [128, NCH, D], BF16)
    wbb = Buf("wbf")
    gbc = P.sb("gbc", [128, D], F32)
    gbb = Buf("gbc")
    P.dma(gbc[:], gb_d, writes=[gbb])
    wfr = Ring(P, "wf", [4, 512], F32, 3)
    wv = w_d.rearrange("(j p) n -> p j n", p=128)
    wbn = [Buf(f"wbf{n}") for n in range(4)]
    for n in range(4):
        gsl = gbc[:, n * 512:(n + 1) * 512].unsqueeze(1).to_broadcast([128, 4, 512])
        for jq in range(4):
            wf, wfb = wfr.next()
            P.dma(wf[:], wv[:, 4 * jq:4 * jq + 4, n * 512:(n + 1) * 512], writes=[wfb])
            dst = wbf[:, 4 * jq:4 * jq + 4, n * 512:(n + 1) * 512]
            P.dve(lambda e, wf=wf, dst=dst, gsl=gsl: e.tensor_tensor(out=dst, in0=wf[:], in1=gsl, op=ALU.mult), reads=[wfb, gbb], writes=[wbn[n]])
    xr = Ring(P, "xt", [D], F32, 3)
    if final:
        psT = Ring(P, "psT", [8, 128], BF16, 2, psum=True)
        fgb_t = P.sb("fgbc", [128, D], F32)
        fgb = Buf("fg")
        P.dma(fgb_t[:], fg_d.partition_broadcast(128).rearrange("p o n -> p (o n)"), writes=[fgb])
        fr = Ring(P, "ft", [D], BF16, 2)
        b0r = Ring(P, "b0", [1024], BF16, 2)
        b1r = Ring(P, "b1", [1024], BF16, 2)
        sgr = Ring(P, "sgt", [D], BF16, 2)
        obr = Ring(P, "ogb", [D], BF16, 2)
        otr = Ring(P, "ogT", [NCH, 128], BF16, 2)
        ssr = Ring(P, "ss", [1], F32, 2)
        sdr = Ring(P, "sd", [1], F32, 2)
        jr = Ring(P, "junk", [D], BF16, 1)
    else:
        ogr = Ring(P, "og", [NCH, 512], BF16, 2)
        OGv = OG_d.rearrange("c p t -> p c t")
    cur_og = [None]

    def front(tt):
        r0 = tt * 128
        st = dict(tt=tt)
        if final:
            ft, ftb = fr.next()
            sg, sgb = sgr.next()
            ob, obb = obr.next()
            oT, oTb = otr.next()
            b0, b0b = b0r.next()
            b1, b1b = b1r.next()
            fob = Buf("fo")
            P.dma(ft[:, 0:1024], T["f_own"][r0:r0 + 128, :], writes=[fob])
            fj, fr0 = tt // 8, (tt % 8) * 128
            P.dma(b0[:], T["fag"][fj][fr0:fr0 + 128, :], writes=[b0b])
            P.dma(b1[:], T["fag"][fj][1024 + fr0:1024 + fr0 + 128, :], writes=[b1b])
            P.dma(sg[:], sg_d[r0:r0 + 128, :], writes=[sgb])
            P.dve(lambda e: e.tensor_scalar(out=b1[:], in0=b1[:], scalar1=sel[:, 1:2], scalar2=None, op0=ALU.mult), reads=[b1b, selb], writes=[b1b])
            P.dve(lambda e: e.scalar_tensor_tensor(out=ft[:, 1024:2048], in0=b0[:], scalar=sel[:, 0:1], in1=b1[:], op0=ALU.mult, op1=ALU.add),
                  reads=[b0b, b1b, selb, fob], writes=[ftb])
            P.dve(lambda e: e.tensor_tensor(out=ob[:], in0=ft[:], in1=sg[:], op=ALU.mult), reads=[ftb, fob, sgb], writes=[obb])
            for half in range(2):
                pt, ptb = psT.next()
                for c in range(8):
                    ch = half * 8 + c
                    P.pe(lambda e, pt=pt, c=c, ch=ch: e.transpose(pt[:, c, :], ob[:, ch * 128:(ch + 1) * 128], C["ident"][:]),
                         reads=[obb, C["buf"]], writes=[ptb])
                P.act(lambda e, pt=pt, half=half: e.activation(out=oT[:, half * 8:half * 8 + 8, :], in_=pt[:], func=AF.Copy), reads=[ptb], writes=[oTb])
            st["lhs"] = lambda c: oT[:, c, :]
            st["lb"] = oTb
        else:
            if tt % 4 == 0:
                og, ogb = ogr.next()
                P.dma(og[:], OGv[:, :, r0:r0 + 512], writes=[ogb])
                cur_og[0] = (og, ogb)
            og, ogb = cur_og[0]
            ti = tt % 4
            st["lhs"] = lambda c: og[:, c, ti * 128:(ti + 1) * 128]
            st["lb"] = ogb
        xt, xb = xr.next()
        if final:
            P.dma(xt[0:64, :], xv[2 * tt], writes=[xb])
            P.dma(xt[64:128, :], xv[2 * tt + 1], writes=[xb])
        else:
            P.dma(xt[:], x_d[r0:r0 + 128, :], writes=[xb])
        st["xt"], st["xb"] = xt, xb
        return st

    def mm(st):
        lhs, lb = st["lhs"], st["lb"]
        st["ps"] = []
        for n in range(4):
            ps, psb = psY.next()
            for c in range(NCH):
                P.pe(lambda e, ps=ps, c=c, n=n: e.matmul(ps[:], lhs(c), wbf[:, c, n * 512:(n + 1) * 512], start=(c == 0), stop=(c == NCH - 1)),
                     reads=[lb, wbn[n]], writes=[psb])
            st["ps"].append((ps, psb))

    def back(st):
        tt, xt, xb = st["tt"], st["xt"], st["xb"]
        r0 = tt * 128
        for n, (ps, psb) in enumerate(st["ps"]):
            P.dve(lambda e, ps=ps, n=n: e.tensor_tensor(out=xt[:, n * 512:(n + 1) * 512], in0=ps[:], in1=xt[:, n * 512:(n + 1) * 512], op=ALU.add),
                  reads=[psb, xb], writes=[xb])
        if final:
            ss, ssb = ssr.next()
            sd, sdb = sdr.next()
            jk, jkb = jr.next()
            P.pool(lambda e: e.memset(ss[:], 0.0), writes=[ssb])
            P.act(lambda e: e.activation(out=jk[:], in_=xt[:], func=AF.Square, accum_out=ss[:]), reads=[xb], writes=[jkb, ssb])
            P.act(lambda e: e.activation(out=sd[:], in_=ss[:], func=AF.Sqrt, scale=1.0 / D, bias=C["eps"][:, 0:1]),
                  reads=[ssb, C["epsb"]], writes=[sdb])
            P.dve(lambda e: e.reciprocal(out=sd[:], in_=sd[:]), reads=[sdb], writes=[sdb])
            P.dve(lambda e: e.scalar_tensor_tensor(out=xt[:], in0=xt[:], scalar=sd[:, 0:1], in1=fgb_t[:], op0=ALU.mult, op1=ALU.mult),
                  reads=[xb, sdb, fgb], writes=[xb])
            P.dma(ov[2 * tt], xt[0:64, :], reads=[xb])
            P.dma(ov[2 * tt + 1], xt[64:128, :], reads=[xb])
        else:
            P.dma(out_d[r0:r0 + 128, :], xt[:], reads=[xb])

    NT = TOWN // 128
    prev = front(0)
    mm(prev)
    for tt in range(1, NT):
        cur = front(tt)
        back(prev)
        mm(cur)
        prev = cur
    back(prev)
    P.end_phase()


def emit_D(P, C, T):
    P.begin_phase()
    x_d, cvec_d, g_d, adaw_d, adab_d, win_d = T["x1"], T["cvec"], T["norm_g1"], T["ada_w1"], T["ada_b1"], T["w_in1"]
    SG_d, GB_d = T["SG1"], T["gate_bc1"]
    psA = Ring(P, "psA", [512], F32, 4, psum=True)
    psT = Ring(P, "psT", [8, 128], BF16, 2, psum=True)
    modp = P.sb("modp", [128, 2 * D], F32)
    modb = Buf("modp")
    with ExitStack() as es:
        gtmp = P.sb("gtmp", [128, D], F32, es)
        gbc = P.sb("gbc", [128, D], F32, es)
        gtb, gbb = Buf(), Buf()
        P.dma(gbc[:], g_d.partition_broadcast(128).rearrange("p o n -> p (o n)"), writes=[gbb])

        def dst_fn(nt):
            if nt < 8:
                return modp[:, nt * 512:(nt + 1) * 512], modb
            return gtmp[:, (nt - 8) * 512:(nt - 7) * 512], gtb
        emit_mod(P, C, cvec_d, adaw_d, adab_d, psA, dst_fn)
        P.dma(GB_d, gtmp[:], reads=[gtb])
        P.dve(lambda e: e.scalar_tensor_tensor(out=modp[:, D:2 * D], in0=modp[:, D:2 * D], scalar=1.0, in1=gbc[:], op0=ALU.add, op1=ALU.mult),
              reads=[modb, gbb], writes=[modb])
        P.barrier()
    hb_ = HBuilder(P, C, modp[:, 0:D], modp[:, D:2 * D], modb, psT)
    TH = 2048
    hT = P.sb("hT", [128, NCH, TH], BF16)
    hTb = Buf("hT")
    W = WStream(P, win_d)
    uor = Ring(P, "uo", [512], BF16, 3)
    gor = Ring(P, "go", [512], BF16, 3)
    xv = x_d.rearrange("(a b) m -> b a m", b=64)
    NG = 8
    for th in range(2):
        tok0 = th * TH
        W.dma(0)
        for tt in range(TH // 128):
            t2a = (tok0 + tt * 128) // 64
            hb_.tile([(0, 64, xv[t2a]), (64, 128, xv[t2a + 1])], hT, hTb, tt * 128)
        wb, wbb = W.casts(0)
        for g in range(NG):
            if g + 1 < NG:
                W.dma(g + 1)
            nxt = None
            if g < 4:
                for ci in range(4):
                    cc = 4 * g + ci
                    if ci == 2 and g + 1 < NG:
                        nxt = W.casts(g + 1)
                    for t5 in range(TH // 512):
                        pst, psb = psA.next()
                        for j in range(NCH):
                            P.pe(lambda e, pst=pst, j=j, t5=t5, wb=wb, ci=ci: e.matmul(pst[:], wb[:, j, ci * 128:(ci + 1) * 128], hT[:, j, t5 * 512:(t5 + 1) * 512], start=(j == 0), stop=(j == NCH - 1)),
                                 reads=[hTb, wbb], writes=[psb])
                        uo, uob = uor.next()
                        if t5 % 2 == 0:
                            P.act(lambda e, uo=uo, pst=pst: e.activation(out=uo[:], in_=pst[:], func=AF.Copy), reads=[psb], writes=[uob])
                        else:
                            P.dve(lambda e, uo=uo, pst=pst: e.tensor_copy(uo[:], pst[:]), reads=[psb], writes=[uob])
                        if cc < 8:
                            udst = T["UTown"][cc, :, tok0 + t5 * 512: tok0 + (t5 + 1) * 512]
                        else:
                            udst = T["UTsend"][(cc - 8) // 2][((cc - 8) % 2) * 128:((cc - 8) % 2) * 128 + 128, tok0 + t5 * 512: tok0 + (t5 + 1) * 512]
                        P.dma(udst, uo[:], reads=[uob])
            else:
                for tt in range(TH // 128):
                    if tt == 8 and g + 1 < NG:
                        nxt = W.casts(g + 1)
                    pst, psb = psA.next()
                    for j in range(NCH):
                        P.pe(lambda e, pst=pst, j=j, tt=tt, wb=wb: e.matmul(pst[:], hT[:, j, tt * 128:(tt + 1) * 128], wb[:, j, :], start=(j == 0), stop=(j == NCH - 1)),
                             reads=[hTb, wbb], writes=[psb])
                    go, gob = gor.next()
                    P.act(lambda e, go=go, pst=pst: e.activation(out=go[:], in_=pst[:], func=AF.Silu), reads=[psb], writes=[gob])
                    r0 = tok0 + tt * 128
                    P.dma(SG_d[r0:r0 + 128, (g - 4) * 512:(g - 3) * 512], go[:], reads=[gob])
            if nxt is not None:
                wb, wbb = nxt
    P.barrier()
    agb = Buf("ag")
    for j in range(4):
        P.allgather(T["UTsend"][j], T["UTag"][j], writes=[agb])
    P.end_phase()


def fourier_tables():
    w = np.arange(256)
    aw = 2 * np.pi * np.outer(w, w) / 256
    csw = np.concatenate([np.cos(aw), np.sin(aw)], 1) / 16.0
    csw = csw.reshape(2, 128, 512).transpose(1, 0, 2)
    t1 = np.arange(128)
    a1 = 2 * np.pi * np.outer(t1, t1) / 128
    nrm = 1.0 / math.sqrt(8192.0)
    ra = np.concatenate([np.cos(a1), np.sin(a1)], 1) * nrm
    rb = np.concatenate([-np.sin(a1), np.cos(a1)], 1) * nrm
    m = np.arange(128)
    t2 = m % 64
    c2 = m // 64
    atw = 2 * np.pi * np.outer(t2, np.arange(128)) / 8192
    ct, st = np.cos(atw), np.sin(atw)
    n = np.arange(128)
    c2n, k2 = n // 64, n % 64
    a2 = 2 * np.pi * np.outer(t2, k2) / 64
    delta = (c2[:, None] == c2n[None, :]).astype(np.float64)
    re = delta * np.cos(a2)
    rf = -delta * np.sin(a2)
    return {
        "t_csw": np.ascontiguousarray(csw).astype(np.float32).astype(NPBF),
        "t_ra": ra.astype(np.float32).astype(NPBF),
        "t_rb": rb.astype(np.float32).astype(NPBF),
        "t_ct": ct.astype(np.float32),
        "t_st": st.astype(np.float32),
        "t_re": re.astype(np.float32).astype(NPBF),
        "t_rf": rf.astype(np.float32).astype(NPBF),
    }


def emit_E(P, C, T):
    P.begin_phase()
    csw_d, ra_d, rb_d, ct_d, st_d, re_d, rf_d = T["t_csw"], T["t_ra"], T["t_rb"], T["t_ct"], T["t_st"], T["t_re"], T["t_rf"]
    sel = P.sb("sel", [128, 2], F32)
    selb = Buf("sel")
    P.dma(sel[:], T["sel"], writes=[selb])
    csw = P.sb("csw", [128, 2, 512], BF16)
    ra = P.sb("ra", [128, 256], BF16)
    rb = P.sb("rb", [128, 256], BF16)
    ct = P.sb("ct", [128, 128], F32)
    st = P.sb("st", [128, 128], F32)
    re = P.sb("re", [128, 128], BF16)
    rf = P.sb("rf", [128, 128], BF16)
    tb = Buf("tables")
    for dst, src in ((csw, csw_d), (ra, ra_d), (rb, rb_d), (ct, ct_d), (st, st_d), (re, re_d), (rf, rf_d)):
        P.dma(dst[:], src, writes=[tb])
    ps1 = Ring(P, "ps1", [512], F32, 2, psum=True)
    ps2 = Ring(P, "ps2", [512], F32, 2, psum=True)
    ps3 = Ring(P, "ps3", [512], F32, 2, psum=True)
    AB = P.sb("AB", [128, 512, 64], BF16)
    ABb = Buf("AB")
    G2 = P.sb("G2", [128, 64, 256], BF16)
    G2b = Buf("G2")
    fS = P.sb("fS", [128, 64, 128], BF16)
    fSb = Buf("fS")
    ur = Ring(P, "u", [2, 8, 128], BF16, 3)
    upb_ = [Buf("up0"), Buf("up1"), Buf("up2")]
    b0r = Ring(P, "ub0", [2, 512], BF16, 2)
    b1r = Ring(P, "ub1", [2, 512], BF16, 2)
    efr = Ring(P, "ef", [512], F32, 2)
    m1r = Ring(P, "m1", [2, 128], F32, 2)
    m2r = Ring(P, "m2", [2, 128], F32, 2)
    m3r = Ring(P, "m3", [2, 128], F32, 2)
    m4r = Ring(P, "m4", [2, 128], F32, 2)
    fov = T["f_own"].rearrange("(t a e) c -> e t a c", t=64, a=32, e=2)
    fsv = [T["fsend"][j].rearrange("(t a e) c -> e t a c", t=16, a=32, e=2) for j in range(4)]
    uk = 0
    ctb = ct[:].unsqueeze(1).to_broadcast([128, 2, 128])
    stb = st[:].unsqueeze(1).to_broadcast([128, 2, 128])
    k = 0
    for g in range(4):
        for t8 in range(8):
            u, ub = ur.next()
            upb = upb_[uk % 3]
            uk += 1
            b0, b0b = b0r.next()
            b1, b1b = b1r.next()
            c0 = t8 * 512
            for kc in range(2):
                P.dma(u[:, kc, :, 0:64], T["UTown"][2 * g + kc, :, c0:c0 + 512].rearrange("p (i t) -> p i t", t=64), writes=[ub])
            agv = T["UTag"][g].rearrange("(r k p) t -> p r k t", r=2, k=2)
            P.dma(b0[:], agv[:, 0, :, c0:c0 + 512], writes=[b0b])
            P.dma(b1[:], agv[:, 1, :, c0:c0 + 512], writes=[b1b])
            P.pool(lambda e, b1=b1: e.tensor_scalar(out=b1[:], in0=b1[:], scalar1=sel[:, 1:2], scalar2=None, op0=ALU.mult), reads=[b1b, selb], writes=[b1b])
            P.dve(lambda e, u=u, b0=b0, b1=b1: e.scalar_tensor_tensor(out=u[:, :, :, 64:128], in0=b0[:].rearrange("p k (i t) -> p k i t", t=64), scalar=sel[:, 0:1],
                                                                     in1=b1[:].rearrange("p k (i t) -> p k i t", t=64), op0=ALU.mult, op1=ALU.add),
                  reads=[b0b, b1b, selb], writes=[upb])
            for ti in range(8):
                t2 = t8 * 8 + ti
                ps, psb = ps1.next()
                for kc in range(2):
                    P.pe(lambda e, ps=ps, u=u, kc=kc, ti=ti: e.matmul(ps[:], u[:, kc, ti, :], csw[:, kc, :], start=(kc == 0), stop=(kc == 1)),
                         reads=[ub, upb, tb], writes=[psb])
                k += 1
                if k % 2 == 0:
                    P.act(lambda e, ps=ps, t2=t2: e.activation(out=AB[:, :, t2], in_=ps[:], func=AF.Copy), reads=[psb], writes=[ABb])
                else:
                    P.dve(lambda e, ps=ps, t2=t2: e.tensor_copy(AB[:, :, t2], ps[:]), reads=[psb], writes=[ABb])
        for half in range(2):
            for pb in range(32):
                ps, psb = ps2.next()
                for pi in range(2):
                    pl = 2 * pb + pi
                    ch = half * 128 + 2 * pl
                    P.pe(lambda e, ps=ps, pi=pi, ch=ch: e.matmul(ps[:, pi * 256:(pi + 1) * 256], AB[:, ch:ch + 2, :].rearrange("p a b -> p (a b)"), ra[:], start=True, stop=False),
                         reads=[ABb, tb], writes=[psb])
                    P.pe(lambda e, ps=ps, pi=pi, ch=ch: e.matmul(ps[:, pi * 256:(pi + 1) * 256], AB[:, 256 + ch:256 + ch + 2, :].rearrange("p a b -> p (a b)"), rb[:], start=False, stop=True),
                         reads=[ABb, tb], writes=[psb])
                ef, efb = efr.next()
                P.act(lambda e, ef=ef, ps=ps: e.activation(out=ef[:], in_=ps[:], func=AF.Copy), reads=[psb], writes=[efb])
                efv = ef[:].rearrange("p (a b) -> p a b", a=2)
                Ev, Fv = efv[:, :, 0:128], efv[:, :, 128:256]
                m1, m1b = m1r.next()
                m2, m2b = m2r.next()
                m3, m3b = m3r.next()
                m4, m4b = m4r.next()
                P.pool(lambda e, m1=m1, Ev=Ev: e.tensor_tensor(out=m1[:], in0=Ev, in1=ctb, op=ALU.mult), reads=[efb, tb], writes=[m1b])
                P.pool(lambda e, m2=m2, Fv=Fv: e.tensor_tensor(out=m2[:], in0=Fv, in1=stb, op=ALU.mult), reads=[efb, tb], writes=[m2b])
                P.dve(lambda e, m3=m3, Ev=Ev: e.tensor_tensor(out=m3[:], in0=Ev, in1=stb, op=ALU.mult), reads=[efb, tb], writes=[m3b])
                P.pool(lambda e, m4=m4, Fv=Fv: e.tensor_tensor(out=m4[:], in0=Fv, in1=ctb, op=ALU.mult), reads=[efb, tb], writes=[m4b])
                P.dve(lambda e, m1=m1, m2=m2, pb=pb: e.tensor_tensor(out=G2[:, 2 * pb:2 * pb + 2, 0:128], in0=m1[:], in1=m2[:], op=ALU.subtract),
                      reads=[m1b, m2b], writes=[G2b])
                P.dve(lambda e, m3=m3, m4=m4, pb=pb: e.tensor_tensor(out=G2[:, 2 * pb:2 * pb + 2, 128:256], in0=m3[:], in1=m4[:], op=ALU.add),
                      reads=[m3b, m4b], writes=[G2b])
            for qb in range(16):
                ps, psb = ps3.next()
                for pi in range(4):
                    pl = 4 * qb + pi
                    P.pe(lambda e, ps=ps, pi=pi, pl=pl: e.matmul(ps[:, pi * 128:(pi + 1) * 128], G2[:, pl, 0:128], re[:], start=True, stop=False),
                         reads=[G2b, tb], writes=[psb])
                    P.pe(lambda e, ps=ps, pi=pi, pl=pl: e.matmul(ps[:, pi * 128:(pi + 1) * 128], G2[:, pl, 128:256], rf[:], start=False, stop=True),
                         reads=[G2b, tb], writes=[psb])
                dst = fS[:, :, 8 * qb:8 * qb + 8].rearrange("p k (a c) -> p a c k", a=4, c=2)
                src_fn = lambda ps=ps: ps[:].rearrange("p (a c k) -> p a c k", a=4, c=2)
                if qb % 2 == 0:
                    P.act(lambda e, dst=dst, src_fn=src_fn: e.activation(out=dst, in_=src_fn(), func=AF.Copy), reads=[psb], writes=[fSb])
                else:
                    P.dve(lambda e, dst=dst, src_fn=src_fn: e.tensor_copy(dst, src_fn()), reads=[psb], writes=[fSb])
            c0 = g * 256 + half * 128
            for e_ in range(2):
                P.dma(fov[e_][:, :, c0:c0 + 128], fS[e_ * 64:(e_ + 1) * 64, 0:32, :], reads=[fSb])
                for j in range(4):
                    P.dma(fsv[j][e_][:, :, c0:c0 + 128], fS[e_ * 64 + 16 * j:e_ * 64 + 16 * j + 16, 32:64, :], reads=[fSb])
    P.barrier()
    agb = Buf("ag")
    for j in range(4):
        P.allgather(T["fsend"][j], T["fag"][j], writes=[agb])
    P.end_phase()


def assemble_UTf(ut0, ut1, hf):
    a = np.asarray(ut0)[8 * hf:8 * hf + 8].reshape(8, 128, 64, 64)
    b = np.asarray(ut1)[8 * hf:8 * hf + 8].reshape(8, 128, 64, 64)
    return np.ascontiguousarray(np.concatenate([a, b], axis=3).reshape(8, 128, S))


def build_fused():
    P = Prog()
    T = {}
    I32 = mybir.dt.int32
    T["x"] = P.din("x", [TOWN, D], F32)
    T["cvec"] = P.din("cvec", [128, NCH], F32)
    T["norm_g0"] = P.din("norm_g0", [1, D], F32)
    T["norm_g1"] = P.din("norm_g1", [1, D], F32)
    T["ada_w0"] = P.din("ada_w0", [D, 3 * D], F32)
    T["ada_w1"] = P.din("ada_w1", [D, 3 * D], F32)
    T["ada_b0"] = P.din("ada_b0", [1, 3 * D], F32)
    T["ada_b1"] = P.din("ada_b1", [1, 3 * D], F32)
    T["w_in0"] = P.din("w_in0", [D, 5120], F32)
    T["qg"] = P.din("qg", [128, 1], F32)
    T["kg"] = P.din("kg", [128, 1], F32)
    T["ropeC"] = P.din("ropeC", [128, TOWN], F32)
    T["ropeS"] = P.din("ropeS", [128, TOWN], F32)
    T["w_out0"] = P.din("w_out0", [D, D], F32)
    T["w_in1"] = P.din("w_in1", [D, 4096], F32)
    T["w_out1"] = P.din("w_out1", [D, D], F32)
    T["final_g"] = P.din("final_g", [1, D], F32)
    T["sel"] = P.din("sel", [128, 2], F32)
    T["t_csw"] = P.din("t_csw", [128, 2, 512], BF16)
    T["t_ra"] = P.din("t_ra", [128, 256], BF16)
    T["t_rb"] = P.din("t_rb", [128, 256], BF16)
    T["t_ct"] = P.din("t_ct", [128, 128], F32)
    T["t_st"] = P.din("t_st", [128, 128], F32)
    T["t_re"] = P.din("t_re", [128, 128], BF16)
    T["t_rf"] = P.din("t_rf", [128, 128], BF16)
    T["out"] = P.dout("out", [TOWN, D], F32)
    T["QT"] = P.dint("QT", [16, 128, TOWN], BF16)
    T["SG"] = P.dint("SG", [16, 128, TOWN], BF16)
    T["gate_bc0"] = P.dint("gate_bc0", [128, D], F32)
    T["gate_bc1"] = P.dint("gate_bc1", [128, D], F32)
    T["KTown"] = [P.dint(f"KTown{j}", [256, TOWN], BF16) for j in range(2)]
    T["KTag"] = [P.dint(f"KTag{j}", [512, TOWN], BF16) for j in range(2)]
    T["Vown"] = [P.dint(f"Vown{j}", [2048, 512], BF16) for j in range(2)]
    T["Vag"] = [P.dint(f"Vag{j}", [4096, 512], BF16) for j in range(2)]
    T["OGT"] = P.dint("OGT", [16, 128, TOWN], BF16)
    T["x1"] = P.dint("x1", [TOWN, D], F32)
    T["UTown"] = P.dint("UTown", [8, 128, TOWN], BF16)
    T["UTsend"] = [P.dint(f"UTsend{j}", [256, TOWN], BF16) for j in range(4)]
    T["UTag"] = [P.dint(f"UTag{j}", [512, TOWN], BF16) for j in range(4)]
    T["SG1"] = P.dint("SG1", [TOWN, D], BF16)
    T["f_own"] = P.dint("f_own", [TOWN, 1024], BF16)
    T["fsend"] = [P.dint(f"fsend{j}", [1024, 1024], BF16) for j in range(4)]
    T["fag"] = [P.dint(f"fag{j}", [2048, 1024], BF16) for j in range(4)]
    C = emit_consts(P)
    emit_A(P, C, T)
    emit_B(P, C, T)
    emit_CF(P, C, T, False)
    emit_D(P, C, T)
    emit_E(P, C, T)
    emit_CF(P, C, T, True)
    return P.finish()


def core_tables(hf):
    tb = fourier_tables()
    q = (np.arange(128) + 64 * hf) % 128
    tb["t_ra"] = np.ascontiguousarray(tb["t_ra"][q])
    tb["t_rb"] = np.ascontiguousarray(tb["t_rb"][q])
    n = np.arange(128)
    k2 = ((n % 64) + 32 * hf) % 64
    col = (n // 64) * 64 + k2
    tb["t_re"] = np.ascontiguousarray(tb["t_re"][:, col])
    tb["t_rf"] = np.ascontiguousarray(tb["t_rf"][:, col])
    return tb


def kernel(x, c, norm_g, ada_w, ada_b, attn_w_in, attn_q_gain, attn_k_gain,
           attn_w_out, fourier_w_in, fourier_w_out, final_g):
    x = np.asarray(x)
    c = np.asarray(c)
    norm_g = np.asarray(norm_g)
    ada_w = np.asarray(ada_w)
    ada_b = np.asarray(ada_b)
    w_in = np.asarray(attn_w_in)[0]
    fw_in = np.asarray(fourier_w_in)[0]
    fw_out = np.asarray(fourier_w_out)[0]
    ropeC, ropeS = rope_tables()
    colperm = np.arange(5120)
    for h in range(20):
        colperm[h * 128:(h + 1) * 128] = h * 128 + PERM
    w_in_p = np.ascontiguousarray(w_in[:, colperm])
    consts = host_consts()
    qg = np.ascontiguousarray(np.asarray(attn_q_gain)[0][PERM].reshape(128, 1))
    kg = np.ascontiguousarray(np.asarray(attn_k_gain)[0][PERM].reshape(128, 1))
    per_hf = []
    for hf in range(2):
        ch = np.concatenate([np.arange(1024 * hf, 1024 * hf + 1024), np.arange(1024 * (1 - hf), 1024 * (1 - hf) + 1024)])
        d = dict(core_tables(hf))
        d["w_in1"] = np.ascontiguousarray(np.concatenate([fw_in[:, ch], fw_in[:, 2048 + ch]], axis=1))
        d["w_out1"] = np.ascontiguousarray(fw_out[ch, :])
        d["sel"] = np.ascontiguousarray(np.broadcast_to(np.array([float(hf), float(1 - hf)], np.float32), (128, 2)))
        d["ropeC"] = np.ascontiguousarray(ropeC[:, hf * TOWN:(hf + 1) * TOWN])
        d["ropeS"] = np.ascontiguousarray(ropeS[:, hf * TOWN:(hf + 1) * TOWN])
        per_hf.append(d)
    maps = []
    for core in range(NCORES):
        b, hf = core % 4, core // 4
        m = dict(consts)
        m.update(per_hf[hf])
        m["x"] = np.ascontiguousarray(x[b, hf * TOWN:(hf + 1) * TOWN])
        m["cvec"] = np.ascontiguousarray(c[b].reshape(NCH, 128).T)
        m["norm_g0"] = np.ascontiguousarray(norm_g[0:1])
        m["norm_g1"] = np.ascontiguousarray(norm_g[1:2])
        m["ada_w0"] = ada_w[0]
        m["ada_w1"] = ada_w[1]
        m["ada_b0"] = np.ascontiguousarray(ada_b[0:1])
        m["ada_b1"] = np.ascontiguousarray(ada_b[1:2])
        m["w_in0"] = w_in_p
        m["qg"] = qg
        m["kg"] = kg
        m["w_out0"] = np.asarray(attn_w_out)[0]
        m["final_g"] = np.ascontiguousarray(np.asarray(final_g).reshape(1, D))
        maps.append(m)
    res = run_bass_kernel_spmd(build_fused(), maps, core_ids=list(range(NCORES))).results
    out = np.empty((NB, S, D), np.float32)
    for core in range(NCORES):
        b, hf = core % 4, core // 4
        out[b, hf * TOWN:(hf + 1) * TOWN] = res[core]["out"]
    return out
```

```python
import math
from contextlib import ExitStack
import numpy as np
import ml_dtypes
import concourse.bass as bass
import concourse.mybir as mybir
from concourse.bass_utils import run_bass_kernel_spmd

F32 = mybir.dt.float32
BF16 = mybir.dt.bfloat16
AF = mybir.ActivationFunctionType
ALU = mybir.AluOpType
NPBF = ml_dtypes.bfloat16

D = 2048
S = 8192
NB = 4
NCORES = 8
TOWN = 4096
HD = 128
EPS = 1e-6
NCH = D // 128
GROUPS = [[0, 4], [1, 5], [2, 6], [3, 7]]


class Buf:
    __slots__ = ("name", "w", "r", "lsem", "ssem")

    def __init__(self, name=""):
        self.name = name
        self.w = None
        self.r = []
        self.lsem = None
        self.ssem = None


class Prog:
    ENG = ("pe", "act", "dve", "pool", "sp")

    def __init__(self):
        self.nc = bass.Bass("TRN2", target_bir_lowering=False)
        self.es = ExitStack()
        self.ops = {e: [] for e in self.ENG}
        self.N = {e: 0 for e in self.ENG}
        self.S = {e: self.es.enter_context(self.nc.semaphore("cnt_" + e)) for e in self.ENG}
        self.waited = {e: {} for e in self.ENG}
        self.semval = {}
        self.nsem = 5
        self.uid = 0
        self.dram = {}
        self.scope = self.es
        self.phase_sems = []
        self.free_sems = []
        self.ccsems = {}

    def name(self, base):
        self.uid += 1
        return f"{base}_{self.uid}"

    def sb(self, name, shape, dt, es=None):
        return (es or self.scope).enter_context(self.nc.sbuf_tensor(self.name(name), list(shape), dt))

    def ps(self, name, shape, dt, es=None):
        return (es or self.scope).enter_context(self.nc.psum_tensor(self.name(name), list(shape), dt))

    def newsem(self, name):
        if self.free_sems:
            s = self.free_sems.pop()
        else:
            self.nsem += 1
            s = self.es.enter_context(self.nc.semaphore(self.name(name)))
            self.semval[s] = 0
        self.phase_sems.append(s)
        return s

    def begin_phase(self):
        self.scope = ExitStack()
        self.phase_sems = []

    def end_phase(self):
        self.barrier()
        self.scope.close()
        self.scope = self.es
        self.free_sems.extend(self.phase_sems)
        self.phase_sems = []

    def dint(self, name, shape, dt):
        return self.nc.dram_tensor(name, list(shape), dt).ap()

    def allgather(self, in_ap, out_ap, reads=(), writes=()):
        q = "pool"
        self._deps(q, reads, writes)
        self.nsem += 1
        sem = self.es.enter_context(self.nc.semaphore(self.name("cc")))
        self.ccsems[sem] = 1
        self.ops[q].append(lambda e: e.collective_compute("AllGather", ALU.bypass, replica_groups=GROUPS,
                                                          ins=[in_ap.opt()], outs=[out_ap.opt()]).then_inc(sem, 1))
        tok = ("dma", sem, 1)
        for b in reads:
            b.r.append(tok)
        for b in writes:
            b.w = tok
            b.r = []
        return tok

    def din(self, name, shape, dt):
        t = self.nc.dram_tensor(name, list(shape), dt, kind="ExternalInput").ap()
        self.dram[name] = t
        return t

    def dout(self, name, shape, dt):
        t = self.nc.dram_tensor(name, list(shape), dt, kind="ExternalOutput").ap()
        self.dram[name] = t
        return t

    def _need(self, eng, tok):
        if tok[0] == "eng":
            _, f, n = tok
            if f == eng and eng == "pe":
                return
            sem, key, val = self.S[f], ("e", f), n
        else:
            _, sem, val = tok
            key = ("d", sem)
        if self.waited[eng].get(key, 0) >= val:
            return
        self.waited[eng][key] = val
        self.ops[eng].append(lambda e, sem=sem, val=val: e.wait_ge(sem, val))

    def _deps(self, eng, reads, writes):
        for b in reads:
            if b.w is not None:
                self._need(eng, b.w)
        for b in writes:
            if b.w is not None:
                self._need(eng, b.w)
            for t in b.r:
                self._need(eng, t)

    def op(self, eng, fn, reads=(), writes=()):
        self._deps(eng, reads, writes)
        self.N[eng] += 1
        tok = ("eng", eng, self.N[eng])
        sem = self.S[eng]
        self.ops[eng].append(lambda e, fn=fn, sem=sem: fn(e).then_inc(sem, 1))
        for b in reads:
            b.r.append(tok)
        for b in writes:
            b.w = tok
            b.r = []
        return tok

    def pe(self, fn, reads=(), writes=()):
        return self.op("pe", fn, reads, writes)

    def act(self, fn, reads=(), writes=()):
        return self.op("act", fn, reads, writes)

    def dve(self, fn, reads=(), writes=()):
        return self.op("dve", fn, reads, writes)

    def pool(self, fn, reads=(), writes=()):
        return self.op("pool", fn, reads, writes)

    def dma(self, out_ap, in_ap, reads=(), writes=(), q="sp", sem=None):
        self._deps(q, reads, writes)
        if sem is None:
            if writes and not reads:
                b = writes[0]
                if b.lsem is None:
                    b.lsem = self.newsem("l")
                sem = b.lsem
            else:
                b = reads[0]
                if b.ssem is None:
                    b.ssem = self.newsem("s")
                sem = b.ssem
        self.semval[sem] += 16
        val = self.semval[sem]
        self.ops[q].append(lambda e, o=out_ap, i=in_ap, sem=sem: e.dma_start(out=o, in_=i).then_inc(sem, 16))
        tok = ("dma", sem, val)
        for b in reads:
            b.r.append(tok)
        for b in writes:
            b.w = tok
            b.r = []
        return tok

    def barrier(self):
        for e in self.ENG:
            for f in self.ENG:
                if f != e and self.N[f] > 0:
                    self._need(e, ("eng", f, self.N[f]))
            for s, v in self.semval.items():
                if v > 0:
                    self._need(e, ("dma", s, v))
            for s, v in self.ccsems.items():
                self._need(e, ("dma", s, v))

    def finish(self):
        for s, v in self.semval.items():
            if v > 0:
                self._need("sp", ("dma", s, v))
        ops = self.ops
        with self.nc.Block() as block:
            @block.tensor
            def _(e):
                for o in ops["pe"]:
                    o(e)

            @block.scalar
            def _(e):
                for o in ops["act"]:
                    o(e)

            @block.vector
            def _(e):
                for o in ops["dve"]:
                    o(e)

            @block.gpsimd
            def _(e):
                for o in ops["pool"]:
                    o(e)

            @block.sync
            def _(e):
                for o in ops["sp"]:
                    o(e)
        self.es.close()
        return self.nc


class Ring:
    def __init__(self, P, name, shape, dt, n, es=None, psum=False):
        self.n = n
        self.i = 0
        alloc = P.ps if psum else P.sb
        self.t = [alloc(name, [128] + list(shape), dt, es) for _ in range(n)]
        self.b = [Buf(f"{name}{k}") for k in range(n)]

    def next(self):
        k = self.i % self.n
        self.i += 1
        return self.t[k], self.b[k]


def emit_consts(P):
    c = {}
    ident_d = P.din("c_ident", [128, 128], BF16)
    ones_d = P.din("c_ones", [128, 128], BF16)
    onesf_d = P.din("c_onesf", [128, 128], F32)
    c["ident"] = P.sb("ident", [128, 128], BF16)
    c["ones"] = P.sb("ones", [128, 128], BF16)
    c["onesf"] = P.sb("onesf", [128, 128], F32)
    c["eps"] = P.sb("eps", [128, 1], F32)
    cb = Buf("consts")
    c["buf"] = cb
    P.dma(c["ident"][:], ident_d, writes=[cb])
    P.dma(c["ones"][:], ones_d, writes=[cb])
    P.dma(c["onesf"][:], onesf_d, writes=[cb])
    eb = Buf("eps")
    P.pool(lambda e: e.memset(c["eps"][:], EPS), writes=[eb])
    c["epsb"] = eb
    return c


def emit_mod(P, C, cvec_d, adaw_d, adab_d, psring, dst_fn):
    with ExitStack() as es:
        cv = P.sb("cv", [128, NCH], F32, es)
        crep = P.sb("crep", [128, NCH, 128], F32, es)
        abrow = P.sb("abrow", [1, 3 * D], F32, es)
        cvb, crb, abb = Buf(), Buf(), Buf()
        P.dma(cv[:], cvec_d, writes=[cvb])
        P.dma(abrow[:], adab_d, writes=[abb])
        P.act(lambda e: e.activation(out=cv[:], in_=cv[:], func=AF.Silu), reads=[cvb], writes=[cvb])
        P.dve(lambda e: e.tensor_copy(crep[:], cv[:].unsqueeze(2).to_broadcast([128, NCH, 128])), reads=[cvb], writes=[crb])
        awr = Ring(P, "aw", [NCH, 512], F32, 2, es)
        awv = adaw_d.rearrange("(j p) n -> p j n", p=128)
        for nt in range(12):
            aw, awb = awr.next()
            P.dma(aw[:], awv[:, :, nt * 512:(nt + 1) * 512], writes=[awb])
            pst, psb = psring.next()
            for j in range(NCH):
                P.pe(lambda e, j=j, aw=aw, pst=pst: e.matmul(pst[:], crep[:, j, :], aw[:, j, :], start=(j == 0), stop=False),
                     reads=[crb, awb], writes=[psb])
            P.pe(lambda e, nt=nt, pst=pst: e.matmul(pst[:], C["onesf"][0:1, :], abrow[0:1, nt * 512:(nt + 1) * 512], start=False, stop=True),
                 reads=[C["buf"], abb], writes=[psb])
            dst, dstb = dst_fn(nt)
            P.act(lambda e, dst=dst, pst=pst: e.activation(out=dst, in_=pst[:], func=AF.Copy), reads=[psb], writes=[dstb])
        P.barrier()


class HBuilder:
    def __init__(self, P, C, shift_bc, gs_bc, modb, ptring):
        self.P, self.C = P, C
        self.shift_bc, self.gs_bc, self.modb = shift_bc, gs_bc, modb
        self.xr = Ring(P, "xt", [D], F32, 2)
        self.hr = Ring(P, "hb", [D], BF16, 2)
        self.ssr = Ring(P, "ss", [1], F32, 2)
        self.sdr = Ring(P, "sd", [1], F32, 2)
        self.ptr = ptring
        self.k = 0

    def tile(self, srcs, hT, hTb, col0):
        P, C = self.P, self.C
        xt, xb = self.xr.next()
        hb, hbb = self.hr.next()
        ss, ssb = self.ssr.next()
        sd, sdb = self.sdr.next()
        for (p0, p1, ap) in srcs:
            P.dma(xt[p0:p1, :], ap, writes=[xb])
        P.pool(lambda e: e.memset(ss[:], 0.0), writes=[ssb])
        P.act(lambda e: e.activation(out=hb[:], in_=xt[:], func=AF.Square, accum_out=ss[:]), reads=[xb], writes=[hbb, ssb])
        P.act(lambda e: e.activation(out=sd[:], in_=ss[:], func=AF.Sqrt, scale=1.0 / D, bias=C["eps"][:, 0:1]),
              reads=[ssb, C["epsb"]], writes=[sdb])
        P.dve(lambda e: e.reciprocal(out=sd[:], in_=sd[:]), reads=[sdb], writes=[sdb])
        P.dve(lambda e: e.scalar_tensor_tensor(out=xt[:], in0=xt[:], scalar=sd[:, 0:1], in1=self.gs_bc, op0=ALU.mult, op1=ALU.mult),
              reads=[xb, sdb, self.modb], writes=[xb])
        P.pool(lambda e: e.tensor_tensor(out=hb[:], in0=xt[:], in1=self.shift_bc, op=ALU.add), reads=[xb, self.modb], writes=[hbb])
        for half in range(2):
            pt, ptb = self.ptr.next()
            for c in range(8):
                ch = half * 8 + c
                P.pe(lambda e, pt=pt, c=c, ch=ch: e.transpose(pt[:, c, :], hb[:, ch * 128:(ch + 1) * 128], C["ident"][:]),
                     reads=[hbb, C["buf"]], writes=[ptb])
            dst = hT[:, half * 8:half * 8 + 8, col0:col0 + 128]
            if (self.k + half) % 2 == 0:
                P.act(lambda e, dst=dst, pt=pt: e.activation(out=dst, in_=pt[:], func=AF.Copy), reads=[ptb], writes=[hTb])
            else:
                P.dve(lambda e, dst=dst, pt=pt: e.tensor_copy(dst, pt[:]), reads=[ptb], writes=[hTb])
        self.k += 1


class WStream:
    def __init__(self, P, w_d):
        self.P = P
        self.wv = w_d.rearrange("(j p) n -> p j n", p=128)
        self.wfr = Ring(P, "wf", [4, 512], F32, 4)
        self.wbr = Ring(P, "wb", [NCH, 512], BF16, 2)
        self.pending = {}
        self.k = 0

    def dma(self, g):
        wb, wbb = self.wbr.next()
        st = []
        for jq in range(4):
            wf, wfb = self.wfr.next()
            self.P.dma(wf[:], self.wv[:, 4 * jq:4 * jq + 4, g * 512:(g + 1) * 512], writes=[wfb])
            st.append((wf, wfb))
        self.pending[g] = (wb, wbb, st)

    def casts(self, g):
        P = self.P
        wb, wbb, st = self.pending.pop(g)
        self.k += 1
        for jq, (wf, wfb) in enumerate(st):
            dst = wb[:, 4 * jq:4 * jq + 4, :]
            if self.k % 2 == 0:
                P.dve(lambda e, dst=dst, wf=wf: e.tensor_copy(dst, wf[:]), reads=[wfb], writes=[wbb])
            else:
                P.act(lambda e, dst=dst, wf=wf: e.activation(out=dst, in_=wf[:], func=AF.Copy), reads=[wfb], writes=[wbb])
        return wb, wbb


def emit_A(P, C, T):
    P.begin_phase()
    x_d, cvec_d, g_d, adaw_d, adab_d, win_d = T["x"], T["cvec"], T["norm_g0"], T["ada_w0"], T["ada_b0"], T["w_in0"]
    qg_d, kg_d, rc_d, rs_d = T["qg"], T["kg"], T["ropeC"], T["ropeS"]
    QT_d, SG_d, GB_d = T["QT"], T["SG"], T["gate_bc0"]

    psA = Ring(P, "psA", [512], F32, 3, psum=True)
    psB = Ring(P, "psB", [512], F32, 2, psum=True)
    psT = Ring(P, "psT", [8, 128], BF16, 2, psum=True)

    modp = P.sb("modp", [128, 2 * D], F32)
    modb = Buf("modp")
    gains = P.sb("gains", [128, 2], F32)
    gainb = Buf("gains")
    P.dma(gains[:, 0:1], qg_d, writes=[gainb])
    P.dma(gains[:, 1:2], kg_d, writes=[gainb])

    with ExitStack() as es:
        gtmp = P.sb("gtmp", [128, D], F32, es)
        gbc = P.sb("gbc", [128, D], F32, es)
        gtb, gbb = Buf(), Buf()
        P.dma(gbc[:], g_d.partition_broadcast(128).rearrange("p o n -> p (o n)"), writes=[gbb])

        def dst_fn(nt):
            if nt < 8:
                return modp[:, nt * 512:(nt + 1) * 512], modb
            return gtmp[:, (nt - 8) * 512:(nt - 7) * 512], gtb
        emit_mod(P, C, cvec_d, adaw_d, adab_d, psA, dst_fn)
        P.dma(GB_d, gtmp[:], reads=[gtb])
        P.dve(lambda e: e.scalar_tensor_tensor(out=modp[:, D:2 * D], in0=modp[:, D:2 * D], scalar=1.0, in1=gbc[:], op0=ALU.add, op1=ALU.mult),
              reads=[modb, gbb], writes=[modb])
        P.barrier()

    hb_ = HBuilder(P, C, modp[:, 0:D], modp[:, D:2 * D], modb, psT)
    TH = 2048
    hT = P.sb("hT", [128, NCH, TH], BF16)
    hTb = Buf("hT")
    rC = P.sb("rC", [128, TH], F32)
    rS = P.sb("rS", [128, TH], F32)
    ropeb = Buf("rope")
    W = WStream(P, win_d)
    sqr = Ring(P, "sq", [512], BF16, 2)
    sdr = Ring(P, "sdq", [512], F32, 2)
    qnr = Ring(P, "qn", [512], F32, 2)
    t1r = Ring(P, "t1", [512], F32, 2)
    t2r = Ring(P, "t2", [512], F32, 2)
    qor = Ring(P, "qo", [512], BF16, 3)
    vor = qor
    NG = 10

    for th in range(2):
        tok0 = th * TH
        Vv = T["Vown"][th]
        W.dma(0)
        for tt in range(TH // 128):
            r0 = tok0 + tt * 128
            hb_.tile([(0, 128, x_d[r0:r0 + 128, :])], hT, hTb, tt * 128)
        P.dma(rC[:], rc_d[:, tok0:tok0 + TH], writes=[ropeb])
        P.dma(rS[:], rs_d[:, tok0:tok0 + TH], writes=[ropeb])
        pend_a, pend_b = [], []

        def stage1a(st):
            pst, psb, sq, sqb, t5, cc = st["pst"], st["psb"], st["sq"], st["sqb"], st["t5"], st["cc"]
            gcol = gains[:, 0:1] if cc < 16 else gains[:, 1:2]
            pB, pBb = psB.next()
            P.pe(lambda e: e.matmul(pB[:], C["ones"][:], sq[:], start=True, stop=True), reads=[C["buf"], sqb], writes=[pBb])
            sd, sdb = sdr.next()
            qn, qnb = qnr.next()
            t1, t1b = t1r.next()
            t2, t2b = t2r.next()
            P.act(lambda e: e.activation(out=sd[:], in_=pB[:], func=AF.Ln, scale=1.0 / HD, bias=C["eps"][:, 0:1]),
                  reads=[pBb, C["epsb"]], writes=[sdb])
            P.act(lambda e: e.activation(out=sd[:], in_=sd[:], func=AF.Exp, scale=-0.5), reads=[sdb], writes=[sdb])
            P.dve(lambda e: e.scalar_tensor_tensor(out=qn[:], in0=pst[:], scalar=gcol, in1=sd[:], op0=ALU.mult, op1=ALU.mult),
                  reads=[psb, sdb, gainb], writes=[qnb])
            cs = slice(t5 * 512, (t5 + 1) * 512)
            P.dve(lambda e: e.tensor_tensor(out=t1[:], in0=qn[:], in1=rC[:, cs], op=ALU.mult), reads=[qnb, ropeb], writes=[t1b])
            P.pool(lambda e: e.tensor_tensor(out=t2[0:64, :], in0=qn[64:128, :], in1=rS[64:128, cs], op=ALU.mult), reads=[qnb, ropeb], writes=[t2b])
            P.pool(lambda e: e.tensor_tensor(out=t2[64:128, :], in0=qn[0:64, :], in1=rS[0:64, cs], op=ALU.mult), reads=[qnb, ropeb], writes=[t2b])
            st.update(t1=t1, t1b=t1b, t2=t2, t2b=t2b)

        def stage1b(st):
            t1, t1b, t2, t2b, t5, cc = st["t1"], st["t1b"], st["t2"], st["t2b"], st["t5"], st["cc"]
            qo, qob = qor.next()
            P.dve(lambda e: e.tensor_tensor(out=qo[:], in0=t1[:], in1=t2[:], op=ALU.add), reads=[t1b, t2b], writes=[qob])
            if cc < 16:
                dstd = QT_d[cc, :, tok0 + t5 * 512: tok0 + (t5 + 1) * 512]
            else:
                dstd = T["KTown"][(cc - 16) // 2][((cc - 16) % 2) * 128:((cc - 16) % 2) * 128 + 128, tok0 + t5 * 512: tok0 + (t5 + 1) * 512]
            P.dma(dstd, qo[:], reads=[qob])

        def flush():
            while pend_a:
                st = pend_a.pop(0)
                stage1a(st)
                pend_b.append(st)
            while pend_b:
                stage1b(pend_b.pop(0))

        wb, wbb = W.casts(0)
        for g in range(NG):
            if g + 1 < NG:
                W.dma(g + 1)
            nxt = None
            if g == 5:
                flush()
                for tt in range(TH // 128):
                    pst, psb = psA.next()
                    for j in range(NCH):
                        P.pe(lambda e, pst=pst, j=j, tt=tt, wb=wb: e.matmul(pst[:], hT[:, j, tt * 128:(tt + 1) * 128], wb[:, j, :], start=(j == 0), stop=(j == NCH - 1)),
                             reads=[hTb, wbb], writes=[psb])
                    vo, vob = vor.next()
                    if tt % 2 == 0:
                        P.dve(lambda e, vo=vo, pst=pst: e.tensor_copy(vo[:], pst[:]), reads=[psb], writes=[vob])
                    else:
                        P.act(lambda e, vo=vo, pst=pst: e.activation(out=vo[:], in_=pst[:], func=AF.Copy), reads=[psb], writes=[vob])
                    P.dma(Vv[tt * 128:(tt + 1) * 128, :], vo[:], reads=[vob])
                    if tt == 8:
                        nxt = W.casts(g + 1)
                wb, wbb = nxt
                continue
            for ci in range(4):
                cc = 4 * g + ci if g < 5 else 4 * g + ci
                if ci == 2 and g + 1 < NG:
                    nxt = W.casts(g + 1)
                for t5 in range(TH // 512):
                    pst, psb = psA.next()
                    for j in range(NCH):
                        P.pe(lambda e, pst=pst, j=j, t5=t5, wb=wb, ci=ci: e.matmul(pst[:], wb[:, j, ci * 128:(ci + 1) * 128], hT[:, j, t5 * 512:(t5 + 1) * 512], start=(j == 0), stop=(j == NCH - 1)),
                             reads=[hTb, wbb], writes=[psb])
                    if cc < 20:
                        sq, sqb = sqr.next()
                        P.act(lambda e, sq=sq, pst=pst: e.activation(out=sq[:], in_=pst[:], func=AF.Square), reads=[psb], writes=[sqb])
                        if pend_b:
                            stage1b(pend_b.pop(0))
                        if pend_a:
                            st = pend_a.pop(0)
                            stage1a(st)
                            pend_b.append(st)
                        pend_a.append(dict(pst=pst, psb=psb, sq=sq, sqb=sqb, t5=t5, cc=cc))
                    else:
                        qo, qob = qor.next()
                        P.act(lambda e, qo=qo, pst=pst: e.activation(out=qo[:], in_=pst[:], func=AF.Silu), reads=[psb], writes=[qob])
                        P.dma(SG_d[cc - 24, :, tok0 + t5 * 512: tok0 + (t5 + 1) * 512], qo[:], reads=[qob])
            if g == 4:
                flush()
            if nxt is not None:
                wb, wbb = nxt
    P.barrier()
    agb = Buf("ag")
    for j in range(2):
        P.allgather(T["KTown"][j], T["KTag"][j], writes=[agb])
        P.allgather(T["Vown"][j], T["Vag"][j], writes=[agb])
    P.end_phase()


def host_consts():
    return {
        "c_ident": np.eye(128, dtype=np.float32).astype(NPBF),
        "c_ones": np.ones((128, 128), np.float32).astype(NPBF),
        "c_onesf": np.ones((128, 128), np.float32),
    }


def rope_tables():
    t = np.arange(S)
    rows = (t // 64).astype(np.float32)
    cols = (t % 64).astype(np.float32)
    inv = (np.float32(10000.0) ** (-np.arange(0, 64, 2, dtype=np.float32) / np.float32(64))).astype(np.float32)
    ang = np.concatenate([rows[:, None] * inv[None, :], cols[:, None] * inv[None, :]], axis=-1)
    c = np.cos(ang).T.astype(np.float32)
    s = np.sin(ang).T.astype(np.float32)
    return np.concatenate([c, c], 0), np.concatenate([s, -s], 0)


PERM = np.concatenate([np.arange(0, 128, 2), np.arange(1, 128, 2)])


def emit_B(P, C, T):
    P.begin_phase()
    QT_d, SG_d, OG_d = T["QT"], T["SG"], T["OGT"]
    psS = Ring(P, "psS", [512], F32, 4, psum=True)
    psO = Ring(P, "psO", [512], F32, 2, psum=True)
    psL = Ring(P, "psL", [512], F32, 2, psum=True)
    kr = Ring(P, "kt", [S], BF16, 2)
    vr = Ring(P, "vt", [64, 128], BF16, 2)
    qr = Ring(P, "qt", [TOWN], BF16, 2)
    sgr = Ring(P, "sg", [TOWN], BF16, 2)
    pr = Ring(P, "p", [512], BF16, 6)
    rlr = Ring(P, "rl", [512], F32, 2)
    lnr = Ring(P, "lnl", [512], F32, 2)
    asr = Ring(P, "asum", [512], BF16, 2)
    aPr = Ring(P, "accP", [512], F32, 2)
    t01r = Ring(P, "t01", [512], BF16, 2)
    t23r = Ring(P, "t23", [512], BF16, 2)
    q4r = Ring(P, "q4", [512], BF16, 2)
    o1r = Ring(P, "o1", [512], F32, 2)
    ogr = Ring(P, "og", [512], BF16, 2)
    scale = 1.0 / math.sqrt(HD)
    NKB = S // 128
    pend = []

    def front(st):
        pS, pSb = psS.next()
        p, pb = pr.next()
        kt, ktb, qt_, qb, qi, kb = st["kt"], st["ktb"], st["qt"], st["qb"], st["qi"], st["kb"]
        P.pe(lambda e: e.matmul(pS[:], kt[:, kb * 128:(kb + 1) * 128], qt_[:, qi * 512:(qi + 1) * 512], start=True, stop=True),
             reads=[ktb, qb], writes=[pSb])
        P.act(lambda e: e.activation(out=p[:], in_=pS[:], func=AF.Exp, scale=scale), reads=[pSb], writes=[pb])
        st["p"], st["pb"] = p, pb

    def back(st):
        p, pb, vt, vtb, kb = st["p"], st["pb"], st["vt"], st["vtb"], st["kb"]
        pO, pOb = st["pO"], st["pOb"]
        P.pe(lambda e: e.matmul(pO[:], vt[:, kb, :], p[:], start=(kb == 0), stop=(kb == NKB - 1)), reads=[vtb, pb], writes=[pOb])
        acc, accb = st["accs"]
        r = kb % 4
        grp = st["grp"]
        if r == 0:
            grp.clear()
        grp.append((p, pb))
        if r == 1:
            t01, t01b = t01r.next()
            (pa, pab), (pc, pcb) = grp[0], grp[1]
            P.dve(lambda e: e.tensor_tensor(out=t01[:], in0=pa[:], in1=pc[:], op=ALU.add), reads=[pab, pcb], writes=[t01b])
            st["grp_t01"][:] = [(t01, t01b)]
        if r == 3:
            t23, t23b = t23r.next()
            q4, q4b = q4r.next()
            (pa, pab), (pc, pcb) = grp[2], grp[3]
            (t01, t01b) = st["grp_t01"][0]
            P.dve(lambda e: e.tensor_tensor(out=t23[:], in0=pa[:], in1=pc[:], op=ALU.add), reads=[pab, pcb], writes=[t23b])
            P.dve(lambda e: e.tensor_tensor(out=q4[:], in0=t01[:], in1=t23[:], op=ALU.add), reads=[t01b, t23b], writes=[q4b])
            if kb == 3:
                P.pool(lambda e: e.tensor_copy(acc[:], q4[:]), reads=[q4b], writes=[accb])
            else:
                P.pool(lambda e: e.tensor_tensor(out=acc[:], in0=acc[:], in1=q4[:], op=ALU.add), reads=[q4b, accb], writes=[accb])
        if kb == NKB - 1:
            h, qi, sg, sgb = st["h"], st["qi"], st["sg"], st["sgb"]
            pL, pLb = psL.next()
            asum, asumb = asr.next()
            ln, lnb = lnr.next()
            rl, rlb = rlr.next()
            o1, o1b = o1r.next()
            og, ogb = ogr.next()
            P.pool(lambda e: e.tensor_copy(asum[:], acc[:]), reads=[accb], writes=[asumb])
            P.pe(lambda e: e.matmul(pL[:], C["ones"][:], asum[:], start=True, stop=True), reads=[C["buf"], asumb], writes=[pLb])
            P.act(lambda e: e.activation(out=ln[:], in_=pL[:], func=AF.Ln), reads=[pLb], writes=[lnb])
            P.act(lambda e: e.activation(out=rl[:], in_=ln[:], func=AF.Exp, scale=-1.0), reads=[lnb], writes=[rlb])
            P.dve(lambda e: e.tensor_tensor(out=o1[:], in0=pO[:], in1=rl[:], op=ALU.mult), reads=[pOb, rlb], writes=[o1b])
            P.pool(lambda e: e.tensor_tensor(out=og[:], in0=o1[:], in1=sg[:, qi * 512:(qi + 1) * 512], op=ALU.mult), reads=[o1b, sgb], writes=[ogb])
            P.dma(OG_d[h, :, qi * 512:(qi + 1) * 512], og[:], reads=[ogb])

    for g in range(4):
        kt, ktb = kr.next()
        vt, vtb = vr.next()
        P.dma(kt[:].rearrange("p (r t) -> p r t", r=2),
              T["KTag"][g // 2].rearrange("(r k p) t -> p k r t", r=2, k=2)[:, g % 2], writes=[ktb])
        for j in range(2):
            for r in range(2):
                P.dma(vt[:, r * 32 + j * 16:r * 32 + j * 16 + 16, :],
                      T["Vag"][j][r * 2048:(r + 1) * 2048, g * 128:(g + 1) * 128].rearrange("(kk p) d -> p kk d", p=128), writes=[vtb])
        for hh in range(4):
            h = 4 * g + hh
            qt_, qb = qr.next()
            sg, sgb = sgr.next()
            P.dma(qt_[:], QT_d[h], writes=[qb])
            P.dma(sg[:], SG_d[h], writes=[sgb])
            for qi in range(TOWN // 512):
                pO, pOb = psO.next()
                accs = aPr.next()
                grp, grp_t01 = [], []
                for kb in range(NKB):
                    st = dict(kt=kt, ktb=ktb, vt=vt, vtb=vtb, qt=qt_, qb=qb, sg=sg, sgb=sgb, h=h, qi=qi, kb=kb,
                              pO=pO, pOb=pOb, accs=accs, grp=grp, grp_t01=grp_t01)
                    front(st)
                    pend.append(st)
                    if len(pend) > 2:
                        back(pend.pop(0))
    while pend:
        back(pend.pop(0))
    P.end_phase()


def emit_CF(P, C, T, final):
    P.begin_phase()
    if final:
        w_d, x_d, gb_d = T["w_out1"], T["x1"], T["gate_bc1"]
        sg_d, fg_d, out_d = T["SG1"], T["final_g"], T["out"]
        xv = x_d.rearrange("(a b) m -> b a m", b=64)
        ov = out_d.rearrange("(a b) m -> b a m", b=64)
        sel = P.sb("sel", [128, 2], F32)
        selb = Buf("sel")
        P.dma(sel[:], T["sel"], writes=[selb])
    else:
        w_d, x_d, gb_d = T["w_out0"], T["x"], T["gate_bc0"]
        OG_d, out_d = T["OGT"], T["x1"]
    psY = Ring(P, "psY", [512], F32, 6 if final else 8, psum=True)
    wbf = P.sb("wbf", [128, NCH, D], BF16)
    wbb = Buf("wbf")
    gbc = P.sb("gbc", [128, D], F32)
    gbb = Buf("gbc")
    P.dma(gbc[:], gb_d, writes=[gbb])
    wfr = Ring(P, "wf", [4, 512], F32, 3)
    wv = w_d.rearrange("(j p) n -> p j n", p=128)
    wbn = [Buf(f"wbf{n}") for n in range(4)]
    for n in range(4):
        gsl = gbc[:, n * 512:(n + 1) * 512].unsqueeze(1).to_broadcast([128, 4, 512])
        for jq in range(4):
            wf, wfb = wfr.next()
            P.dma(wf[:], wv[:, 4 * jq:4 * jq + 4, n * 512:(n + 1) * 512], writes=[wfb])
            dst = wbf[:, 4 * jq:4 * jq + 4, n * 512:(n + 1) * 512]
            P.dve(lambda e, wf=wf, dst=dst, gsl=gsl: e.tensor_tensor(out=dst, in0=wf[:], in1=gsl, op=ALU.mult), reads=[wfb, gbb], writes=[wbn[n]])
    xr = Ring(P, "xt", [D], F32, 3)
    if final:
        psT = Ring(P, "psT", [8, 128], BF16, 2, psum=True)
        fgb_t = P.sb("fgbc", [128, D], F32)
        fgb = Buf("fg")
        P.dma(fgb_t[:], fg_d.partition_broadcast(128).rearrange("p o n -> p (o n)"), writes=[fgb])
        fr = Ring(P, "ft", [D], BF16, 2)
        b0r = Ring(P, "b0", [1024], BF16, 2)
        b1r = Ring(P, "b1", [1024], BF16, 2)
        sgr = Ring(P, "sgt", [D], BF16, 2)
        obr = Ring(P, "ogb", [D], BF16, 2)
        otr = Ring(P, "ogT", [NCH, 128], BF16, 2)
        ssr = Ring(P, "ss", [1], F32, 2)
        sdr = Ring(P, "sd", [1], F32, 2)
        jr = Ring(P, "junk", [D], BF16, 1)
    else:
        ogr = Ring(P, "og", [NCH, 512], BF16, 2)
        OGv = OG_d.rearrange("c p t -> p c t")
    cur_og = [None]

    def front(tt):
        r0 = tt * 128
        st = dict(tt=tt)
        if final:
            ft, ftb = fr.next()
            sg, sgb = sgr.next()
            ob, obb = obr.next()
            oT, oTb = otr.next()
            b0, b0b = b0r.next()
            b1, b1b = b1r.next()
            fob = Buf("fo")
            P.dma(ft[:, 0:1024], T["f_own"][r0:r0 + 128, :], writes=[fob])
            fj, fr0 = tt // 8, (tt % 8) * 128
            P.dma(b0[:], T["fag"][fj][fr0:fr0 + 128, :], writes=[b0b])
            P.dma(b1[:], T["fag"][fj][1024 + fr0:1024 + fr0 + 128, :], writes=[b1b])
            P.dma(sg[:], sg_d[r0:r0 + 128, :], writes=[sgb])
            P.dve(lambda e: e.tensor_scalar(out=b1[:], in0=b1[:], scalar1=sel[:, 1:2], scalar2=None, op0=ALU.mult), reads=[b1b, selb], writes=[b1b])
            P.dve(lambda e: e.scalar_tensor_tensor(out=ft[:, 1024:2048], in0=b0[:], scalar=sel[:, 0:1], in1=b1[:], op0=ALU.mult, op1=ALU.add),
                  reads=[b0b, b1b, selb, fob], writes=[ftb])
            P.dve(lambda e: e.tensor_tensor(out=ob[:], in0=ft[:], in1=sg[:], op=ALU.mult), reads=[ftb, fob, sgb], writes=[obb])
            for half in range(2):
                pt, ptb = psT.next()
                for c in range(8):
                    ch = half * 8 + c
                    P.pe(lambda e, pt=pt, c=c, ch=ch: e.transpose(pt[:, c, :], ob[:, ch * 128:(ch + 1) * 128], C["ident"][:]),
                         reads=[obb, C["buf"]], writes=[ptb])
                P.act(lambda e, pt=pt, half=half: e.activation(out=oT[:, half * 8:half * 8 + 8, :], in_=pt[:], func=AF.Copy), reads=[ptb], writes=[oTb])
            st["lhs"] = lambda c: oT[:, c, :]
            st["lb"] = oTb
        else:
            if tt % 4 == 0:
                og, ogb = ogr.next()
                P.dma(og[:], OGv[:, :, r0:r0 + 512], writes=[ogb])
                cur_og[0] = (og, ogb)
            og, ogb = cur_og[0]
            ti = tt % 4
            st["lhs"] = lambda c: og[:, c, ti * 128:(ti + 1) * 128]
            st["lb"] = ogb
        xt, xb = xr.next()
        if final:
            P.dma(xt[0:64, :], xv[2 * tt], writes=[xb])
            P.dma(xt[64:128, :], xv[2 * tt + 1], writes=[xb])
        else:
            P.dma(xt[:], x_d[r0:r0 + 128, :], writes=[xb])
        st["xt"], st["xb"] = xt, xb
        return st

    def mm(st):
        lhs, lb = st["lhs"], st["lb"]
        st["ps"] = []
        for n in range(4):
            ps, psb = psY.next()
            for c in range(NCH):
                P.pe(lambda e, ps=ps, c=c, n=n: e.matmul(ps[:], lhs(c), wbf[:, c, n * 512:(n + 1) * 512], start=(c == 0), stop=(c == NCH - 1)),
                     reads=[lb, wbn[n]], writes=[psb])
            st["ps"].append((ps, psb))

    def back(st):
        tt, xt, xb = st["tt"], st["xt"], st["xb"]
        r0 = tt * 128
        for n, (ps, psb) in enumerate(st["ps"]):
            P.dve(lambda e, ps=ps, n=n: e.tensor_tensor(out=xt[:, n * 512:(n + 1) * 512], in0=ps[:], in1=xt[:, n * 512:(n + 1) * 512], op=ALU.add),
                  reads=[psb, xb], writes=[xb])
        if final:
            ss, ssb = ssr.next()
            sd, sdb = sdr.next()
            jk, jkb = jr.next()
            P.pool(lambda e: e.memset(ss[:], 0.0), writes=[ssb])
            P.act(lambda e: e.activation(out=jk[:], in_=xt[:], func=AF.Square, accum_out=ss[:]), reads=[xb], writes=[jkb, ssb])
            P.act(lambda e: e.activation(out=sd[:], in_=ss[:], func=AF.Sqrt, scale=1.0 / D, bias=C["eps"][:, 0:1]),
                  reads=[ssb, C["epsb"]], writes=[sdb])
            P.dve(lambda e: e.reciprocal(out=sd[:], in_=sd[:]), reads=[sdb], writes=[sdb])
            P.dve(lambda e: e.scalar_tensor_tensor(out=xt[:], in0=xt[:], scalar=sd[:, 0:1], in1=fgb_t[:], op0=ALU.mult, op1=ALU.mult),
                  reads=[xb, sdb, fgb], writes=[xb])
            P.dma(ov[2 * tt], xt[0:64, :], reads=[xb])
            P.dma(ov[2 * tt + 1], xt[64:128, :], reads=[xb])
        else:
            P.dma(out_d[r0:r0 + 128, :], xt[:], reads=[xb])

    NT = TOWN // 128
    prev = front(0)
    mm(prev)
    for tt in range(1, NT):
        cur = front(tt)
        back(prev)
        mm(cur)
        prev = cur
    back(prev)
    P.end_phase()


def emit_D(P, C, T):
    P.begin_phase()
    x_d, cvec_d, g_d, adaw_d, adab_d, win_d = T["x1"], T["cvec"], T["norm_g1"], T["ada_w1"], T["ada_b1"], T["w_in1"]
    SG_d, GB_d = T["SG1"], T["gate_bc1"]
    psA = Ring(P, "psA", [512], F32, 4, psum=True)
    psT = Ring(P, "psT", [8, 128], BF16, 2, psum=True)
    modp = P.sb("modp", [128, 2 * D], F32)
    modb = Buf("modp")
    with ExitStack() as es:
        gtmp = P.sb("gtmp", [128, D], F32, es)
        gbc = P.sb("gbc", [128, D], F32, es)
        gtb, gbb = Buf(), Buf()
        P.dma(gbc[:], g_d.partition_broadcast(128).rearrange("p o n -> p (o n)"), writes=[gbb])

        def dst_fn(nt):
            if nt < 8:
                return modp[:, nt * 512:(nt + 1) * 512], modb
            return gtmp[:, (nt - 8) * 512:(nt - 7) * 512], gtb
        emit_mod(P, C, cvec_d, adaw_d, adab_d, psA, dst_fn)
        P.dma(GB_d, gtmp[:], reads=[gtb])
        P.dve(lambda e: e.scalar_tensor_tensor(out=modp[:, D:2 * D], in0=modp[:, D:2 * D], scalar=1.0, in1=gbc[:], op0=ALU.add, op1=ALU.mult),
              reads=[modb, gbb], writes=[modb])
        P.barrier()
    hb_ = HBuilder(P, C, modp[:, 0:D], modp[:, D:2 * D], modb, psT)
    TH = 2048
    hT = P.sb("hT", [128, NCH, TH], BF16)
    hTb = Buf("hT")
    W = WStream(P, win_d)
    uor = Ring(P, "uo", [512], BF16, 3)
    gor = Ring(P, "go", [512], BF16, 3)
    xv = x_d.rearrange("(a b) m -> b a m", b=64)
    NG = 8
    for th in range(2):
        tok0 = th * TH
        W.dma(0)
        for tt in range(TH // 128):
            t2a = (tok0 + tt * 128) // 64
            hb_.tile([(0, 64, xv[t2a]), (64, 128, xv[t2a + 1])], hT, hTb, tt * 128)
        wb, wbb = W.casts(0)
        for g in range(NG):
            if g + 1 < NG:
                W.dma(g + 1)
            nxt = None
            if g < 4:
                for ci in range(4):
                    cc = 4 * g + ci
                    if ci == 2 and g + 1 < NG:
                        nxt = W.casts(g + 1)
                    for t5 in range(TH // 512):
                        pst, psb = psA.next()
                        for j in range(NCH):
                            P.pe(lambda e, pst=pst, j=j, t5=t5, wb=wb, ci=ci: e.matmul(pst[:], wb[:, j, ci * 128:(ci + 1) * 128], hT[:, j, t5 * 512:(t5 + 1) * 512], start=(j == 0), stop=(j == NCH - 1)),
                                 reads=[hTb, wbb], writes=[psb])
                        uo, uob = uor.next()
                        if t5 % 2 == 0:
                            P.act(lambda e, uo=uo, pst=pst: e.activation(out=uo[:], in_=pst[:], func=AF.Copy), reads=[psb], writes=[uob])
                        else:
                            P.dve(lambda e, uo=uo, pst=pst: e.tensor_copy(uo[:], pst[:]), reads=[psb], writes=[uob])
                        if cc < 8:
                            udst = T["UTown"][cc, :, tok0 + t5 * 512: tok0 + (t5 + 1) * 512]
                        else:
                            udst = T["UTsend"][(cc - 8) // 2][((cc - 8) % 2) * 128:((cc - 8) % 2) * 128 + 128, tok0 + t5 * 512: tok0 + (t5 + 1) * 512]
                        P.dma(udst, uo[:], reads=[uob])
            else:
                for tt in range(TH // 128):
                    if tt == 8 and g + 1 < NG:
                        nxt = W.casts(g + 1)
                    pst, psb = psA.next()
                    for j in range(NCH):
                        P.pe(lambda e, pst=pst, j=j, tt=tt, wb=wb: e.matmul(pst[:], hT[:, j, tt * 128:(tt + 1) * 128], wb[:, j, :], start=(j == 0), stop=(j == NCH - 1)),
                             reads=[hTb, wbb], writes=[psb])
                    go, gob = gor.next()
                    P.act(lambda e, go=go, pst=pst: e.activation(out=go[:], in_=pst[:], func=AF.Silu), reads=[psb], writes=[gob])
                    r0 = tok0 + tt * 128
                    P.dma(SG_d[r0:r0 + 128, (g - 4) * 512:(g - 3) * 512], go[:], reads=[gob])
            if nxt is not None:
                wb, wbb = nxt
    P.barrier()
    agb = Buf("ag")
    for j in range(4):
        P.allgather(T["UTsend"][j], T["UTag"][j], writes=[agb])
    P.end_phase()


def fourier_tables():
    w = np.arange(256)
    aw = 2 * np.pi * np.outer(w, w) / 256
    csw = np.concatenate([np.cos(aw), np.sin(aw)], 1) / 16.0
    csw = csw.reshape(2, 128, 512).transpose(1, 0, 2)
    t1 = np.arange(128)
    a1 = 2 * np.pi * np.outer(t1, t1) / 128
    nrm = 1.0 / math.sqrt(8192.0)
    ra = np.concatenate([np.cos(a1), np.sin(a1)], 1) * nrm
    rb = np.concatenate([-np.sin(a1), np.cos(a1)], 1) * nrm
    m = np.arange(128)
    t2 = m % 64
    c2 = m // 64
    atw = 2 * np.pi * np.outer(t2, np.arange(128)) / 8192
    ct, st = np.cos(atw), np.sin(atw)
    n = np.arange(128)
    c2n, k2 = n // 64, n % 64
    a2 = 2 * np.pi * np.outer(t2, k2) / 64
    delta = (c2[:, None] == c2n[None, :]).astype(np.float64)
    re = delta * np.cos(a2)
    rf = -delta * np.sin(a2)
    return {
        "t_csw": np.ascontiguousarray(csw).astype(np.float32).astype(NPBF),
        "t_ra": ra.astype(np.float32).astype(NPBF),
        "t_rb": rb.astype(np.float32).astype(NPBF),
        "t_ct": ct.astype(np.float32),
        "t_st": st.astype(np.float32),
        "t_re": re.astype(np.float32).astype(NPBF),
        "t_rf": rf.astype(np.float32).astype(NPBF),
    }


def emit_E(P, C, T):
    P.begin_phase()
    csw_d, ra_d, rb_d, ct_d, st_d, re_d, rf_d = T["t_csw"], T["t_ra"], T["t_rb"], T["t_ct"], T["t_st"], T["t_re"], T["t_rf"]
    sel = P.sb("sel", [128, 2], F32)
    selb = Buf("sel")
    P.dma(sel[:], T["sel"], writes=[selb])
    csw = P.sb("csw", [128, 2, 512], BF16)
    ra = P.sb("ra", [128, 256], BF16)
    rb = P.sb("rb", [128, 256], BF16)
    ct = P.sb("ct", [128, 128], F32)
    st = P.sb("st", [128, 128], F32)
    re = P.sb("re", [128, 128], BF16)
    rf = P.sb("rf", [128, 128], BF16)
    tb = Buf("tables")
    for dst, src in ((csw, csw_d), (ra, ra_d), (rb, rb_d), (ct, ct_d), (st, st_d), (re, re_d), (rf, rf_d)):
        P.dma(dst[:], src, writes=[tb])
    ps1 = Ring(P, "ps1", [512], F32, 2, psum=True)
    ps2 = Ring(P, "ps2", [512], F32, 2, psum=True)
    ps3 = Ring(P, "ps3", [512], F32, 2, psum=True)
    AB = P.sb("AB", [128, 512, 64], BF16)
    ABb = Buf("AB")
    G2 = P.sb("G2", [128, 64, 256], BF16)
    G2b = Buf("G2")
    fS = P.sb("fS", [128, 64, 128], BF16)
    fSb = Buf("fS")
    ur = Ring(P, "u", [2, 8, 128], BF16, 3)
    upb_ = [Buf("up0"), Buf("up1"), Buf("up2")]
    b0r = Ring(P, "ub0", [2, 512], BF16, 2)
    b1r = Ring(P, "ub1", [2, 512], BF16, 2)
    efr = Ring(P, "ef", [512], F32, 2)
    m1r = Ring(P, "m1", [2, 128], F32, 2)
    m2r = Ring(P, "m2", [2, 128], F32, 2)
    m3r = Ring(P, "m3", [2, 128], F32, 2)
    m4r = Ring(P, "m4", [2, 128], F32, 2)
    fov = T["f_own"].rearrange("(t a e) c -> e t a c", t=64, a=32, e=2)
    fsv = [T["fsend"][j].rearrange("(t a e) c -> e t a c", t=16, a=32, e=2) for j in range(4)]
    uk = 0
    ctb = ct[:].unsqueeze(1).to_broadcast([128, 2, 128])
    stb = st[:].unsqueeze(1).to_broadcast([128, 2, 128])
    k = 0
    for g in range(4):
        for t8 in range(8):
            u, ub = ur.next()
            upb = upb_[uk % 3]
            uk += 1
            b0, b0b = b0r.next()
            b1, b1b = b1r.next()
            c0 = t8 * 512
            for kc in range(2):
                P.dma(u[:, kc, :, 0:64], T["UTown"][2 * g + kc, :, c0:c0 + 512].rearrange("p (i t) -> p i t", t=64), writes=[ub])
            agv = T["UTag"][g].rearrange("(r k p) t -> p r k t", r=2, k=2)
            P.dma(b0[:], agv[:, 0, :, c0:c0 + 512], writes=[b0b])
            P.dma(b1[:], agv[:, 1, :, c0:c0 + 512], writes=[b1b])
            P.dve(lambda e, b1=b1: e.tensor_scalar(out=b1[:], in0=b1[:], scalar1=sel[:, 1:2], scalar2=None, op0=ALU.mult), reads=[b1b, selb], writes=[b1b])
            P.dve(lambda e, u=u, b0=b0, b1=b1: e.scalar_tensor_tensor(out=u[:, :, :, 64:128], in0=b0[:].rearrange("p k (i t) -> p k i t", t=64), scalar=sel[:, 0:1],
                                                                     in1=b1[:].rearrange("p k (i t) -> p k i t", t=64), op0=ALU.mult, op1=ALU.add),
                  reads=[b0b, b1b, selb], writes=[upb])
            for ti in range(8):
                t2 = t8 * 8 + ti
                ps, psb = ps1.next()
                for kc in range(2):
                    P.pe(lambda e, ps=ps, u=u, kc=kc, ti=ti: e.matmul(ps[:], u[:, kc, ti, :], csw[:, kc, :], start=(kc == 0), stop=(kc == 1)),
                         reads=[ub, upb, tb], writes=[psb])
                k += 1
                if k % 2 == 0:
                    P.act(lambda e, ps=ps, t2=t2: e.activation(out=AB[:, :, t2], in_=ps[:], func=AF.Copy), reads=[psb], writes=[ABb])
                else:
                    P.dve(lambda e, ps=ps, t2=t2: e.tensor_copy(AB[:, :, t2], ps[:]), reads=[psb], writes=[ABb])
        for half in range(2):
            for pb in range(32):
                ps, psb = ps2.next()
                for pi in range(2):
                    pl = 2 * pb + pi
                    ch = half * 128 + 2 * pl
                    P.pe(lambda e, ps=ps, pi=pi, ch=ch: e.matmul(ps[:, pi * 256:(pi + 1) * 256], AB[:, ch:ch + 2, :].rearrange("p a b -> p (a b)"), ra[:], start=True, stop=False),
                         reads=[ABb, tb], writes=[psb])
                    P.pe(lambda e, ps=ps, pi=pi, ch=ch: e.matmul(ps[:, pi * 256:(pi + 1) * 256], AB[:, 256 + ch:256 + ch + 2, :].rearrange("p a b -> p (a b)"), rb[:], start=False, stop=True),
                         reads=[ABb, tb], writes=[psb])
                ef, efb = efr.next()
                P.act(lambda e, ef=ef, ps=ps: e.activation(out=ef[:], in_=ps[:], func=AF.Copy), reads=[psb], writes=[efb])
                efv = ef[:].rearrange("p (a b) -> p a b", a=2)
                Ev, Fv = efv[:, :, 0:128], efv[:, :, 128:256]
                m1, m1b = m1r.next()
                m2, m2b = m2r.next()
                m3, m3b = m3r.next()
                m4, m4b = m4r.next()
                P.pool(lambda e, m1=m1, Ev=Ev: e.tensor_tensor(out=m1[:], in0=Ev, in1=ctb, op=ALU.mult), reads=[efb, tb], writes=[m1b])
                P.pool(lambda e, m2=m2, Fv=Fv: e.tensor_tensor(out=m2[:], in0=Fv, in1=stb, op=ALU.mult), reads=[efb, tb], writes=[m2b])
                P.dve(lambda e, m3=m3, Ev=Ev: e.tensor_tensor(out=m3[:], in0=Ev, in1=stb, op=ALU.mult), reads=[efb, tb], writes=[m3b])
                P.pool(lambda e, m4=m4, Fv=Fv: e.tensor_tensor(out=m4[:], in0=Fv, in1=ctb, op=ALU.mult), reads=[efb, tb], writes=[m4b])
                P.dve(lambda e, m1=m1, m2=m2, pb=pb: e.tensor_tensor(out=G2[:, 2 * pb:2 * pb + 2, 0:128], in0=m1[:], in1=m2[:], op=ALU.subtract),
                      reads=[m1b, m2b], writes=[G2b])
                P.dve(lambda e, m3=m3, m4=m4, pb=pb: e.tensor_tensor(out=G2[:, 2 * pb:2 * pb + 2, 128:256], in0=m3[:], in1=m4[:], op=ALU.add),
                      reads=[m3b, m4b], writes=[G2b])
            for qb in range(16):
                ps, psb = ps3.next()
                for pi in range(4):
                    pl = 4 * qb + pi
                    P.pe(lambda e, ps=ps, pi=pi, pl=pl: e.matmul(ps[:, pi * 128:(pi + 1) * 128], G2[:, pl, 0:128], re[:], start=True, stop=False),
                         reads=[G2b, tb], writes=[psb])
                    P.pe(lambda e, ps=ps, pi=pi, pl=pl: e.matmul(ps[:, pi * 128:(pi + 1) * 128], G2[:, pl, 128:256], rf[:], start=False, stop=True),
                         reads=[G2b, tb], writes=[psb])
                dst = fS[:, :, 8 * qb:8 * qb + 8].rearrange("p k (a c) -> p a c k", a=4, c=2)
                src_fn = lambda ps=ps: ps[:].rearrange("p (a c k) -> p a c k", a=4, c=2)
                if qb % 2 == 0:
                    P.act(lambda e, dst=dst, src_fn=src_fn: e.activation(out=dst, in_=src_fn(), func=AF.Copy), reads=[psb], writes=[fSb])
                else:
                    P.dve(lambda e, dst=dst, src_fn=src_fn: e.tensor_copy(dst, src_fn()), reads=[psb], writes=[fSb])
            c0 = g * 256 + half * 128
            for e_ in range(2):
                P.dma(fov[e_][:, :, c0:c0 + 128], fS[e_ * 64:(e_ + 1) * 64, 0:32, :], reads=[fSb])
                for j in range(4):
                    P.dma(fsv[j][e_][:, :, c0:c0 + 128], fS[e_ * 64 + 16 * j:e_ * 64 + 16 * j + 16, 32:64, :], reads=[fSb])
    P.barrier()
    agb = Buf("ag")
    for j in range(4):
        P.allgather(T["fsend"][j], T["fag"][j], writes=[agb])
    P.end_phase()


def assemble_UTf(ut0, ut1, hf):
    a = np.asarray(ut0)[8 * hf:8 * hf + 8].reshape(8, 128, 64, 64)
    b = np.asarray(ut1)[8 * hf:8 * hf + 8].reshape(8, 128, 64, 64)
    return np.ascontiguousarray(np.concatenate([a, b], axis=3).reshape(8, 128, S))


def build_fused():
    P = Prog()
    T = {}
    I32 = mybir.dt.int32
    T["x"] = P.din("x", [TOWN, D], F32)
    T["cvec"] = P.din("cvec", [128, NCH], F32)
    T["norm_g0"] = P.din("norm_g0", [1, D], F32)
    T["norm_g1"] = P.din("norm_g1", [1, D], F32)
    T["ada_w0"] = P.din("ada_w0", [D, 3 * D], F32)
    T["ada_w1"] = P.din("ada_w1", [D, 3 * D], F32)
    T["ada_b0"] = P.din("ada_b0", [1, 3 * D], F32)
    T["ada_b1"] = P.din("ada_b1", [1, 3 * D], F32)
    T["w_in0"] = P.din("w_in0", [D, 5120], F32)
    T["qg"] = P.din("qg", [128, 1], F32)
    T["kg"] = P.din("kg", [128, 1], F32)
    T["ropeC"] = P.din("ropeC", [128, TOWN], F32)
    T["ropeS"] = P.din("ropeS", [128, TOWN], F32)
    T["w_out0"] = P.din("w_out0", [D, D], F32)
    T["w_in1"] = P.din("w_in1", [D, 4096], F32)
    T["w_out1"] = P.din("w_out1", [D, D], F32)
    T["final_g"] = P.din("final_g", [1, D], F32)
    T["sel"] = P.din("sel", [128, 2], F32)
    T["t_csw"] = P.din("t_csw", [128, 2, 512], BF16)
    T["t_ra"] = P.din("t_ra", [128, 256], BF16)
    T["t_rb"] = P.din("t_rb", [128, 256], BF16)
    T["t_ct"] = P.din("t_ct", [128, 128], F32)
    T["t_st"] = P.din("t_st", [128, 128], F32)
    T["t_re"] = P.din("t_re", [128, 128], BF16)
    T["t_rf"] = P.din("t_rf", [128, 128], BF16)
    T["out"] = P.dout("out", [TOWN, D], F32)
    T["QT"] = P.dint("QT", [16, 128, TOWN], BF16)
    T["SG"] = P.dint("SG", [16, 128, TOWN], BF16)
    T["gate_bc0"] = P.dint("gate_bc0", [128, D], F32)
    T["gate_bc1"] = P.dint("gate_bc1", [128, D], F32)
    T["KTown"] = [P.dint(f"KTown{j}", [256, TOWN], BF16) for j in range(2)]
    T["KTag"] = [P.dint(f"KTag{j}", [512, TOWN], BF16) for j in range(2)]
    T["Vown"] = [P.dint(f"Vown{j}", [2048, 512], BF16) for j in range(2)]
    T["Vag"] = [P.dint(f"Vag{j}", [4096, 512], BF16) for j in range(2)]
    T["OGT"] = P.dint("OGT", [16, 128, TOWN], BF16)
    T["x1"] = P.dint("x1", [TOWN, D], F32)
    T["UTown"] = P.dint("UTown", [8, 128, TOWN], BF16)
    T["UTsend"] = [P.dint(f"UTsend{j}", [256, TOWN], BF16) for j in range(4)]
    T["UTag"] = [P.dint(f"UTag{j}", [512, TOWN], BF16) for j in range(4)]
    T["SG1"] = P.dint("SG1", [TOWN, D], BF16)
    T["f_own"] = P.dint("f_own", [TOWN, 1024], BF16)
    T["fsend"] = [P.dint(f"fsend{j}", [1024, 1024], BF16) for j in range(4)]
    T["fag"] = [P.dint(f"fag{j}", [2048, 1024], BF16) for j in range(4)]
    C = emit_consts(P)
    emit_A(P, C, T)
    emit_B(P, C, T)
    emit_CF(P, C, T, False)
    emit_D(P, C, T)
    emit_E(P, C, T)
    emit_CF(P, C, T, True)
    return P.finish()


def core_tables(hf):
    tb = fourier_tables()
    q = (np.arange(128) + 64 * hf) % 128
    tb["t_ra"] = np.ascontiguousarray(tb["t_ra"][q])
    tb["t_rb"] = np.ascontiguousarray(tb["t_rb"][q])
    n = np.arange(128)
    k2 = ((n % 64) + 32 * hf) % 64
    col = (n // 64) * 64 + k2
    tb["t_re"] = np.ascontiguousarray(tb["t_re"][:, col])
    tb["t_rf"] = np.ascontiguousarray(tb["t_rf"][:, col])
    return tb


def kernel(x, c, norm_g, ada_w, ada_b, attn_w_in, attn_q_gain, attn_k_gain,
           attn_w_out, fourier_w_in, fourier_w_out, final_g):
    x = np.asarray(x)
    c = np.asarray(c)
    norm_g = np.asarray(norm_g)
    ada_w = np.asarray(ada_w)
    ada_b = np.asarray(ada_b)
    w_in = np.asarray(attn_w_in)[0]
    fw_in = np.asarray(fourier_w_in)[0]
    fw_out = np.asarray(fourier_w_out)[0]
    ropeC, ropeS = rope_tables()
    colperm = np.arange(5120)
    for h in range(20):
        colperm[h * 128:(h + 1) * 128] = h * 128 + PERM
    w_in_p = np.ascontiguousarray(w_in[:, colperm])
    consts = host_consts()
    qg = np.ascontiguousarray(np.asarray(attn_q_gain)[0][PERM].reshape(128, 1))
    kg = np.ascontiguousarray(np.asarray(attn_k_gain)[0][PERM].reshape(128, 1))
    per_hf = []
    for hf in range(2):
        ch = np.concatenate([np.arange(1024 * hf, 1024 * hf + 1024), np.arange(1024 * (1 - hf), 1024 * (1 - hf) + 1024)])
        d = dict(core_tables(hf))
        d["w_in1"] = np.ascontiguousarray(np.concatenate([fw_in[:, ch], fw_in[:, 2048 + ch]], axis=1))
        d["w_out1"] = np.ascontiguousarray(fw_out[ch, :])
        d["sel"] = np.ascontiguousarray(np.broadcast_to(np.array([float(hf), float(1 - hf)], np.float32), (128, 2)))
        d["ropeC"] = np.ascontiguousarray(ropeC[:, hf * TOWN:(hf + 1) * TOWN])
        d["ropeS"] = np.ascontiguousarray(ropeS[:, hf * TOWN:(hf + 1) * TOWN])
        per_hf.append(d)
    maps = []
    for core in range(NCORES):
        b, hf = core % 4, core // 4
        m = dict(consts)
        m.update(per_hf[hf])
        m["x"] = np.ascontiguousarray(x[b, hf * TOWN:(hf + 1) * TOWN])
        m["cvec"] = np.ascontiguousarray(c[b].reshape(NCH, 128).T)
        m["norm_g0"] = np.ascontiguousarray(norm_g[0:1])
        m["norm_g1"] = np.ascontiguousarray(norm_g[1:2])
        m["ada_w0"] = ada_w[0]
        m["ada_w1"] = ada_w[1]
        m["ada_b0"] = np.ascontiguousarray(ada_b[0:1])
        m["ada_b1"] = np.ascontiguousarray(ada_b[1:2])
        m["w_in0"] = w_in_p
        m["qg"] = qg
        m["kg"] = kg
        m["w_out0"] = np.asarray(attn_w_out)[0]
        m["final_g"] = np.ascontiguousarray(np.asarray(final_g).reshape(1, D))
        maps.append(m)
    res = run_bass_kernel_spmd(build_fused(), maps, core_ids=list(range(NCORES))).results
    out = np.empty((NB, S, D), np.float32)
    for core in range(NCORES):
        b, hf = core % 4, core // 4
        out[b, hf * TOWN:(hf + 1) * TOWN] = res[core]["out"]
    return out
```

```python
import math
from contextlib import ExitStack
import numpy as np
import ml_dtypes
import concourse.bass as bass
import concourse.mybir as mybir
from concourse.bass_utils import run_bass_kernel_spmd

F32 = mybir.dt.float32
BF16 = mybir.dt.bfloat16
AF = mybir.ActivationFunctionType
ALU = mybir.AluOpType
NPBF = ml_dtypes.bfloat16

D = 2048
S = 8192
NB = 4
NCORES = 8
TOWN = 4096
HD = 128
EPS = 1e-6
NCH = D // 128
GROUPS = [[0, 4], [1, 5], [2, 6], [3, 7]]


class Buf:
    __slots__ = ("name", "w", "r", "lsem", "ssem")

    def __init__(self, name=""):
        self.name = name
        self.w = None
        self.r = []
        self.lsem = None
        self.ssem = None


class Prog:
    ENG = ("pe", "act", "dve", "pool", "sp")

    def __init__(self):
        self.nc = bass.Bass("TRN2", target_bir_lowering=False)
        self.es = ExitStack()
        self.ops = {e: [] for e in self.ENG}
        self.N = {e: 0 for e in self.ENG}
        self.S = {e: self.es.enter_context(self.nc.semaphore("cnt_" + e)) for e in self.ENG}
        self.waited = {e: {} for e in self.ENG}
        self.semval = {}
        self.nsem = 5
        self.uid = 0
        self.dram = {}
        self.scope = self.es
        self.phase_sems = []
        self.free_sems = []
        self.ccsems = {}

    def name(self, base):
        self.uid += 1
        return f"{base}_{self.uid}"

    def sb(self, name, shape, dt, es=None):
        return (es or self.scope).enter_context(self.nc.sbuf_tensor(self.name(name), list(shape), dt))

    def ps(self, name, shape, dt, es=None):
        return (es or self.scope).enter_context(self.nc.psum_tensor(self.name(name), list(shape), dt))

    def newsem(self, name):
        if self.free_sems:
            s = self.free_sems.pop()
        else:
            self.nsem += 1
            s = self.es.enter_context(self.nc.semaphore(self.name(name)))
            self.semval[s] = 0
        self.phase_sems.append(s)
        return s

    def begin_phase(self):
        self.scope = ExitStack()
        self.phase_sems = []

    def end_phase(self):
        self.barrier()
        self.scope.close()
        self.scope = self.es
        self.free_sems.extend(self.phase_sems)
        self.phase_sems = []

    def dint(self, name, shape, dt):
        return self.nc.dram_tensor(name, list(shape), dt).ap()

    def allgather(self, in_ap, out_ap, reads=(), writes=()):
        q = "pool"
        self._deps(q, reads, writes)
        self.nsem += 1
        sem = self.es.enter_context(self.nc.semaphore(self.name("cc")))
        self.ccsems[sem] = 1
        self.ops[q].append(lambda e: e.collective_compute("AllGather", ALU.bypass, replica_groups=GROUPS,
                                                          ins=[in_ap.opt()], outs=[out_ap.opt()]).then_inc(sem, 1))
        tok = ("dma", sem, 1)
        for b in reads:
            b.r.append(tok)
        for b in writes:
            b.w = tok
            b.r = []
        return tok

    def din(self, name, shape, dt):
        t = self.nc.dram_tensor(name, list(shape), dt, kind="ExternalInput").ap()
        self.dram[name] = t
        return t

    def dout(self, name, shape, dt):
        t = self.nc.dram_tensor(name, list(shape), dt, kind="ExternalOutput").ap()
        self.dram[name] = t
        return t

    def _need(self, eng, tok):
        if tok[0] == "eng":
            _, f, n = tok
            if f == eng and eng == "pe":
                return
            sem, key, val = self.S[f], ("e", f), n
        else:
            _, sem, val = tok
            key = ("d", sem)
        if self.waited[eng].get(key, 0) >= val:
            return
        self.waited[eng][key] = val
        self.ops[eng].append(lambda e, sem=sem, val=val: e.wait_ge(sem, val))

    def _deps(self, eng, reads, writes):
        for b in reads:
            if b.w is not None:
                self._need(eng, b.w)
        for b in writes:
            if b.w is not None:
                self._need(eng, b.w)
            for t in b.r:
                self._need(eng, t)

    def op(self, eng, fn, reads=(), writes=()):
        self._deps(eng, reads, writes)
        self.N[eng] += 1
        tok = ("eng", eng, self.N[eng])
        sem = self.S[eng]
        self.ops[eng].append(lambda e, fn=fn, sem=sem: fn(e).then_inc(sem, 1))
        for b in reads:
            b.r.append(tok)
        for b in writes:
            b.w = tok
            b.r = []
        return tok

    def pe(self, fn, reads=(), writes=()):
        return self.op("pe", fn, reads, writes)

    def act(self, fn, reads=(), writes=()):
        return self.op("act", fn, reads, writes)

    def dve(self, fn, reads=(), writes=()):
        return self.op("dve", fn, reads, writes)

    def pool(self, fn, reads=(), writes=()):
        return self.op("pool", fn, reads, writes)

    def dma(self, out_ap, in_ap, reads=(), writes=(), q="sp", sem=None):
        self._deps(q, reads, writes)
        if sem is None:
            if writes and not reads:
                b = writes[0]
                if b.lsem is None:
                    b.lsem = self.newsem("l")
                sem = b.lsem
            else:
                b = reads[0]
                if b.ssem is None:
                    b.ssem = self.newsem("s")
                sem = b.ssem
        self.semval[sem] += 16
        val = self.semval[sem]
        self.ops[q].append(lambda e, o=out_ap, i=in_ap, sem=sem: e.dma_start(out=o, in_=i).then_inc(sem, 16))
        tok = ("dma", sem, val)
        for b in reads:
            b.r.append(tok)
        for b in writes:
            b.w = tok
            b.r = []
        return tok

    def barrier(self):
        for e in self.ENG:
            for f in self.ENG:
                if f != e and self.N[f] > 0:
                    self._need(e, ("eng", f, self.N[f]))
            for s, v in self.semval.items():
                if v > 0:
                    self._need(e, ("dma", s, v))
            for s, v in self.ccsems.items():
                self._need(e, ("dma", s, v))

    def finish(self):
        for s, v in self.semval.items():
            if v > 0:
                self._need("sp", ("dma", s, v))
        ops = self.ops
        with self.nc.Block() as block:
            @block.tensor
            def _(e):
                for o in ops["pe"]:
                    o(e)

            @block.scalar
            def _(e):
                for o in ops["act"]:
                    o(e)

            @block.vector
            def _(e):
                for o in ops["dve"]:
                    o(e)

            @block.gpsimd
            def _(e):
                for o in ops["pool"]:
                    o(e)

            @block.sync
            def _(e):
                for o in ops["sp"]:
                    o(e)
        self.es.close()
        return self.nc


class Ring:
    def __init__(self, P, name, shape, dt, n, es=None, psum=False):
        self.n = n
        self.i = 0
        alloc = P.ps if psum else P.sb
        self.t = [alloc(name, [128] + list(shape), dt, es) for _ in range(n)]
        self.b = [Buf(f"{name}{k}") for k in range(n)]

    def next(self):
        k = self.i % self.n
        self.i += 1
        return self.t[k], self.b[k]


def emit_consts(P):
    c = {}
    ident_d = P.din("c_ident", [128, 128], BF16)
    ones_d = P.din("c_ones", [128, 128], BF16)
    onesf_d = P.din("c_onesf", [128, 128], F32)
    c["ident"] = P.sb("ident", [128, 128], BF16)
    c["ones"] = P.sb("ones", [128, 128], BF16)
    c["onesf"] = P.sb("onesf", [128, 128], F32)
    c["eps"] = P.sb("eps", [128, 1], F32)
    cb = Buf("consts")
    c["buf"] = cb
    P.dma(c["ident"][:], ident_d, writes=[cb])
    P.dma(c["ones"][:], ones_d, writes=[cb])
    P.dma(c["onesf"][:], onesf_d, writes=[cb])
    eb = Buf("eps")
    P.pool(lambda e: e.memset(c["eps"][:], EPS), writes=[eb])
    c["epsb"] = eb
    return c


def emit_mod(P, C, cvec_d, adaw_d, adab_d, psring, dst_fn):
    with ExitStack() as es:
        cv = P.sb("cv", [128, NCH], F32, es)
        crep = P.sb("crep", [128, NCH, 128], F32, es)
        abrow = P.sb("abrow", [1, 3 * D], F32, es)
        cvb, crb, abb = Buf(), Buf(), Buf()
        P.dma(cv[:], cvec_d, writes=[cvb])
        P.dma(abrow[:], adab_d, writes=[abb])
        P.act(lambda e: e.activation(out=cv[:], in_=cv[:], func=AF.Silu), reads=[cvb], writes=[cvb])
        P.dve(lambda e: e.tensor_copy(crep[:], cv[:].unsqueeze(2).to_broadcast([128, NCH, 128])), reads=[cvb], writes=[crb])
        awr = Ring(P, "aw", [NCH, 512], F32, 2, es)
        awv = adaw_d.rearrange("(j p) n -> p j n", p=128)
        for nt in range(12):
            aw, awb = awr.next()
            P.dma(aw[:], awv[:, :, nt * 512:(nt + 1) * 512], writes=[awb])
            pst, psb = psring.next()
            for j in range(NCH):
                P.pe(lambda e, j=j, aw=aw, pst=pst: e.matmul(pst[:], crep[:, j, :], aw[:, j, :], start=(j == 0), stop=False),
                     reads=[crb, awb], writes=[psb])
            P.pe(lambda e, nt=nt, pst=pst: e.matmul(pst[:], C["onesf"][0:1, :], abrow[0:1, nt * 512:(nt + 1) * 512], start=False, stop=True),
                 reads=[C["buf"], abb], writes=[psb])
            dst, dstb = dst_fn(nt)
            P.act(lambda e, dst=dst, pst=pst: e.activation(out=dst, in_=pst[:], func=AF.Copy), reads=[psb], writes=[dstb])
        P.barrier()


class HBuilder:
    def __init__(self, P, C, shift_bc, gs_bc, modb, ptring):
        self.P, self.C = P, C
        self.shift_bc, self.gs_bc, self.modb = shift_bc, gs_bc, modb
        self.xr = Ring(P, "xt", [D], F32, 2)
        self.hr = Ring(P, "hb", [D], BF16, 2)
        self.ssr = Ring(P, "ss", [1], F32, 2)
        self.sdr = Ring(P, "sd", [1], F32, 2)
        self.ptr = ptring
        self.k = 0

    def tile(self, srcs, hT, hTb, col0):
        P, C = self.P, self.C
        xt, xb = self.xr.next()
        hb, hbb = self.hr.next()
        ss, ssb = self.ssr.next()
        sd, sdb = self.sdr.next()
        for (p0, p1, ap) in srcs:
            P.dma(xt[p0:p1, :], ap, writes=[xb])
        P.pool(lambda e: e.memset(ss[:], 0.0), writes=[ssb])
        P.act(lambda e: e.activation(out=hb[:], in_=xt[:], func=AF.Square, accum_out=ss[:]), reads=[xb], writes=[hbb, ssb])
        P.act(lambda e: e.activation(out=sd[:], in_=ss[:], func=AF.Sqrt, scale=1.0 / D, bias=C["eps"][:, 0:1]),
              reads=[ssb, C["epsb"]], writes=[sdb])
        P.dve(lambda e: e.reciprocal(out=sd[:], in_=sd[:]), reads=[sdb], writes=[sdb])
        P.dve(lambda e: e.scalar_tensor_tensor(out=xt[:], in0=xt[:], scalar=sd[:, 0:1], in1=self.gs_bc, op0=ALU.mult, op1=ALU.mult),
              reads=[xb, sdb, self.modb], writes=[xb])
        P.pool(lambda e: e.tensor_tensor(out=hb[:], in0=xt[:], in1=self.shift_bc, op=ALU.add), reads=[xb, self.modb], writes=[hbb])
        for half in range(2):
            pt, ptb = self.ptr.next()
            for c in range(8):
                ch = half * 8 + c
                P.pe(lambda e, pt=pt, c=c, ch=ch: e.transpose(pt[:, c, :], hb[:, ch * 128:(ch + 1) * 128], C["ident"][:]),
                     reads=[hbb, C["buf"]], writes=[ptb])
            dst = hT[:, half * 8:half * 8 + 8, col0:col0 + 128]
            if (self.k + half) % 2 == 0:
                P.act(lambda e, dst=dst, pt=pt: e.activation(out=dst, in_=pt[:], func=AF.Copy), reads=[ptb], writes=[hTb[0]])
            else:
                P.dve(lambda e, dst=dst, pt=pt: e.tensor_copy(dst, pt[:]), reads=[ptb], writes=[hTb[1]])
        self.k += 1


class WStream:
    def __init__(self, P, w_d):
        self.P = P
        self.wv = w_d.rearrange("(j p) n -> p j n", p=128)
        self.wfr = Ring(P, "wf", [4, 512], F32, 4)
        self.wbr = Ring(P, "wb", [NCH, 512], BF16, 2)
        self.pending = {}
        self.k = 0

    def dma(self, g):
        wb, wbb = self.wbr.next()
        st = []
        for jq in range(4):
            wf, wfb = self.wfr.next()
            self.P.dma(wf[:], self.wv[:, 4 * jq:4 * jq + 4, g * 512:(g + 1) * 512], writes=[wfb])
            st.append((wf, wfb))
        self.pending[g] = (wb, wbb, st)

    def casts(self, g):
        P = self.P
        wb, wbb, st = self.pending.pop(g)
        self.k += 1
        for jq, (wf, wfb) in enumerate(st):
            dst = wb[:, 4 * jq:4 * jq + 4, :]
            if self.k % 2 == 0:
                P.dve(lambda e, dst=dst, wf=wf: e.tensor_copy(dst, wf[:]), reads=[wfb], writes=[wbb])
            else:
                P.act(lambda e, dst=dst, wf=wf: e.activation(out=dst, in_=wf[:], func=AF.Copy), reads=[wfb], writes=[wbb])
        return wb, wbb


def emit_A(P, C, T):
    P.begin_phase()
    x_d, cvec_d, g_d, adaw_d, adab_d, win_d = T["x"], T["cvec"], T["norm_g0"], T["ada_w0"], T["ada_b0"], T["w_in0"]
    qg_d, kg_d, rc_d, rs_d = T["qg"], T["kg"], T["ropeC"], T["ropeS"]
    QT_d, SG_d, GB_d = T["QT"], T["SG"], T["gate_bc0"]

    psA = Ring(P, "psA", [512], F32, 3, psum=True)
    psB = Ring(P, "psB", [512], F32, 2, psum=True)
    psT = Ring(P, "psT", [8, 128], BF16, 2, psum=True)

    modp = P.sb("modp", [128, 2 * D], F32)
    modb = Buf("modp")
    gains = P.sb("gains", [128, 2], F32)
    gainb = Buf("gains")
    P.dma(gains[:, 0:1], qg_d, writes=[gainb])
    P.dma(gains[:, 1:2], kg_d, writes=[gainb])

    with ExitStack() as es:
        gtmp = P.sb("gtmp", [128, D], F32, es)
        gbc = P.sb("gbc", [128, D], F32, es)
        gtb, gbb = Buf(), Buf()
        P.dma(gbc[:], g_d.partition_broadcast(128).rearrange("p o n -> p (o n)"), writes=[gbb])

        def dst_fn(nt):
            if nt < 8:
                return modp[:, nt * 512:(nt + 1) * 512], modb
            return gtmp[:, (nt - 8) * 512:(nt - 7) * 512], gtb
        emit_mod(P, C, cvec_d, adaw_d, adab_d, psA, dst_fn)
        P.dma(GB_d, gtmp[:], reads=[gtb])
        P.dve(lambda e: e.scalar_tensor_tensor(out=modp[:, D:2 * D], in0=modp[:, D:2 * D], scalar=1.0, in1=gbc[:], op0=ALU.add, op1=ALU.mult),
              reads=[modb, gbb], writes=[modb])
        P.barrier()

    hb_ = HBuilder(P, C, modp[:, 0:D], modp[:, D:2 * D], modb, psT)
    TH = 2048
    hT = P.sb("hT", [128, NCH, TH], BF16)
    hTb = (Buf("hTa"), Buf("hTd"))
    rC = P.sb("rC", [128, TH], F32)
    rS = P.sb("rS", [128, TH], F32)
    ropeb = Buf("rope")
    W = WStream(P, win_d)
    sqr = Ring(P, "sq", [512], BF16, 2)
    sdr = Ring(P, "sdq", [512], F32, 2)
    qnr = Ring(P, "qn", [512], F32, 2)
    t1r = Ring(P, "t1", [512], F32, 2)
    t2r = Ring(P, "t2", [512], F32, 2)
    qor = Ring(P, "qo", [512], BF16, 3)
    vor = qor
    NG = 10

    for th in range(2):
        tok0 = th * TH
        Vv = T["Vown"][th]
        W.dma(0)
        for tt in range(TH // 128):
            r0 = tok0 + tt * 128
            hb_.tile([(0, 128, x_d[r0:r0 + 128, :])], hT, hTb, tt * 128)
        P.dma(rC[:], rc_d[:, tok0:tok0 + TH], writes=[ropeb])
        P.dma(rS[:], rs_d[:, tok0:tok0 + TH], writes=[ropeb])
        pend_a, pend_b = [], []

        def stage1a(st):
            pst, psb, sq, sqb, t5, cc = st["pst"], st["psb"], st["sq"], st["sqb"], st["t5"], st["cc"]
            gcol = gains[:, 0:1] if cc < 16 else gains[:, 1:2]
            pB, pBb = psB.next()
            P.pe(lambda e: e.matmul(pB[:], C["ones"][:], sq[:], start=True, stop=True), reads=[C["buf"], sqb], writes=[pBb])
            sd, sdb = sdr.next()
            qn, qnb = qnr.next()
            t1, t1b = t1r.next()
            t2, t2b = t2r.next()
            P.act(lambda e: e.activation(out=sd[:], in_=pB[:], func=AF.Ln, scale=1.0 / HD, bias=C["eps"][:, 0:1]),
                  reads=[pBb, C["epsb"]], writes=[sdb])
            P.act(lambda e: e.activation(out=sd[:], in_=sd[:], func=AF.Exp, scale=-0.5), reads=[sdb], writes=[sdb])
            P.dve(lambda e: e.scalar_tensor_tensor(out=qn[:], in0=pst[:], scalar=gcol, in1=sd[:], op0=ALU.mult, op1=ALU.mult),
                  reads=[psb, sdb, gainb], writes=[qnb])
            cs = slice(t5 * 512, (t5 + 1) * 512)
            P.dve(lambda e: e.tensor_tensor(out=t1[:], in0=qn[:], in1=rC[:, cs], op=ALU.mult), reads=[qnb, ropeb], writes=[t1b])
            P.pool(lambda e: e.tensor_tensor(out=t2[0:64, :], in0=qn[64:128, :], in1=rS[64:128, cs], op=ALU.mult), reads=[qnb, ropeb], writes=[t2b])
            P.pool(lambda e: e.tensor_tensor(out=t2[64:128, :], in0=qn[0:64, :], in1=rS[0:64, cs], op=ALU.mult), reads=[qnb, ropeb], writes=[t2b])
            st.update(t1=t1, t1b=t1b, t2=t2, t2b=t2b)

        def stage1b(st):
            t1, t1b, t2, t2b, t5, cc = st["t1"], st["t1b"], st["t2"], st["t2b"], st["t5"], st["cc"]
            qo, qob = qor.next()
            P.dve(lambda e: e.tensor_tensor(out=qo[:], in0=t1[:], in1=t2[:], op=ALU.add), reads=[t1b, t2b], writes=[qob])
            if cc < 16:
                dstd = QT_d[cc, :, tok0 + t5 * 512: tok0 + (t5 + 1) * 512]
            else:
                dstd = T["KTown"][(cc - 16) // 2][((cc - 16) % 2) * 128:((cc - 16) % 2) * 128 + 128, tok0 + t5 * 512: tok0 + (t5 + 1) * 512]
            P.dma(dstd, qo[:], reads=[qob])

        def flush():
            while pend_a:
                st = pend_a.pop(0)
                stage1a(st)
                pend_b.append(st)
            while pend_b:
                stage1b(pend_b.pop(0))

        wb, wbb = W.casts(0)
        for g in range(NG):
            if g + 1 < NG:
                W.dma(g + 1)
            nxt = None
            if g == 5:
                flush()
                for tt in range(TH // 128):
                    pst, psb = psA.next()
                    for j in range(NCH):
                        P.pe(lambda e, pst=pst, j=j, tt=tt, wb=wb: e.matmul(pst[:], hT[:, j, tt * 128:(tt + 1) * 128], wb[:, j, :], start=(j == 0), stop=(j == NCH - 1)),
                             reads=[hTb[0], hTb[1], wbb], writes=[psb])
                    vo, vob = vor.next()
                    if tt % 2 == 0:
                        P.dve(lambda e, vo=vo, pst=pst: e.tensor_copy(vo[:], pst[:]), reads=[psb], writes=[vob])
                    else:
                        P.act(lambda e, vo=vo, pst=pst: e.activation(out=vo[:], in_=pst[:], func=AF.Copy), reads=[psb], writes=[vob])
                    P.dma(Vv[tt * 128:(tt + 1) * 128, :], vo[:], reads=[vob])
                    if tt == 8:
                        nxt = W.casts(g + 1)
                wb, wbb = nxt
                continue
            for ci in range(4):
                cc = 4 * g + ci if g < 5 else 4 * g + ci
                if ci == 2 and g + 1 < NG:
                    nxt = W.casts(g + 1)
                for t5 in range(TH // 512):
                    pst, psb = psA.next()
                    for j in range(NCH):
                        P.pe(lambda e, pst=pst, j=j, t5=t5, wb=wb, ci=ci: e.matmul(pst[:], wb[:, j, ci * 128:(ci + 1) * 128], hT[:, j, t5 * 512:(t5 + 1) * 512], start=(j == 0), stop=(j == NCH - 1)),
                             reads=[hTb[0], hTb[1], wbb], writes=[psb])
                    if cc < 20:
                        sq, sqb = sqr.next()
                        P.act(lambda e, sq=sq, pst=pst: e.activation(out=sq[:], in_=pst[:], func=AF.Square), reads=[psb], writes=[sqb])
                        if pend_b:
                            stage1b(pend_b.pop(0))
                        if pend_a:
                            st = pend_a.pop(0)
                            stage1a(st)
                            pend_b.append(st)
                        pend_a.append(dict(pst=pst, psb=psb, sq=sq, sqb=sqb, t5=t5, cc=cc))
                    else:
                        qo, qob = qor.next()
                        P.act(lambda e, qo=qo, pst=pst: e.activation(out=qo[:], in_=pst[:], func=AF.Silu), reads=[psb], writes=[qob])
                        P.dma(SG_d[cc - 24, :, tok0 + t5 * 512: tok0 + (t5 + 1) * 512], qo[:], reads=[qob])
            if g == 4:
                flush()
            if nxt is not None:
                wb, wbb = nxt
    P.barrier()
    agb = Buf("ag")
    for j in range(2):
        P.allgather(T["KTown"][j], T["KTag"][j], writes=[agb])
        P.allgather(T["Vown"][j], T["Vag"][j], writes=[agb])
    P.end_phase()


def host_consts():
    return {
        "c_ident": np.eye(128, dtype=np.float32).astype(NPBF),
        "c_ones": np.ones((128, 128), np.float32).astype(NPBF),
        "c_onesf": np.ones((128, 128), np.float32),
    }


def rope_tables():
    t = np.arange(S)
    rows = (t // 64).astype(np.float32)
    cols = (t % 64).astype(np.float32)
    inv = (np.float32(10000.0) ** (-np.arange(0, 64, 2, dtype=np.float32) / np.float32(64))).astype(np.float32)
    ang = np.concatenate([rows[:, None] * inv[None, :], cols[:, None] * inv[None, :]], axis=-1)
    c = np.cos(ang).T.astype(np.float32)
    s = np.sin(ang).T.astype(np.float32)
    return np.concatenate([c, c], 0), np.concatenate([s, -s], 0)


PERM = np.concatenate([np.arange(0, 128, 2), np.arange(1, 128, 2)])


def emit_B(P, C, T):
    P.begin_phase()
    QT_d, SG_d, OG_d = T["QT"], T["SG"], T["OGT"]
    psS = Ring(P, "psS", [512], F32, 4, psum=True)
    psO = Ring(P, "psO", [512], F32, 2, psum=True)
    psL = Ring(P, "psL", [512], F32, 2, psum=True)
    kr = Ring(P, "kt", [S], BF16, 2)
    vr = Ring(P, "vt", [64, 128], BF16, 2)
    qr = Ring(P, "qt", [TOWN], BF16, 2)
    sgr = Ring(P, "sg", [TOWN], BF16, 2)
    pr = Ring(P, "p", [512], BF16, 6)
    rlr = Ring(P, "rl", [512], F32, 2)
    lnr = Ring(P, "lnl", [512], F32, 2)
    asr = Ring(P, "asum", [512], BF16, 2)
    aPr = Ring(P, "accP", [512], F32, 2)
    t01r = Ring(P, "t01", [512], BF16, 2)
    t23r = Ring(P, "t23", [512], BF16, 2)
    q4r = Ring(P, "q4", [512], BF16, 2)
    o1r = Ring(P, "o1", [512], F32, 2)
    ogr = Ring(P, "og", [512], BF16, 2)
    scale = 1.0 / math.sqrt(HD)
    NKB = S // 128
    pend = []

    def front(st):
        pS, pSb = psS.next()
        p, pb = pr.next()
        kt, ktb, qt_, qb, qi, kb = st["kt"], st["ktb"], st["qt"], st["qb"], st["qi"], st["kb"]
        P.pe(lambda e: e.matmul(pS[:], kt[:, kb * 128:(kb + 1) * 128], qt_[:, qi * 512:(qi + 1) * 512], start=True, stop=True),
             reads=[ktb, qb], writes=[pSb])
        P.act(lambda e: e.activation(out=p[:], in_=pS[:], func=AF.Exp, scale=scale), reads=[pSb], writes=[pb])
        st["p"], st["pb"] = p, pb

    def back(st):
        p, pb, vt, vtb, kb = st["p"], st["pb"], st["vt"], st["vtb"], st["kb"]
        pO, pOb = st["pO"], st["pOb"]
        P.pe(lambda e: e.matmul(pO[:], vt[:, kb, :], p[:], start=(kb == 0), stop=(kb == NKB - 1)), reads=[vtb, pb], writes=[pOb])
        acc, accb = st["accs"]
        r = kb % 4
        grp = st["grp"]
        if r == 0:
            grp.clear()
        grp.append((p, pb))
        if r == 1:
            t01, t01b = t01r.next()
            (pa, pab), (pc, pcb) = grp[0], grp[1]
            P.dve(lambda e: e.tensor_tensor(out=t01[:], in0=pa[:], in1=pc[:], op=ALU.add), reads=[pab, pcb], writes=[t01b])
            st["grp_t01"][:] = [(t01, t01b)]
        if r == 3:
            t23, t23b = t23r.next()
            q4, q4b = q4r.next()
            (pa, pab), (pc, pcb) = grp[2], grp[3]
            (t01, t01b) = st["grp_t01"][0]
            P.dve(lambda e: e.tensor_tensor(out=t23[:], in0=pa[:], in1=pc[:], op=ALU.add), reads=[pab, pcb], writes=[t23b])
            P.dve(lambda e: e.tensor_tensor(out=q4[:], in0=t01[:], in1=t23[:], op=ALU.add), reads=[t01b, t23b], writes=[q4b])
            if kb == 3:
                P.pool(lambda e: e.tensor_copy(acc[:], q4[:]), reads=[q4b], writes=[accb])
            else:
                P.pool(lambda e: e.tensor_tensor(out=acc[:], in0=acc[:], in1=q4[:], op=ALU.add), reads=[q4b, accb], writes=[accb])
        if kb == NKB - 1:
            h, qi, sg, sgb = st["h"], st["qi"], st["sg"], st["sgb"]
            pL, pLb = psL.next()
            asum, asumb = asr.next()
            ln, lnb = lnr.next()
            rl, rlb = rlr.next()
            o1, o1b = o1r.next()
            og, ogb = ogr.next()
            P.pool(lambda e: e.tensor_copy(asum[:], acc[:]), reads=[accb], writes=[asumb])
            P.pe(lambda e: e.matmul(pL[:], C["ones"][:], asum[:], start=True, stop=True), reads=[C["buf"], asumb], writes=[pLb])
            P.act(lambda e: e.activation(out=ln[:], in_=pL[:], func=AF.Ln), reads=[pLb], writes=[lnb])
            P.act(lambda e: e.activation(out=rl[:], in_=ln[:], func=AF.Exp, scale=-1.0), reads=[lnb], writes=[rlb])
            P.dve(lambda e: e.tensor_tensor(out=o1[:], in0=pO[:], in1=rl[:], op=ALU.mult), reads=[pOb, rlb], writes=[o1b])
            P.pool(lambda e: e.tensor_tensor(out=og[:], in0=o1[:], in1=sg[:, qi * 512:(qi + 1) * 512], op=ALU.mult), reads=[o1b, sgb], writes=[ogb])
            P.dma(OG_d[h, :, qi * 512:(qi + 1) * 512], og[:], reads=[ogb])

    for g in range(4):
        kt, ktb = kr.next()
        vt, vtb = vr.next()
        P.dma(kt[:].rearrange("p (r t) -> p r t", r=2),
              T["KTag"][g // 2].rearrange("(r k p) t -> p k r t", r=2, k=2)[:, g % 2], writes=[ktb])
        for j in range(2):
            for r in range(2):
                P.dma(vt[:, r * 32 + j * 16:r * 32 + j * 16 + 16, :],
                      T["Vag"][j][r * 2048:(r + 1) * 2048, g * 128:(g + 1) * 128].rearrange("(kk p) d -> p kk d", p=128), writes=[vtb])
        for hh in range(4):
            h = 4 * g + hh
            qt_, qb = qr.next()
            sg, sgb = sgr.next()
            P.dma(qt_[:], QT_d[h], writes=[qb])
            P.dma(sg[:], SG_d[h], writes=[sgb])
            for qi in range(TOWN // 512):
                pO, pOb = psO.next()
                accs = aPr.next()
                grp, grp_t01 = [], []
                for kb in range(NKB):
                    st = dict(kt=kt, ktb=ktb, vt=vt, vtb=vtb, qt=qt_, qb=qb, sg=sg, sgb=sgb, h=h, qi=qi, kb=kb,
                              pO=pO, pOb=pOb, accs=accs, grp=grp, grp_t01=grp_t01)
                    front(st)
                    pend.append(st)
                    if len(pend) > 2:
                        back(pend.pop(0))
    while pend:
        back(pend.pop(0))
    P.end_phase()


def emit_CF(P, C, T, final):
    P.begin_phase()
    if final:
        w_d, x_d, gb_d = T["w_out1"], T["x1"], T["gate_bc1"]
        sg_d, fg_d, out_d = T["SG1"], T["final_g"], T["out"]
        xv = x_d.rearrange("(a b) m -> b a m", b=64)
        ov = out_d.rearrange("(a b) m -> b a m", b=64)
        sel = P.sb("sel", [128, 2], F32)
        selb = Buf("sel")
        P.dma(sel[:], T["sel"], writes=[selb])
    else:
        w_d, x_d, gb_d = T["w_out0"], T["x"], T["gate_bc0"]
        OG_d, out_d = T["OGT"], T["x1"]
    psY = Ring(P, "psY", [512], F32, 6 if final else 8, psum=True)
    wbf = P.sb("wbf", [128, NCH, D], BF16)
    wbb = Buf("wbf")
    gbc = P.sb("gbc", [128, D], F32)
    gbb = Buf("gbc")
    P.dma(gbc[:], gb_d, writes=[gbb])
    wfr = Ring(P, "wf", [4, 512], F32, 3)
    wv = w_d.rearrange("(j p) n -> p j n", p=128)
    wbn = [Buf(f"wbf{n}") for n in range(4)]
    for n in range(4):
        gsl = gbc[:, n * 512:(n + 1) * 512].unsqueeze(1).to_broadcast([128, 4, 512])
        for jq in range(4):
            wf, wfb = wfr.next()
            P.dma(wf[:], wv[:, 4 * jq:4 * jq + 4, n * 512:(n + 1) * 512], writes=[wfb])
            dst = wbf[:, 4 * jq:4 * jq + 4, n * 512:(n + 1) * 512]
            P.dve(lambda e, wf=wf, dst=dst, gsl=gsl: e.tensor_tensor(out=dst, in0=wf[:], in1=gsl, op=ALU.mult), reads=[wfb, gbb], writes=[wbn[n]])
    xr = Ring(P, "xt", [D], F32, 3)
    if final:
        psT = Ring(P, "psT", [8, 128], BF16, 2, psum=True)
        fgb_t = P.sb("fgbc", [128, D], F32)
        fgb = Buf("fg")
        P.dma(fgb_t[:], fg_d.partition_broadcast(128).rearrange("p o n -> p (o n)"), writes=[fgb])
        fr = Ring(P, "ft", [D], BF16, 2)
        b0r = Ring(P, "b0", [1024], BF16, 2)
        b1r = Ring(P, "b1", [1024], BF16, 2)
        sgr = Ring(P, "sgt", [D], BF16, 2)
        obr = Ring(P, "ogb", [D], BF16, 2)
        otr = Ring(P, "ogT", [NCH, 128], BF16, 2)
        ssr = Ring(P, "ss", [1], F32, 2)
        sdr = Ring(P, "sd", [1], F32, 2)
        jr = Ring(P, "junk", [D], BF16, 1)
    else:
        ogr = Ring(P, "og", [NCH, 512], BF16, 2)
        OGv = OG_d.rearrange("c p t -> p c t")
    cur_og = [None]

    def front(tt):
        r0 = tt * 128
        st = dict(tt=tt)
        if final:
            ft, ftb = fr.next()
            sg, sgb = sgr.next()
            ob, obb = obr.next()
            oT, oTb = otr.next()
            b0, b0b = b0r.next()
            b1, b1b = b1r.next()
            fob = Buf("fo")
            P.dma(ft[:, 0:1024], T["f_own"][r0:r0 + 128, :], writes=[fob])
            fj, fr0 = tt // 8, (tt % 8) * 128
            P.dma(b0[:], T["fag"][fj][fr0:fr0 + 128, :], writes=[b0b])
            P.dma(b1[:], T["fag"][fj][1024 + fr0:1024 + fr0 + 128, :], writes=[b1b])
            P.dma(sg[:], sg_d[r0:r0 + 128, :], writes=[sgb])
            P.dve(lambda e: e.tensor_scalar(out=b1[:], in0=b1[:], scalar1=sel[:, 1:2], scalar2=None, op0=ALU.mult), reads=[b1b, selb], writes=[b1b])
            P.dve(lambda e: e.scalar_tensor_tensor(out=ft[:, 1024:2048], in0=b0[:], scalar=sel[:, 0:1], in1=b1[:], op0=ALU.mult, op1=ALU.add),
                  reads=[b0b, b1b, selb, fob], writes=[ftb])
            P.dve(lambda e: e.tensor_tensor(out=ob[:], in0=ft[:], in1=sg[:], op=ALU.mult), reads=[ftb, fob, sgb], writes=[obb])
            for half in range(2):
                pt, ptb = psT.next()
                for c in range(8):
                    ch = half * 8 + c
                    P.pe(lambda e, pt=pt, c=c, ch=ch: e.transpose(pt[:, c, :], ob[:, ch * 128:(ch + 1) * 128], C["ident"][:]),
                         reads=[obb, C["buf"]], writes=[ptb])
                P.act(lambda e, pt=pt, half=half: e.activation(out=oT[:, half * 8:half * 8 + 8, :], in_=pt[:], func=AF.Copy), reads=[ptb], writes=[oTb])
            st["lhs"] = lambda c: oT[:, c, :]
            st["lb"] = oTb
        else:
            if tt % 4 == 0:
                og, ogb = ogr.next()
                P.dma(og[:], OGv[:, :, r0:r0 + 512], writes=[ogb])
                cur_og[0] = (og, ogb)
            og, ogb = cur_og[0]
            ti = tt % 4
            st["lhs"] = lambda c: og[:, c, ti * 128:(ti + 1) * 128]
            st["lb"] = ogb
        xt, xb = xr.next()
        if final:
            P.dma(xt[0:64, :], xv[2 * tt], writes=[xb])
            P.dma(xt[64:128, :], xv[2 * tt + 1], writes=[xb])
        else:
            P.dma(xt[:], x_d[r0:r0 + 128, :], writes=[xb])
        st["xt"], st["xb"] = xt, xb
        return st

    def mm(st):
        lhs, lb = st["lhs"], st["lb"]
        st["ps"] = []
        for n in range(4):
            ps, psb = psY.next()
            for c in range(NCH):
                P.pe(lambda e, ps=ps, c=c, n=n: e.matmul(ps[:], lhs(c), wbf[:, c, n * 512:(n + 1) * 512], start=(c == 0), stop=(c == NCH - 1)),
                     reads=[lb, wbn[n]], writes=[psb])
            st["ps"].append((ps, psb))

    def back(st):
        tt, xt, xb = st["tt"], st["xt"], st["xb"]
        r0 = tt * 128
        for n, (ps, psb) in enumerate(st["ps"]):
            P.dve(lambda e, ps=ps, n=n: e.tensor_tensor(out=xt[:, n * 512:(n + 1) * 512], in0=ps[:], in1=xt[:, n * 512:(n + 1) * 512], op=ALU.add),
                  reads=[psb, xb], writes=[xb])
        if final:
            ss, ssb = ssr.next()
            sd, sdb = sdr.next()
            jk, jkb = jr.next()
            P.pool(lambda e: e.memset(ss[:], 0.0), writes=[ssb])
            P.act(lambda e: e.activation(out=jk[:], in_=xt[:], func=AF.Square, accum_out=ss[:]), reads=[xb], writes=[jkb, ssb])
            P.act(lambda e: e.activation(out=sd[:], in_=ss[:], func=AF.Sqrt, scale=1.0 / D, bias=C["eps"][:, 0:1]),
                  reads=[ssb, C["epsb"]], writes=[sdb])
            P.dve(lambda e: e.reciprocal(out=sd[:], in_=sd[:]), reads=[sdb], writes=[sdb])
            P.dve(lambda e: e.scalar_tensor_tensor(out=xt[:], in0=xt[:], scalar=sd[:, 0:1], in1=fgb_t[:], op0=ALU.mult, op1=ALU.mult),
                  reads=[xb, sdb, fgb], writes=[xb])
            P.dma(ov[2 * tt], xt[0:64, :], reads=[xb])
            P.dma(ov[2 * tt + 1], xt[64:128, :], reads=[xb])
        else:
            P.dma(out_d[r0:r0 + 128, :], xt[:], reads=[xb])

    NT = TOWN // 128
    prev = front(0)
    mm(prev)
    for tt in range(1, NT):
        cur = front(tt)
        back(prev)
        mm(cur)
        prev = cur
    back(prev)
    P.end_phase()


def emit_D(P, C, T):
    P.begin_phase()
    x_d, cvec_d, g_d, adaw_d, adab_d, win_d = T["x1"], T["cvec"], T["norm_g1"], T["ada_w1"], T["ada_b1"], T["w_in1"]
    SG_d, GB_d = T["SG1"], T["gate_bc1"]
    psA = Ring(P, "psA", [512], F32, 4, psum=True)
    psT = Ring(P, "psT", [8, 128], BF16, 2, psum=True)
    modp = P.sb("modp", [128, 2 * D], F32)
    modb = Buf("modp")
    with ExitStack() as es:
        gtmp = P.sb("gtmp", [128, D], F32, es)
        gbc = P.sb("gbc", [128, D], F32, es)
        gtb, gbb = Buf(), Buf()
        P.dma(gbc[:], g_d.partition_broadcast(128).rearrange("p o n -> p (o n)"), writes=[gbb])

        def dst_fn(nt):
            if nt < 8:
                return modp[:, nt * 512:(nt + 1) * 512], modb
            return gtmp[:, (nt - 8) * 512:(nt - 7) * 512], gtb
        emit_mod(P, C, cvec_d, adaw_d, adab_d, psA, dst_fn)
        P.dma(GB_d, gtmp[:], reads=[gtb])
        P.dve(lambda e: e.scalar_tensor_tensor(out=modp[:, D:2 * D], in0=modp[:, D:2 * D], scalar=1.0, in1=gbc[:], op0=ALU.add, op1=ALU.mult),
              reads=[modb, gbb], writes=[modb])
        P.barrier()
    hb_ = HBuilder(P, C, modp[:, 0:D], modp[:, D:2 * D], modb, psT)
    TH = 2048
    hT = P.sb("hT", [128, NCH, TH], BF16)
    hTb = (Buf("hTa"), Buf("hTd"))
    W = WStream(P, win_d)
    uor = Ring(P, "uo", [512], BF16, 3)
    gor = Ring(P, "go", [512], BF16, 3)
    xv = x_d.rearrange("(a b) m -> b a m", b=64)
    NG = 8
    for th in range(2):
        tok0 = th * TH
        W.dma(0)
        for tt in range(TH // 128):
            t2a = (tok0 + tt * 128) // 64
            hb_.tile([(0, 64, xv[t2a]), (64, 128, xv[t2a + 1])], hT, hTb, tt * 128)
        wb, wbb = W.casts(0)
        for g in range(NG):
            if g + 1 < NG:
                W.dma(g + 1)
            nxt = None
            if g < 4:
                for ci in range(4):
                    cc = 4 * g + ci
                    if ci == 2 and g + 1 < NG:
                        nxt = W.casts(g + 1)
                    for t5 in range(TH // 512):
                        pst, psb = psA.next()
                        for j in range(NCH):
                            P.pe(lambda e, pst=pst, j=j, t5=t5, wb=wb, ci=ci: e.matmul(pst[:], wb[:, j, ci * 128:(ci + 1) * 128], hT[:, j, t5 * 512:(t5 + 1) * 512], start=(j == 0), stop=(j == NCH - 1)),
                                 reads=[hTb[0], hTb[1], wbb], writes=[psb])
                        uo, uob = uor.next()
                        if t5 % 2 == 0:
                            P.act(lambda e, uo=uo, pst=pst: e.activation(out=uo[:], in_=pst[:], func=AF.Copy), reads=[psb], writes=[uob])
                        else:
                            P.dve(lambda e, uo=uo, pst=pst: e.tensor_copy(uo[:], pst[:]), reads=[psb], writes=[uob])
                        if cc < 8:
                            udst = T["UTown"][cc, :, tok0 + t5 * 512: tok0 + (t5 + 1) * 512]
                        else:
                            udst = T["UTsend"][(cc - 8) // 2][((cc - 8) % 2) * 128:((cc - 8) % 2) * 128 + 128, tok0 + t5 * 512: tok0 + (t5 + 1) * 512]
                        P.dma(udst, uo[:], reads=[uob])
            else:
                for tt in range(TH // 128):
                    if tt == 8 and g + 1 < NG:
                        nxt = W.casts(g + 1)
                    pst, psb = psA.next()
                    for j in range(NCH):
                        P.pe(lambda e, pst=pst, j=j, tt=tt, wb=wb: e.matmul(pst[:], hT[:, j, tt * 128:(tt + 1) * 128], wb[:, j, :], start=(j == 0), stop=(j == NCH - 1)),
                             reads=[hTb[0], hTb[1], wbb], writes=[psb])
                    go, gob = gor.next()
                    P.act(lambda e, go=go, pst=pst: e.activation(out=go[:], in_=pst[:], func=AF.Silu), reads=[psb], writes=[gob])
                    r0 = tok0 + tt * 128
                    P.dma(SG_d[r0:r0 + 128, (g - 4) * 512:(g - 3) * 512], go[:], reads=[gob])
            if nxt is not None:
                wb, wbb = nxt
    P.barrier()
    agb = Buf("ag")
    for j in range(4):
        P.allgather(T["UTsend"][j], T["UTag"][j], writes=[agb])
    P.end_phase()


def fourier_tables():
    w = np.arange(256)
    aw = 2 * np.pi * np.outer(w, w) / 256
    csw = np.concatenate([np.cos(aw), np.sin(aw)], 1) / 16.0
    csw = csw.reshape(2, 128, 512).transpose(1, 0, 2)
    t1 = np.arange(128)
    a1 = 2 * np.pi * np.outer(t1, t1) / 128
    nrm = 1.0 / math.sqrt(8192.0)
    ra = np.concatenate([np.cos(a1), np.sin(a1)], 1) * nrm
    rb = np.concatenate([-np.sin(a1), np.cos(a1)], 1) * nrm
    m = np.arange(128)
    t2 = m % 64
    c2 = m // 64
    atw = 2 * np.pi * np.outer(t2, np.arange(128)) / 8192
    ct, st = np.cos(atw), np.sin(atw)
    n = np.arange(128)
    c2n, k2 = n // 64, n % 64
    a2 = 2 * np.pi * np.outer(t2, k2) / 64
    delta = (c2[:, None] == c2n[None, :]).astype(np.float64)
    re = delta * np.cos(a2)
    rf = -delta * np.sin(a2)
    return {
        "t_csw": np.ascontiguousarray(csw).astype(np.float32).astype(NPBF),
        "t_ra": ra.astype(np.float32).astype(NPBF),
        "t_rb": rb.astype(np.float32).astype(NPBF),
        "t_ct": ct.astype(np.float32).astype(NPBF),
        "t_st": st.astype(np.float32).astype(NPBF),
        "t_re": re.astype(np.float32).astype(NPBF),
        "t_rf": rf.astype(np.float32).astype(NPBF),
    }


def emit_E(P, C, T):
    P.begin_phase()
    csw_d, ra_d, rb_d, ct_d, st_d, re_d, rf_d = T["t_csw"], T["t_ra"], T["t_rb"], T["t_ct"], T["t_st"], T["t_re"], T["t_rf"]
    sel = P.sb("sel", [128, 2], F32)
    selb = Buf("sel")
    P.dma(sel[:], T["sel"], writes=[selb])
    csw = P.sb("csw", [128, 2, 512], BF16)
    ra = P.sb("ra", [128, 256], BF16)
    rb = P.sb("rb", [128, 256], BF16)
    ct = P.sb("ct", [128, 128], BF16)
    st = P.sb("st", [128, 128], BF16)
    re = P.sb("re", [128, 128], BF16)
    rf = P.sb("rf", [128, 128], BF16)
    tb = Buf("tables")
    for dst, src in ((csw, csw_d), (ra, ra_d), (rb, rb_d), (ct, ct_d), (st, st_d), (re, re_d), (rf, rf_d)):
        P.dma(dst[:], src, writes=[tb])
    ps1 = Ring(P, "ps1", [512], F32, 2, psum=True)
    ps2 = Ring(P, "ps2", [512], F32, 2, psum=True)
    ps3 = Ring(P, "ps3", [512], F32, 2, psum=True)
    AB = P.sb("AB", [128, 512, 64], BF16)
    ABa, ABd = Buf("ABa"), Buf("ABd")
    G2 = P.sb("G2", [128, 64, 256], BF16)
    G2b = Buf("G2")
    fS = P.sb("fS", [128, 64, 128], BF16)
    fSa, fSd = Buf("fSa"), Buf("fSd")
    ur = Ring(P, "u", [2, 8, 128], BF16, 3)
    upb_ = [Buf("up0"), Buf("up1"), Buf("up2")]
    b0r = Ring(P, "ub0", [2, 512], BF16, 2)
    b1r = Ring(P, "ub1", [2, 512], BF16, 2)
    efr = Ring(P, "ef", [512], BF16, 3)
    m1r = Ring(P, "m1", [2, 128], BF16, 3)
    m2r = Ring(P, "m2", [2, 128], BF16, 3)
    m3r = Ring(P, "m3", [2, 128], BF16, 3)
    m4r = Ring(P, "m4", [2, 128], BF16, 3)
    fov = T["f_own"].rearrange("(t a e) c -> e t a c", t=64, a=32, e=2)
    fsv = [T["fsend"][j].rearrange("(t a e) c -> e t a c", t=16, a=32, e=2) for j in range(4)]
    uk = 0
    ctb = ct[:].unsqueeze(1).to_broadcast([128, 2, 128])
    stb = st[:].unsqueeze(1).to_broadcast([128, 2, 128])
    k = 0
    for g in range(4):
        for t8 in range(8):
            u, ub = ur.next()
            upb = upb_[uk % 3]
            uk += 1
            b0, b0b = b0r.next()
            b1, b1b = b1r.next()
            c0 = t8 * 512
            for kc in range(2):
                P.dma(u[:, kc, :, 0:64], T["UTown"][2 * g + kc, :, c0:c0 + 512].rearrange("p (i t) -> p i t", t=64), writes=[ub])
            agv = T["UTag"][g].rearrange("(r k p) t -> p r k t", r=2, k=2)
            P.dma(b0[:], agv[:, 0, :, c0:c0 + 512], writes=[b0b])
            P.dma(b1[:], agv[:, 1, :, c0:c0 + 512], writes=[b1b])
            P.dve(lambda e, b1=b1: e.tensor_scalar(out=b1[:], in0=b1[:], scalar1=sel[:, 1:2], scalar2=None, op0=ALU.mult), reads=[b1b, selb], writes=[b1b])
            P.dve(lambda e, u=u, b0=b0, b1=b1: e.scalar_tensor_tensor(out=u[:, :, :, 64:128], in0=b0[:].rearrange("p k (i t) -> p k i t", t=64), scalar=sel[:, 0:1],
                                                                     in1=b1[:].rearrange("p k (i t) -> p k i t", t=64), op0=ALU.mult, op1=ALU.add),
                  reads=[b0b, b1b, selb], writes=[upb])
            for ti in range(8):
                t2 = t8 * 8 + ti
                ps, psb = ps1.next()
                for kc in range(2):
                    P.pe(lambda e, ps=ps, u=u, kc=kc, ti=ti: e.matmul(ps[:], u[:, kc, ti, :], csw[:, kc, :], start=(kc == 0), stop=(kc == 1)),
                         reads=[ub, upb, tb], writes=[psb])
                k += 1
                if k % 3 != 0:
                    P.act(lambda e, ps=ps, t2=t2: e.activation(out=AB[:, :, t2], in_=ps[:], func=AF.Copy), reads=[psb], writes=[ABa])
                else:
                    P.dve(lambda e, ps=ps, t2=t2: e.tensor_copy(AB[:, :, t2], ps[:]), reads=[psb], writes=[ABd])
        for half in range(2):
            for pb in range(32):
                ps, psb = ps2.next()
                for pi in range(2):
                    pl = 2 * pb + pi
                    ch = half * 128 + 2 * pl
                    P.pe(lambda e, ps=ps, pi=pi, ch=ch: e.matmul(ps[:, pi * 256:(pi + 1) * 256], AB[:, ch:ch + 2, :].rearrange("p a b -> p (a b)"), ra[:], start=True, stop=False),
                         reads=[ABa, ABd, tb], writes=[psb])
                    P.pe(lambda e, ps=ps, pi=pi, ch=ch: e.matmul(ps[:, pi * 256:(pi + 1) * 256], AB[:, 256 + ch:256 + ch + 2, :].rearrange("p a b -> p (a b)"), rb[:], start=False, stop=True),
                         reads=[ABa, ABd, tb], writes=[psb])
                ef, efb = efr.next()
                P.act(lambda e, ef=ef, ps=ps: e.activation(out=ef[:], in_=ps[:], func=AF.Copy), reads=[psb], writes=[efb])
                efv = ef[:].rearrange("p (a b) -> p a b", a=2)
                Ev, Fv = efv[:, :, 0:128], efv[:, :, 128:256]
                m1, m1b = m1r.next()
                m2, m2b = m2r.next()
                m3, m3b = m3r.next()
                m4, m4b = m4r.next()
                P.pool(lambda e, m1=m1, Ev=Ev: e.tensor_tensor(out=m1[:], in0=Ev, in1=ctb, op=ALU.mult), reads=[efb, tb], writes=[m1b])
                P.pool(lambda e, m2=m2, Fv=Fv: e.tensor_tensor(out=m2[:], in0=Fv, in1=stb, op=ALU.mult), reads=[efb, tb], writes=[m2b])
                P.dve(lambda e, m3=m3, Ev=Ev: e.tensor_tensor(out=m3[:], in0=Ev, in1=stb, op=ALU.mult), reads=[efb, tb], writes=[m3b])
                P.pool(lambda e, m4=m4, Fv=Fv: e.tensor_tensor(out=m4[:], in0=Fv, in1=ctb, op=ALU.mult), reads=[efb, tb], writes=[m4b])
                P.dve(lambda e, m1=m1, m2=m2, pb=pb: e.tensor_tensor(out=G2[:, 2 * pb:2 * pb + 2, 0:128], in0=m1[:], in1=m2[:], op=ALU.subtract),
                      reads=[m1b, m2b], writes=[G2b])
                P.dve(lambda e, m3=m3, m4=m4, pb=pb: e.tensor_tensor(out=G2[:, 2 * pb:2 * pb + 2, 128:256], in0=m3[:], in1=m4[:], op=ALU.add),
                      reads=[m3b, m4b], writes=[G2b])
            for qb in range(16):
                ps, psb = ps3.next()
                for pi in range(4):
                    pl = 4 * qb + pi
                    P.pe(lambda e, ps=ps, pi=pi, pl=pl: e.matmul(ps[:, pi * 128:(pi + 1) * 128], G2[:, pl, 0:128], re[:], start=True, stop=False),
                         reads=[G2b, tb], writes=[psb])
                    P.pe(lambda e, ps=ps, pi=pi, pl=pl: e.matmul(ps[:, pi * 128:(pi + 1) * 128], G2[:, pl, 128:256], rf[:], start=False, stop=True),
                         reads=[G2b, tb], writes=[psb])
                dst = fS[:, :, 8 * qb:8 * qb + 8].rearrange("p k (a c) -> p a c k", a=4, c=2)
                src_fn = lambda ps=ps: ps[:].rearrange("p (a c k) -> p a c k", a=4, c=2)
                if qb % 2 == 0:
                    P.act(lambda e, dst=dst, src_fn=src_fn: e.activation(out=dst, in_=src_fn(), func=AF.Copy), reads=[psb], writes=[fSa])
                else:
                    P.dve(lambda e, dst=dst, src_fn=src_fn: e.tensor_copy(dst, src_fn()), reads=[psb], writes=[fSd])
            c0 = g * 256 + half * 128
            for e_ in range(2):
                P.dma(fov[e_][:, :, c0:c0 + 128], fS[e_ * 64:(e_ + 1) * 64, 0:32, :], reads=[fSa, fSd])
                for j in range(4):
                    P.dma(fsv[j][e_][:, :, c0:c0 + 128], fS[e_ * 64 + 16 * j:e_ * 64 + 16 * j + 16, 32:64, :], reads=[fSa, fSd])
    P.barrier()
    agb = Buf("ag")
    for j in range(4):
        P.allgather(T["fsend"][j], T["fag"][j], writes=[agb])
    P.end_phase()


def assemble_UTf(ut0, ut1, hf):
    a = np.asarray(ut0)[8 * hf:8 * hf + 8].reshape(8, 128, 64, 64)
    b = np.asarray(ut1)[8 * hf:8 * hf + 8].reshape(8, 128, 64, 64)
    return np.ascontiguousarray(np.concatenate([a, b], axis=3).reshape(8, 128, S))


def build_fused():
    P = Prog()
    T = {}
    I32 = mybir.dt.int32
    T["x"] = P.din("x", [TOWN, D], F32)
    T["cvec"] = P.din("cvec", [128, NCH], F32)
    T["norm_g0"] = P.din("norm_g0", [1, D], F32)
    T["norm_g1"] = P.din("norm_g1", [1, D], F32)
    T["ada_w0"] = P.din("ada_w0", [D, 3 * D], F32)
    T["ada_w1"] = P.din("ada_w1", [D, 3 * D], F32)
    T["ada_b0"] = P.din("ada_b0", [1, 3 * D], F32)
    T["ada_b1"] = P.din("ada_b1", [1, 3 * D], F32)
    T["w_in0"] = P.din("w_in0", [D, 5120], F32)
    T["qg"] = P.din("qg", [128, 1], F32)
    T["kg"] = P.din("kg", [128, 1], F32)
    T["ropeC"] = P.din("ropeC", [128, TOWN], F32)
    T["ropeS"] = P.din("ropeS", [128, TOWN], F32)
    T["w_out0"] = P.din("w_out0", [D, D], F32)
    T["w_in1"] = P.din("w_in1", [D, 4096], F32)
    T["w_out1"] = P.din("w_out1", [D, D], F32)
    T["final_g"] = P.din("final_g", [1, D], F32)
    T["sel"] = P.din("sel", [128, 2], F32)
    T["t_csw"] = P.din("t_csw", [128, 2, 512], BF16)
    T["t_ra"] = P.din("t_ra", [128, 256], BF16)
    T["t_rb"] = P.din("t_rb", [128, 256], BF16)
    T["t_ct"] = P.din("t_ct", [128, 128], BF16)
    T["t_st"] = P.din("t_st", [128, 128], BF16)
    T["t_re"] = P.din("t_re", [128, 128], BF16)
    T["t_rf"] = P.din("t_rf", [128, 128], BF16)
    T["out"] = P.dout("out", [TOWN, D], F32)
    T["QT"] = P.dint("QT", [16, 128, TOWN], BF16)
    T["SG"] = P.dint("SG", [16, 128, TOWN], BF16)
    T["gate_bc0"] = P.dint("gate_bc0", [128, D], F32)
    T["gate_bc1"] = P.dint("gate_bc1", [128, D], F32)
    T["KTown"] = [P.dint(f"KTown{j}", [256, TOWN], BF16) for j in range(2)]
    T["KTag"] = [P.dint(f"KTag{j}", [512, TOWN], BF16) for j in range(2)]
    T["Vown"] = [P.dint(f"Vown{j}", [2048, 512], BF16) for j in range(2)]
    T["Vag"] = [P.dint(f"Vag{j}", [4096, 512], BF16) for j in range(2)]
    T["OGT"] = P.dint("OGT", [16, 128, TOWN], BF16)
    T["x1"] = P.dint("x1", [TOWN, D], F32)
    T["UTown"] = P.dint("UTown", [8, 128, TOWN], BF16)
    T["UTsend"] = [P.dint(f"UTsend{j}", [256, TOWN], BF16) for j in range(4)]
    T["UTag"] = [P.dint(f"UTag{j}", [512, TOWN], BF16) for j in range(4)]
    T["SG1"] = P.dint("SG1", [TOWN, D], BF16)
    T["f_own"] = P.dint("f_own", [TOWN, 1024], BF16)
    T["fsend"] = [P.dint(f"fsend{j}", [1024, 1024], BF16) for j in range(4)]
    T["fag"] = [P.dint(f"fag{j}", [2048, 1024], BF16) for j in range(4)]
    C = emit_consts(P)
    emit_A(P, C, T)
    emit_B(P, C, T)
    emit_CF(P, C, T, False)
    emit_D(P, C, T)
    emit_E(P, C, T)
    emit_CF(P, C, T, True)
    return P.finish()


def core_tables(hf):
    tb = fourier_tables()
    q = (np.arange(128) + 64 * hf) % 128
    tb["t_ra"] = np.ascontiguousarray(tb["t_ra"][q])
    tb["t_rb"] = np.ascontiguousarray(tb["t_rb"][q])
    n = np.arange(128)
    k2 = ((n % 64) + 32 * hf) % 64
    col = (n // 64) * 64 + k2
    tb["t_re"] = np.ascontiguousarray(tb["t_re"][:, col])
    tb["t_rf"] = np.ascontiguousarray(tb["t_rf"][:, col])
    return tb


def kernel(x, c, norm_g, ada_w, ada_b, attn_w_in, attn_q_gain, attn_k_gain,
           attn_w_out, fourier_w_in, fourier_w_out, final_g):
    x = np.asarray(x)
    c = np.asarray(c)
    norm_g = np.asarray(norm_g)
    ada_w = np.asarray(ada_w)
    ada_b = np.asarray(ada_b)
    w_in = np.asarray(attn_w_in)[0]
    fw_in = np.asarray(fourier_w_in)[0]
    fw_out = np.asarray(fourier_w_out)[0]
    ropeC, ropeS = rope_tables()
    colperm = np.arange(5120)
    for h in range(20):
        colperm[h * 128:(h + 1) * 128] = h * 128 + PERM
    w_in_p = np.ascontiguousarray(w_in[:, colperm])
    consts = host_consts()
    qg = np.ascontiguousarray(np.asarray(attn_q_gain)[0][PERM].reshape(128, 1))
    kg = np.ascontiguousarray(np.asarray(attn_k_gain)[0][PERM].reshape(128, 1))
    per_hf = []
    for hf in range(2):
        ch = np.concatenate([np.arange(1024 * hf, 1024 * hf + 1024), np.arange(1024 * (1 - hf), 1024 * (1 - hf) + 1024)])
        d = dict(core_tables(hf))
        d["w_in1"] = np.ascontiguousarray(np.concatenate([fw_in[:, ch], fw_in[:, 2048 + ch]], axis=1))
        d["w_out1"] = np.ascontiguousarray(fw_out[ch, :])
        d["sel"] = np.ascontiguousarray(np.broadcast_to(np.array([float(hf), float(1 - hf)], np.float32), (128, 2)))
        d["ropeC"] = np.ascontiguousarray(ropeC[:, hf * TOWN:(hf + 1) * TOWN])
        d["ropeS"] = np.ascontiguousarray(ropeS[:, hf * TOWN:(hf + 1) * TOWN])
        per_hf.append(d)
    maps = []
    for core in range(NCORES):
        b, hf = core % 4, core // 4
        m = dict(consts)
        m.update(per_hf[hf])
        m["x"] = np.ascontiguousarray(x[b, hf * TOWN:(hf + 1) * TOWN])
        m["cvec"] = np.ascontiguousarray(c[b].reshape(NCH, 128).T)
        m["norm_g0"] = np.ascontiguousarray(norm_g[0:1])
        m["norm_g1"] = np.ascontiguousarray(norm_g[1:2])
        m["ada_w0"] = ada_w[0]
        m["ada_w1"] = ada_w[1]
        m["ada_b0"] = np.ascontiguousarray(ada_b[0:1])
        m["ada_b1"] = np.ascontiguousarray(ada_b[1:2])
        m["w_in0"] = w_in_p
        m["qg"] = qg
        m["kg"] = kg
        m["w_out0"] = np.asarray(attn_w_out)[0]
        m["final_g"] = np.ascontiguousarray(np.asarray(final_g).reshape(1, D))
        maps.append(m)
    res = run_bass_kernel_spmd(build_fused(), maps, core_ids=list(range(NCORES))).results
    out = np.empty((NB, S, D), np.float32)
    for core in range(NCORES):
        b, hf = core % 4, core // 4
        out[b, hf * TOWN:(hf + 1) * TOWN] = res[core]["out"]
    return out
```

```python
import math
from contextlib import ExitStack
import numpy as np
import ml_dtypes
import concourse.bass as bass
import concourse.mybir as mybir
from concourse.bass_utils import run_bass_kernel_spmd

F32 = mybir.dt.float32
BF16 = mybir.dt.bfloat16
AF = mybir.ActivationFunctionType
ALU = mybir.AluOpType
NPBF = ml_dtypes.bfloat16

D = 2048
S = 8192
NB = 4
NCORES = 8
TOWN = 4096
HD = 128
EPS = 1e-6
NCH = D // 128
GROUPS = [[0, 4], [1, 5], [2, 6], [3, 7]]


class Buf:
    __slots__ = ("name", "w", "r", "lsem", "ssem")

    def __init__(self, name=""):
        self.name = name
        self.w = None
        self.r = []
        self.lsem = None
        self.ssem = None


class Prog:
    ENG = ("pe", "act", "dve", "pool", "sp")

    def __init__(self):
        self.nc = bass.Bass("TRN2", target_bir_lowering=False)
        self.es = ExitStack()
        self.ops = {e: [] for e in self.ENG}
        self.N = {e: 0 for e in self.ENG}
        self.S = {e: self.es.enter_context(self.nc.semaphore("cnt_" + e)) for e in self.ENG}
        self.waited = {e: {} for e in self.ENG}
        self.semval = {}
        self.nsem = 5
        self.uid = 0
        self.dram = {}
        self.scope = self.es
        self.phase_sems = []
        self.free_sems = []
        self.ccsems = {}

    def name(self, base):
        self.uid += 1
        return f"{base}_{self.uid}"

    def sb(self, name, shape, dt, es=None):
        return (es or self.scope).enter_context(self.nc.sbuf_tensor(self.name(name), list(shape), dt))

    def ps(self, name, shape, dt, es=None):
        return (es or self.scope).enter_context(self.nc.psum_tensor(self.name(name), list(shape), dt))

    def newsem(self, name):
        if self.free_sems:
            s = self.free_sems.pop()
        else:
            self.nsem += 1
            s = self.es.enter_context(self.nc.semaphore(self.name(name)))
            self.semval[s] = 0
        self.phase_sems.append(s)
        return s

    def begin_phase(self):
        self.scope = ExitStack()
        self.phase_sems = []

    def end_phase(self):
        self.barrier()
        self.scope.close()
        self.scope = self.es
        self.free_sems.extend(self.phase_sems)
        self.phase_sems = []

    def dint(self, name, shape, dt):
        return self.nc.dram_tensor(name, list(shape), dt).ap()

    def allgather(self, in_ap, out_ap, reads=(), writes=()):
        q = "pool"
        self._deps(q, reads, writes)
        self.nsem += 1
        sem = self.es.enter_context(self.nc.semaphore(self.name("cc")))
        self.ccsems[sem] = 1
        self.ops[q].append(lambda e: e.collective_compute("AllGather", ALU.bypass, replica_groups=GROUPS,
                                                          ins=[in_ap.opt()], outs=[out_ap.opt()]).then_inc(sem, 1))
        tok = ("dma", sem, 1)
        for b in reads:
            b.r.append(tok)
        for b in writes:
            b.w = tok
            b.r = []
        return tok

    def din(self, name, shape, dt):
        t = self.nc.dram_tensor(name, list(shape), dt, kind="ExternalInput").ap()
        self.dram[name] = t
        return t

    def dout(self, name, shape, dt):
        t = self.nc.dram_tensor(name, list(shape), dt, kind="ExternalOutput").ap()
        self.dram[name] = t
        return t

    def _need(self, eng, tok):
        if tok[0] == "eng":
            _, f, n = tok
            if f == eng and eng == "pe":
                return
            sem, key, val = self.S[f], ("e", f), n
        else:
            _, sem, val = tok
            key = ("d", sem)
        if self.waited[eng].get(key, 0) >= val:
            return
        self.waited[eng][key] = val
        self.ops[eng].append(lambda e, sem=sem, val=val: e.wait_ge(sem, val))

    def _deps(self, eng, reads, writes):
        for b in reads:
            if b.w is not None:
                self._need(eng, b.w)
        for b in writes:
            if b.w is not None:
                self._need(eng, b.w)
            for t in b.r:
                self._need(eng, t)

    def op(self, eng, fn, reads=(), writes=()):
        self._deps(eng, reads, writes)
        self.N[eng] += 1
        tok = ("eng", eng, self.N[eng])
        sem = self.S[eng]
        self.ops[eng].append(lambda e, fn=fn, sem=sem: fn(e).then_inc(sem, 1))
        for b in reads:
            b.r.append(tok)
        for b in writes:
            b.w = tok
            b.r = []
        return tok

    def pe(self, fn, reads=(), writes=()):
        return self.op("pe", fn, reads, writes)

    def act(self, fn, reads=(), writes=()):
        return self.op("act", fn, reads, writes)

    def dve(self, fn, reads=(), writes=()):
        return self.op("dve", fn, reads, writes)

    def pool(self, fn, reads=(), writes=()):
        return self.op("pool", fn, reads, writes)

    def dma(self, out_ap, in_ap, reads=(), writes=(), q="sp", sem=None):
        self._deps(q, reads, writes)
        if sem is None:
            if writes and not reads:
                b = writes[0]
                if b.lsem is None:
                    b.lsem = self.newsem("l")
                sem = b.lsem
            else:
                b = reads[0]
                if b.ssem is None:
                    b.ssem = self.newsem("s")
                sem = b.ssem
        self.semval[sem] += 16
        val = self.semval[sem]
        self.ops[q].append(lambda e, o=out_ap, i=in_ap, sem=sem: e.dma_start(out=o, in_=i).then_inc(sem, 16))
        tok = ("dma", sem, val)
        for b in reads:
            b.r.append(tok)
        for b in writes:
            b.w = tok
            b.r = []
        return tok

    def wait_all_dma(self, eng):
        for s_, v in self.semval.items():
            if v > 0:
                self._need(eng, ("dma", s_, v))

    def barrier(self):
        for e in self.ENG:
            for f in self.ENG:
                if f != e and self.N[f] > 0:
                    self._need(e, ("eng", f, self.N[f]))
            for s, v in self.semval.items():
                if v > 0:
                    self._need(e, ("dma", s, v))
            for s, v in self.ccsems.items():
                self._need(e, ("dma", s, v))

    def finish(self):
        for s, v in self.semval.items():
            if v > 0:
                self._need("sp", ("dma", s, v))
        ops = self.ops
        with self.nc.Block() as block:
            @block.tensor
            def _(e):
                for o in ops["pe"]:
                    o(e)

            @block.scalar
            def _(e):
                for o in ops["act"]:
                    o(e)

            @block.vector
            def _(e):
                for o in ops["dve"]:
                    o(e)

            @block.gpsimd
            def _(e):
                for o in ops["pool"]:
                    o(e)

            @block.sync
            def _(e):
                for o in ops["sp"]:
                    o(e)
        self.es.close()
        return self.nc


class Ring:
    def __init__(self, P, name, shape, dt, n, es=None, psum=False):
        self.n = n
        self.i = 0
        alloc = P.ps if psum else P.sb
        self.t = [alloc(name, [128] + list(shape), dt, es) for _ in range(n)]
        self.b = [Buf(f"{name}{k}") for k in range(n)]

    def next(self):
        k = self.i % self.n
        self.i += 1
        return self.t[k], self.b[k]


def emit_consts(P):
    c = {}
    ident_d = P.din("c_ident", [128, 128], BF16)
    ones_d = P.din("c_ones", [128, 128], BF16)
    onesf_d = P.din("c_onesf", [128, 128], F32)
    c["ident"] = P.sb("ident", [128, 128], BF16)
    c["ones"] = P.sb("ones", [128, 128], BF16)
    c["onesf"] = P.sb("onesf", [128, 128], F32)
    c["eps"] = P.sb("eps", [128, 1], F32)
    cb = Buf("consts")
    c["buf"] = cb
    P.dma(c["ident"][:], ident_d, writes=[cb])
    P.dma(c["ones"][:], ones_d, writes=[cb])
    P.dma(c["onesf"][:], onesf_d, writes=[cb])
    eb = Buf("eps")
    P.pool(lambda e: e.memset(c["eps"][:], EPS), writes=[eb])
    c["epsb"] = eb
    return c


def emit_mod(P, C, cvec_d, adaw_d, adab_d, psring, dst_fn):
    with ExitStack() as es:
        cv = P.sb("cv", [128, NCH], F32, es)
        crep = P.sb("crep", [128, NCH, 128], F32, es)
        abrow = P.sb("abrow", [1, 3 * D], F32, es)
        cvb, crb, abb = Buf(), Buf(), Buf()
        P.dma(cv[:], cvec_d, writes=[cvb])
        P.dma(abrow[:], adab_d, writes=[abb])
        P.act(lambda e: e.activation(out=cv[:], in_=cv[:], func=AF.Silu), reads=[cvb], writes=[cvb])
        P.dve(lambda e: e.tensor_copy(crep[:], cv[:].unsqueeze(2).to_broadcast([128, NCH, 128])), reads=[cvb], writes=[crb])
        awr = Ring(P, "aw", [NCH, 512], F32, 2, es)
        awv = adaw_d.rearrange("(j p) n -> p j n", p=128)
        for nt in range(12):
            aw, awb = awr.next()
            P.dma(aw[:], awv[:, :, nt * 512:(nt + 1) * 512], writes=[awb])
            pst, psb = psring.next()
            for j in range(NCH):
                P.pe(lambda e, j=j, aw=aw, pst=pst: e.matmul(pst[:], crep[:, j, :], aw[:, j, :], start=(j == 0), stop=False),
                     reads=[crb, awb], writes=[psb])
            P.pe(lambda e, nt=nt, pst=pst: e.matmul(pst[:], C["onesf"][0:1, :], abrow[0:1, nt * 512:(nt + 1) * 512], start=False, stop=True),
                 reads=[C["buf"], abb], writes=[psb])
            dst, dstb = dst_fn(nt)
            P.act(lambda e, dst=dst, pst=pst: e.activation(out=dst, in_=pst[:], func=AF.Copy), reads=[psb], writes=[dstb])
        P.barrier()


class HBuilder:
    def __init__(self, P, C, shift_bc, gs_bc, modb, ptring):
        self.P, self.C = P, C
        self.shift_bc, self.gs_bc, self.modb = shift_bc, gs_bc, modb
        self.xr = Ring(P, "xt", [D], F32, 2)
        self.hr = Ring(P, "hb", [D], BF16, 2)
        self.ssr = Ring(P, "ss", [1], F32, 2)
        self.sdr = Ring(P, "sd", [1], F32, 2)
        self.ptr = ptring
        self.k = 0

    def tile(self, srcs, hT, hTb, col0):
        P, C = self.P, self.C
        xt, xb = self.xr.next()
        hb, hbb = self.hr.next()
        ss, ssb = self.ssr.next()
        sd, sdb = self.sdr.next()
        for (p0, p1, ap) in srcs:
            P.dma(xt[p0:p1, :], ap, writes=[xb])
        P.pool(lambda e: e.memset(ss[:], 0.0), writes=[ssb])
        P.act(lambda e: e.activation(out=hb[:], in_=xt[:], func=AF.Square, accum_out=ss[:]), reads=[xb], writes=[hbb, ssb])
        P.act(lambda e: e.activation(out=sd[:], in_=ss[:], func=AF.Sqrt, scale=1.0 / D, bias=C["eps"][:, 0:1]),
              reads=[ssb, C["epsb"]], writes=[sdb])
        P.dve(lambda e: e.reciprocal(out=sd[:], in_=sd[:]), reads=[sdb], writes=[sdb])
        P.dve(lambda e: e.scalar_tensor_tensor(out=xt[:], in0=xt[:], scalar=sd[:, 0:1], in1=self.gs_bc, op0=ALU.mult, op1=ALU.mult),
              reads=[xb, sdb, self.modb], writes=[xb])
        P.dve(lambda e: e.tensor_tensor(out=hb[:], in0=xt[:], in1=self.shift_bc, op=ALU.add), reads=[xb, self.modb], writes=[hbb])
        for half in range(2):
            pt, ptb = self.ptr.next()
            for c in range(8):
                ch = half * 8 + c
                P.pe(lambda e, pt=pt, c=c, ch=ch: e.transpose(pt[:, c, :], hb[:, ch * 128:(ch + 1) * 128], C["ident"][:]),
                     reads=[hbb, C["buf"]], writes=[ptb])
            dst = hT[:, half * 8:half * 8 + 8, col0:col0 + 128]
            if (self.k + half) % 2 == 0:
                P.act(lambda e, dst=dst, pt=pt: e.activation(out=dst, in_=pt[:], func=AF.Copy), reads=[ptb], writes=[hTb[0]])
            else:
                P.dve(lambda e, dst=dst, pt=pt: e.tensor_copy(dst, pt[:]), reads=[ptb], writes=[hTb[1]])
        self.k += 1


class WStream:
    def __init__(self, P, w_d):
        self.P = P
        self.wv = w_d.rearrange("(j p) n -> p j n", p=128)
        self.wfr = Ring(P, "wf", [4, 512], F32, 4)
        self.wbr = Ring(P, "wb", [NCH, 512], BF16, 2)
        self.pending = {}
        self.k = 0

    def dma(self, g):
        wb, wbb = self.wbr.next()
        st = []
        for jq in range(4):
            wf, wfb = self.wfr.next()
            self.P.dma(wf[:], self.wv[:, 4 * jq:4 * jq + 4, g * 512:(g + 1) * 512], writes=[wfb])
            st.append((wf, wfb))
        self.pending[g] = (wb, wbb, st)

    def casts(self, g):
        P = self.P
        wb, wbb, st = self.pending.pop(g)
        self.k += 1
        for jq, (wf, wfb) in enumerate(st):
            dst = wb[:, 4 * jq:4 * jq + 4, :]
            if self.k % 2 == 0:
                P.dve(lambda e, dst=dst, wf=wf: e.tensor_copy(dst, wf[:]), reads=[wfb], writes=[wbb])
            else:
                P.act(lambda e, dst=dst, wf=wf: e.activation(out=dst, in_=wf[:], func=AF.Copy), reads=[wfb], writes=[wbb])
        return wb, wbb


def emit_A(P, C, T):
    P.begin_phase()
    x_d, cvec_d, g_d, adaw_d, adab_d, win_d = T["x"], T["cvec"], T["norm_g0"], T["ada_w0"], T["ada_b0"], T["w_in0"]
    qg_d, kg_d, rc_d, rs_d = T["qg"], T["kg"], T["ropeC"], T["ropeS"]
    QT_d, SG_d, GB_d = T["QT"], T["SG"], T["gate_bc0"]

    psA = Ring(P, "psA", [512], F32, 3, psum=True)
    psB = Ring(P, "psB", [512], F32, 2, psum=True)
    psT = Ring(P, "psT", [8, 128], BF16, 2, psum=True)

    modp = P.sb("modp", [128, 2 * D], F32)
    modb = Buf("modp")
    gains = P.sb("gains", [128, 2], F32)
    gainb = Buf("gains")
    P.dma(gains[:, 0:1], qg_d, writes=[gainb])
    P.dma(gains[:, 1:2], kg_d, writes=[gainb])

    with ExitStack() as es:
        gtmp = P.sb("gtmp", [128, D], F32, es)
        gbc = P.sb("gbc", [128, D], F32, es)
        gtb, gbb = Buf(), Buf()
        P.dma(gbc[:], g_d.partition_broadcast(128).rearrange("p o n -> p (o n)"), writes=[gbb])

        def dst_fn(nt):
            if nt < 8:
                return modp[:, nt * 512:(nt + 1) * 512], modb
            return gtmp[:, (nt - 8) * 512:(nt - 7) * 512], gtb
        emit_mod(P, C, cvec_d, adaw_d, adab_d, psA, dst_fn)
        P.dma(GB_d, gtmp[:], reads=[gtb])
        P.dve(lambda e: e.scalar_tensor_tensor(out=modp[:, D:2 * D], in0=modp[:, D:2 * D], scalar=1.0, in1=gbc[:], op0=ALU.add, op1=ALU.mult),
              reads=[modb, gbb], writes=[modb])
        P.barrier()

    hb_ = HBuilder(P, C, modp[:, 0:D], modp[:, D:2 * D], modb, psT)
    TH = 2048
    hT = P.sb("hT", [128, NCH, TH], BF16)
    hTb = (Buf("hTa"), Buf("hTd"))
    rC = P.sb("rC", [128, TH], F32)
    rS = P.sb("rS", [128, TH], F32)
    ropeb = Buf("rope")
    W = WStream(P, win_d)
    agb = Buf("ag")
    sqr = Ring(P, "sq", [512], BF16, 2)
    sdr = Ring(P, "sdq", [512], F32, 2)
    qnr = Ring(P, "qn", [512], F32, 2)
    t1r = Ring(P, "t1", [512], F32, 2)
    t2r = Ring(P, "t2", [512], F32, 2)
    qor = Ring(P, "qo", [512], BF16, 3)
    vor = qor
    NG = 10

    for th in range(2):
        tok0 = th * TH
        Vv = T["Vown"][th]
        W.dma(0)
        for tt in range(TH // 128):
            r0 = tok0 + tt * 128
            hb_.tile([(0, 128, x_d[r0:r0 + 128, :])], hT, hTb, tt * 128)
        P.dma(rC[:], rc_d[:, tok0:tok0 + TH], writes=[ropeb])
        P.dma(rS[:], rs_d[:, tok0:tok0 + TH], writes=[ropeb])
        pend_a, pend_b = [], []

        def stage1a(st):
            pst, psb, sq, sqb, t5, cc = st["pst"], st["psb"], st["sq"], st["sqb"], st["t5"], st["cc"]
            gcol = gains[:, 0:1] if cc < 16 else gains[:, 1:2]
            pB, pBb = psB.next()
            P.pe(lambda e: e.matmul(pB[:], C["ones"][:], sq[:], start=True, stop=True), reads=[C["buf"], sqb], writes=[pBb])
            sd, sdb = sdr.next()
            qn, qnb = qnr.next()
            t1, t1b = t1r.next()
            t2, t2b = t2r.next()
            P.act(lambda e: e.activation(out=sd[:], in_=pB[:], func=AF.Ln, scale=1.0 / HD, bias=C["eps"][:, 0:1]),
                  reads=[pBb, C["epsb"]], writes=[sdb])
            P.act(lambda e: e.activation(out=sd[:], in_=sd[:], func=AF.Exp, scale=-0.5), reads=[sdb], writes=[sdb])
            P.dve(lambda e: e.scalar_tensor_tensor(out=qn[:], in0=pst[:], scalar=gcol, in1=sd[:], op0=ALU.mult, op1=ALU.mult),
                  reads=[psb, sdb, gainb], writes=[qnb])
            cs = slice(t5 * 512, (t5 + 1) * 512)
            P.dve(lambda e: e.tensor_tensor(out=t1[:], in0=qn[:], in1=rC[:, cs], op=ALU.mult), reads=[qnb, ropeb], writes=[t1b])
            P.pool(lambda e: e.tensor_tensor(out=t2[0:64, :], in0=qn[64:128, :], in1=rS[64:128, cs], op=ALU.mult), reads=[qnb, ropeb], writes=[t2b])
            P.pool(lambda e: e.tensor_tensor(out=t2[64:128, :], in0=qn[0:64, :], in1=rS[0:64, cs], op=ALU.mult), reads=[qnb, ropeb], writes=[t2b])
            st.update(t1=t1, t1b=t1b, t2=t2, t2b=t2b)

        def stage1b(st):
            t1, t1b, t2, t2b, t5, cc = st["t1"], st["t1b"], st["t2"], st["t2b"], st["t5"], st["cc"]
            qo, qob = qor.next()
            P.dve(lambda e: e.tensor_tensor(out=qo[:], in0=t1[:], in1=t2[:], op=ALU.add), reads=[t1b, t2b], writes=[qob])
            if cc < 16:
                dstd = QT_d[cc, :, tok0 + t5 * 512: tok0 + (t5 + 1) * 512]
            else:
                dstd = T["KTown"][(cc - 16) // 2][((cc - 16) % 2) * 128:((cc - 16) % 2) * 128 + 128, tok0 + t5 * 512: tok0 + (t5 + 1) * 512]
            P.dma(dstd, qo[:], reads=[qob])

        def flush():
            while pend_a:
                st = pend_a.pop(0)
                stage1a(st)
                pend_b.append(st)
            while pend_b:
                stage1b(pend_b.pop(0))

        wb, wbb = W.casts(0)
        for g in range(NG):
            if g + 1 < NG:
                W.dma(g + 1)
            nxt = None
            if g == 5:
                flush()
                for tt in range(TH // 128):
                    pst, psb = psA.next()
                    for j in range(NCH):
                        P.pe(lambda e, pst=pst, j=j, tt=tt, wb=wb: e.matmul(pst[:], hT[:, j, tt * 128:(tt + 1) * 128], wb[:, j, :], start=(j == 0), stop=(j == NCH - 1)),
                             reads=[hTb[0], hTb[1], wbb], writes=[psb])
                    vo, vob = vor.next()
                    if tt % 2 == 0:
                        P.dve(lambda e, vo=vo, pst=pst: e.tensor_copy(vo[:], pst[:]), reads=[psb], writes=[vob])
                    else:
                        P.act(lambda e, vo=vo, pst=pst: e.activation(out=vo[:], in_=pst[:], func=AF.Copy), reads=[psb], writes=[vob])
                    P.dma(Vv[tt * 128:(tt + 1) * 128, :], vo[:], reads=[vob])
                    if tt == 8:
                        nxt = W.casts(g + 1)
                wb, wbb = nxt
                if th == 1:
                    P.wait_all_dma("pool")
                    for j in range(2):
                        P.allgather(T["Vown"][j], T["Vag"][j], writes=[agb])
                continue
            for ci in range(4):
                cc = 4 * g + ci if g < 5 else 4 * g + ci
                if ci == 2 and g + 1 < NG:
                    nxt = W.casts(g + 1)
                for t5 in range(TH // 512):
                    pst, psb = psA.next()
                    for j in range(NCH):
                        P.pe(lambda e, pst=pst, j=j, t5=t5, wb=wb, ci=ci: e.matmul(pst[:], wb[:, j, ci * 128:(ci + 1) * 128], hT[:, j, t5 * 512:(t5 + 1) * 512], start=(j == 0), stop=(j == NCH - 1)),
                             reads=[hTb[0], hTb[1], wbb], writes=[psb])
                    if cc < 20:
                        sq, sqb = sqr.next()
                        P.act(lambda e, sq=sq, pst=pst: e.activation(out=sq[:], in_=pst[:], func=AF.Square), reads=[psb], writes=[sqb])
                        if pend_b:
                            stage1b(pend_b.pop(0))
                        if pend_a:
                            st = pend_a.pop(0)
                            stage1a(st)
                            pend_b.append(st)
                        pend_a.append(dict(pst=pst, psb=psb, sq=sq, sqb=sqb, t5=t5, cc=cc))
                    else:
                        qo, qob = qor.next()
                        P.act(lambda e, qo=qo, pst=pst: e.activation(out=qo[:], in_=pst[:], func=AF.Silu), reads=[psb], writes=[qob])
                        P.dma(SG_d[cc - 24, :, tok0 + t5 * 512: tok0 + (t5 + 1) * 512], qo[:], reads=[qob])
            if g == 4:
                flush()
                if th == 1:
                    P.wait_all_dma("pool")
                    for j in range(2):
                        P.allgather(T["KTown"][j], T["KTag"][j], writes=[agb])
            if nxt is not None:
                wb, wbb = nxt
    P.end_phase()


def host_consts():
    return {
        "c_ident": np.eye(128, dtype=np.float32).astype(NPBF),
        "c_ones": np.ones((128, 128), np.float32).astype(NPBF),
        "c_onesf": np.ones((128, 128), np.float32),
    }


def rope_tables():
    t = np.arange(S)
    rows = (t // 64).astype(np.float32)
    cols = (t % 64).astype(np.float32)
    inv = (np.float32(10000.0) ** (-np.arange(0, 64, 2, dtype=np.float32) / np.float32(64))).astype(np.float32)
    ang = np.concatenate([rows[:, None] * inv[None, :], cols[:, None] * inv[None, :]], axis=-1)
    c = np.cos(ang).T.astype(np.float32)
    s = np.sin(ang).T.astype(np.float32)
    return np.concatenate([c, c], 0), np.concatenate([s, -s], 0)


PERM = np.concatenate([np.arange(0, 128, 2), np.arange(1, 128, 2)])


def emit_B(P, C, T):
    P.begin_phase()
    QT_d, SG_d, OG_d = T["QT"], T["SG"], T["OGT"]
    psS = Ring(P, "psS", [512], F32, 4, psum=True)
    psO = Ring(P, "psO", [512], F32, 2, psum=True)
    psL = Ring(P, "psL", [512], F32, 2, psum=True)
    kr = Ring(P, "kt", [S], BF16, 2)
    vr = Ring(P, "vt", [64, 128], BF16, 2)
    qr = Ring(P, "qt", [TOWN], BF16, 2)
    sgr = Ring(P, "sg", [TOWN], BF16, 2)
    pr = Ring(P, "p", [512], BF16, 6)
    rlr = Ring(P, "rl", [512], F32, 2)
    lnr = Ring(P, "lnl", [512], F32, 2)
    asr = Ring(P, "asum", [512], BF16, 2)
    aPr = Ring(P, "accP", [512], F32, 2)
    t01r = Ring(P, "t01", [512], BF16, 2)
    t23r = Ring(P, "t23", [512], BF16, 2)
    q4r = Ring(P, "q4", [512], BF16, 2)
    o1r = Ring(P, "o1", [512], F32, 2)
    ogr = Ring(P, "og", [512], BF16, 2)
    scale = 1.0 / math.sqrt(HD)
    NKB = S // 128
    pend = []

    def front(st):
        pS, pSb = psS.next()
        p, pb = pr.next()
        kt, ktb, qt_, qb, qi, kb = st["kt"], st["ktb"], st["qt"], st["qb"], st["qi"], st["kb"]
        P.pe(lambda e: e.matmul(pS[:], kt[:, kb * 128:(kb + 1) * 128], qt_[:, qi * 512:(qi + 1) * 512], start=True, stop=True),
             reads=[ktb, qb], writes=[pSb])
        P.act(lambda e: e.activation(out=p[:], in_=pS[:], func=AF.Exp, scale=scale), reads=[pSb], writes=[pb])
        st["p"], st["pb"] = p, pb

    def back(st):
        p, pb, vt, vtb, kb = st["p"], st["pb"], st["vt"], st["vtb"], st["kb"]
        pO, pOb = st["pO"], st["pOb"]
        P.pe(lambda e: e.matmul(pO[:], vt[:, kb, :], p[:], start=(kb == 0), stop=(kb == NKB - 1)), reads=[vtb, pb], writes=[pOb])
        acc, accb = st["accs"]
        r = kb % 4
        grp = st["grp"]
        if r == 0:
            grp.clear()
        grp.append((p, pb))
        if r == 1:
            t01, t01b = t01r.next()
            (pa, pab), (pc, pcb) = grp[0], grp[1]
            P.dve(lambda e: e.tensor_tensor(out=t01[:], in0=pa[:], in1=pc[:], op=ALU.add), reads=[pab, pcb], writes=[t01b])
            st["grp_t01"][:] = [(t01, t01b)]
        if r == 3:
            t23, t23b = t23r.next()
            q4, q4b = q4r.next()
            (pa, pab), (pc, pcb) = grp[2], grp[3]
            (t01, t01b) = st["grp_t01"][0]
            P.dve(lambda e: e.tensor_tensor(out=t23[:], in0=pa[:], in1=pc[:], op=ALU.add), reads=[pab, pcb], writes=[t23b])
            P.dve(lambda e: e.tensor_tensor(out=q4[:], in0=t01[:], in1=t23[:], op=ALU.add), reads=[t01b, t23b], writes=[q4b])
            if kb == 3:
                P.pool(lambda e: e.tensor_copy(acc[:], q4[:]), reads=[q4b], writes=[accb])
            else:
                P.pool(lambda e: e.tensor_tensor(out=acc[:], in0=acc[:], in1=q4[:], op=ALU.add), reads=[q4b, accb], writes=[accb])
        if kb == NKB - 1:
            h, qi, sg, sgb = st["h"], st["qi"], st["sg"], st["sgb"]
            pL, pLb = psL.next()
            asum, asumb = asr.next()
            ln, lnb = lnr.next()
            rl, rlb = rlr.next()
            o1, o1b = o1r.next()
            og, ogb = ogr.next()
            P.pool(lambda e: e.tensor_copy(asum[:], acc[:]), reads=[accb], writes=[asumb])
            P.pe(lambda e: e.matmul(pL[:], C["ones"][:], asum[:], start=True, stop=True), reads=[C["buf"], asumb], writes=[pLb])
            P.act(lambda e: e.activation(out=ln[:], in_=pL[:], func=AF.Ln), reads=[pLb], writes=[lnb])
            P.act(lambda e: e.activation(out=rl[:], in_=ln[:], func=AF.Exp, scale=-1.0), reads=[lnb], writes=[rlb])
            P.dve(lambda e: e.tensor_tensor(out=o1[:], in0=pO[:], in1=rl[:], op=ALU.mult), reads=[pOb, rlb], writes=[o1b])
            P.pool(lambda e: e.tensor_tensor(out=og[:], in0=o1[:], in1=sg[:, qi * 512:(qi + 1) * 512], op=ALU.mult), reads=[o1b, sgb], writes=[ogb])
            P.dma(OG_d[h, :, qi * 512:(qi + 1) * 512], og[:], reads=[ogb])

    for g in range(4):
        kt, ktb = kr.next()
        vt, vtb = vr.next()
        P.dma(kt[:].rearrange("p (r t) -> p r t", r=2),
              T["KTag"][g // 2].rearrange("(r k p) t -> p k r t", r=2, k=2)[:, g % 2], writes=[ktb])
        for j in range(2):
            for r in range(2):
                P.dma(vt[:, r * 32 + j * 16:r * 32 + j * 16 + 16, :],
                      T["Vag"][j][r * 2048:(r + 1) * 2048, g * 128:(g + 1) * 128].rearrange("(kk p) d -> p kk d", p=128), writes=[vtb])
        for hh in range(4):
            h = 4 * g + hh
            qt_, qb = qr.next()
            sg, sgb = sgr.next()
            P.dma(qt_[:], QT_d[h], writes=[qb])
            P.dma(sg[:], SG_d[h], writes=[sgb])
            for qi in range(TOWN // 512):
                pO, pOb = psO.next()
                accs = aPr.next()
                grp, grp_t01 = [], []
                for kb in range(NKB):
                    st = dict(kt=kt, ktb=ktb, vt=vt, vtb=vtb, qt=qt_, qb=qb, sg=sg, sgb=sgb, h=h, qi=qi, kb=kb,
                              pO=pO, pOb=pOb, accs=accs, grp=grp, grp_t01=grp_t01)
                    front(st)
                    pend.append(st)
                    if len(pend) > 2:
                        back(pend.pop(0))
    while pend:
        back(pend.pop(0))
    P.end_phase()


def emit_CF(P, C, T, final):
    P.begin_phase()
    if final:
        w_d, x_d, gb_d = T["w_out1"], T["x1"], T["gate_bc1"]
        sg_d, fg_d, out_d = T["SG1"], T["final_g"], T["out"]
        xv = x_d.rearrange("(a b) m -> b a m", b=64)
        ov = out_d.rearrange("(a b) m -> b a m", b=64)
        sel = P.sb("sel", [128, 2], F32)
        selb = Buf("sel")
        P.dma(sel[:], T["sel"], writes=[selb])
    else:
        w_d, x_d, gb_d = T["w_out0"], T["x"], T["gate_bc0"]
        OG_d, out_d = T["OGT"], T["x1"]
    psY = Ring(P, "psY", [512], F32, 6 if final else 8, psum=True)
    wbf = P.sb("wbf", [128, NCH, D], BF16)
    wbb = Buf("wbf")
    gbc = P.sb("gbc", [128, D], F32)
    gbb = Buf("gbc")
    P.dma(gbc[:], gb_d, writes=[gbb])
    wfr = Ring(P, "wf", [4, 512], F32, 3)
    wv = w_d.rearrange("(j p) n -> p j n", p=128)
    wbn = [Buf(f"wbf{n}") for n in range(4)]
    for n in range(4):
        gsl = gbc[:, n * 512:(n + 1) * 512].unsqueeze(1).to_broadcast([128, 4, 512])
        for jq in range(4):
            wf, wfb = wfr.next()
            P.dma(wf[:], wv[:, 4 * jq:4 * jq + 4, n * 512:(n + 1) * 512], writes=[wfb])
            dst = wbf[:, 4 * jq:4 * jq + 4, n * 512:(n + 1) * 512]
            P.dve(lambda e, wf=wf, dst=dst, gsl=gsl: e.tensor_tensor(out=dst, in0=wf[:], in1=gsl, op=ALU.mult), reads=[wfb, gbb], writes=[wbn[n]])
    xr = Ring(P, "xt", [D], F32, 3)
    if final:
        psT = Ring(P, "psT", [8, 128], BF16, 2, psum=True)
        fgb_t = P.sb("fgbc", [128, D], F32)
        fgb = Buf("fg")
        P.dma(fgb_t[:], fg_d.partition_broadcast(128).rearrange("p o n -> p (o n)"), writes=[fgb])
        fr = Ring(P, "ft", [D], BF16, 2)
        b0r = Ring(P, "b0", [1024], BF16, 2)
        b1r = Ring(P, "b1", [1024], BF16, 2)
        sgr = Ring(P, "sgt", [D], BF16, 2)
        obr = Ring(P, "ogb", [D], BF16, 2)
        otr = Ring(P, "ogT", [NCH, 128], BF16, 2)
        ssr = Ring(P, "ss", [1], F32, 2)
        sdr = Ring(P, "sd", [1], F32, 2)
        jr = Ring(P, "junk", [D], BF16, 1)
    else:
        ogr = Ring(P, "og", [NCH, 512], BF16, 2)
        OGv = OG_d.rearrange("c p t -> p c t")
    cur_og = [None]

    def front(tt):
        r0 = tt * 128
        st = dict(tt=tt)
        if final:
            ft, ftb = fr.next()
            sg, sgb = sgr.next()
            ob, obb = obr.next()
            oT, oTb = otr.next()
            b0, b0b = b0r.next()
            b1, b1b = b1r.next()
            fob = Buf("fo")
            P.dma(ft[:, 0:1024], T["f_own"][r0:r0 + 128, :], writes=[fob])
            fj, fr0 = tt // 8, (tt % 8) * 128
            P.dma(b0[:], T["fag"][fj][fr0:fr0 + 128, :], writes=[b0b])
            P.dma(b1[:], T["fag"][fj][1024 + fr0:1024 + fr0 + 128, :], writes=[b1b])
            P.dma(sg[:], sg_d[r0:r0 + 128, :], writes=[sgb])
            P.dve(lambda e: e.tensor_scalar(out=b1[:], in0=b1[:], scalar1=sel[:, 1:2], scalar2=None, op0=ALU.mult), reads=[b1b, selb], writes=[b1b])
            P.dve(lambda e: e.scalar_tensor_tensor(out=ft[:, 1024:2048], in0=b0[:], scalar=sel[:, 0:1], in1=b1[:], op0=ALU.mult, op1=ALU.add),
                  reads=[b0b, b1b, selb, fob], writes=[ftb])
            P.dve(lambda e: e.tensor_tensor(out=ob[:], in0=ft[:], in1=sg[:], op=ALU.mult), reads=[ftb, fob, sgb], writes=[obb])
            for half in range(2):
                pt, ptb = psT.next()
                for c in range(8):
                    ch = half * 8 + c
                    P.pe(lambda e, pt=pt, c=c, ch=ch: e.transpose(pt[:, c, :], ob[:, ch * 128:(ch + 1) * 128], C["ident"][:]),
                         reads=[obb, C["buf"]], writes=[ptb])
                P.act(lambda e, pt=pt, half=half: e.activation(out=oT[:, half * 8:half * 8 + 8, :], in_=pt[:], func=AF.Copy), reads=[ptb], writes=[oTb])
            st["lhs"] = lambda c: oT[:, c, :]
            st["lb"] = oTb
        else:
            if tt % 4 == 0:
                og, ogb = ogr.next()
                P.dma(og[:], OGv[:, :, r0:r0 + 512], writes=[ogb])
                cur_og[0] = (og, ogb)
            og, ogb = cur_og[0]
            ti = tt % 4
            st["lhs"] = lambda c: og[:, c, ti * 128:(ti + 1) * 128]
            st["lb"] = ogb
        xt, xb = xr.next()
        if final:
            P.dma(xt[0:64, :], xv[2 * tt], writes=[xb])
            P.dma(xt[64:128, :], xv[2 * tt + 1], writes=[xb])
        else:
            P.dma(xt[:], x_d[r0:r0 + 128, :], writes=[xb])
        st["xt"], st["xb"] = xt, xb
        return st

    def mm(st):
        lhs, lb = st["lhs"], st["lb"]
        st["ps"] = []
        for n in range(4):
            ps, psb = psY.next()
            for c in range(NCH):
                P.pe(lambda e, ps=ps, c=c, n=n: e.matmul(ps[:], lhs(c), wbf[:, c, n * 512:(n + 1) * 512], start=(c == 0), stop=(c == NCH - 1)),
                     reads=[lb, wbn[n]], writes=[psb])
            st["ps"].append((ps, psb))

    def back(st):
        tt, xt, xb = st["tt"], st["xt"], st["xb"]
        r0 = tt * 128
        for n, (ps, psb) in enumerate(st["ps"]):
            P.dve(lambda e, ps=ps, n=n: e.tensor_tensor(out=xt[:, n * 512:(n + 1) * 512], in0=ps[:], in1=xt[:, n * 512:(n + 1) * 512], op=ALU.add),
                  reads=[psb, xb], writes=[xb])
        if final:
            ss, ssb = ssr.next()
            sd, sdb = sdr.next()
            jk, jkb = jr.next()
            P.pool(lambda e: e.memset(ss[:], 0.0), writes=[ssb])
            P.act(lambda e: e.activation(out=jk[:], in_=xt[:], func=AF.Square, accum_out=ss[:]), reads=[xb], writes=[jkb, ssb])
            P.act(lambda e: e.activation(out=sd[:], in_=ss[:], func=AF.Sqrt, scale=1.0 / D, bias=C["eps"][:, 0:1]),
                  reads=[ssb, C["epsb"]], writes=[sdb])
            P.dve(lambda e: e.reciprocal(out=sd[:], in_=sd[:]), reads=[sdb], writes=[sdb])
            P.dve(lambda e: e.scalar_tensor_tensor(out=xt[:], in0=xt[:], scalar=sd[:, 0:1], in1=fgb_t[:], op0=ALU.mult, op1=ALU.mult),
                  reads=[xb, sdb, fgb], writes=[xb])
            P.dma(ov[2 * tt], xt[0:64, :], reads=[xb])
            P.dma(ov[2 * tt + 1], xt[64:128, :], reads=[xb])
        else:
            P.dma(out_d[r0:r0 + 128, :], xt[:], reads=[xb])

    NT = TOWN // 128
    prev = front(0)
    mm(prev)
    for tt in range(1, NT):
        cur = front(tt)
        back(prev)
        mm(cur)
        prev = cur
    back(prev)
    P.end_phase()


def emit_D(P, C, T):
    P.begin_phase()
    x_d, cvec_d, g_d, adaw_d, adab_d, win_d = T["x1"], T["cvec"], T["norm_g1"], T["ada_w1"], T["ada_b1"], T["w_in1"]
    SG_d, GB_d = T["SG1"], T["gate_bc1"]
    psA = Ring(P, "psA", [512], F32, 4, psum=True)
    psT = Ring(P, "psT", [8, 128], BF16, 2, psum=True)
    modp = P.sb("modp", [128, 2 * D], F32)
    modb = Buf("modp")
    with ExitStack() as es:
        gtmp = P.sb("gtmp", [128, D], F32, es)
        gbc = P.sb("gbc", [128, D], F32, es)
        gtb, gbb = Buf(), Buf()
        P.dma(gbc[:], g_d.partition_broadcast(128).rearrange("p o n -> p (o n)"), writes=[gbb])

        def dst_fn(nt):
            if nt < 8:
                return modp[:, nt * 512:(nt + 1) * 512], modb
            return gtmp[:, (nt - 8) * 512:(nt - 7) * 512], gtb
        emit_mod(P, C, cvec_d, adaw_d, adab_d, psA, dst_fn)
        P.dma(GB_d, gtmp[:], reads=[gtb])
        P.dve(lambda e: e.scalar_tensor_tensor(out=modp[:, D:2 * D], in0=modp[:, D:2 * D], scalar=1.0, in1=gbc[:], op0=ALU.add, op1=ALU.mult),
              reads=[modb, gbb], writes=[modb])
        P.barrier()
    hb_ = HBuilder(P, C, modp[:, 0:D], modp[:, D:2 * D], modb, psT)
    TH = 2048
    hT = P.sb("hT", [128, NCH, TH], BF16)
    hTb = (Buf("hTa"), Buf("hTd"))
    W = WStream(P, win_d)
    agb = Buf("ag")
    uor = Ring(P, "uo", [512], BF16, 3)
    gor = Ring(P, "go", [512], BF16, 3)
    xv = x_d.rearrange("(a b) m -> b a m", b=64)
    NG = 8
    for th in range(2):
        tok0 = th * TH
        W.dma(0)
        for tt in range(TH // 128):
            t2a = (tok0 + tt * 128) // 64
            hb_.tile([(0, 64, xv[t2a]), (64, 128, xv[t2a + 1])], hT, hTb, tt * 128)
        wb, wbb = W.casts(0)
        for g in range(NG):
            if g + 1 < NG:
                W.dma(g + 1)
            nxt = None
            if g < 4:
                for ci in range(4):
                    cc = 4 * g + ci
                    if ci == 2 and g + 1 < NG:
                        nxt = W.casts(g + 1)
                    for t5 in range(TH // 512):
                        pst, psb = psA.next()
                        for j in range(NCH):
                            P.pe(lambda e, pst=pst, j=j, t5=t5, wb=wb, ci=ci: e.matmul(pst[:], wb[:, j, ci * 128:(ci + 1) * 128], hT[:, j, t5 * 512:(t5 + 1) * 512], start=(j == 0), stop=(j == NCH - 1)),
                                 reads=[hTb[0], hTb[1], wbb], writes=[psb])
                        uo, uob = uor.next()
                        if t5 % 2 == 0:
                            P.act(lambda e, uo=uo, pst=pst: e.activation(out=uo[:], in_=pst[:], func=AF.Copy), reads=[psb], writes=[uob])
                        else:
                            P.dve(lambda e, uo=uo, pst=pst: e.tensor_copy(uo[:], pst[:]), reads=[psb], writes=[uob])
                        if cc < 8:
                            udst = T["UTown"][cc, :, tok0 + t5 * 512: tok0 + (t5 + 1) * 512]
                        else:
                            udst = T["UTsend"][(cc - 8) // 2][((cc - 8) % 2) * 128:((cc - 8) % 2) * 128 + 128, tok0 + t5 * 512: tok0 + (t5 + 1) * 512]
                        P.dma(udst, uo[:], reads=[uob])
                if g == 3 and th == 1:
                    P.wait_all_dma("pool")
                    for j in range(4):
                        P.allgather(T["UTsend"][j], T["UTag"][j], writes=[agb])
            else:
                for tt in range(TH // 128):
                    if tt == 8 and g + 1 < NG:
                        nxt = W.casts(g + 1)
                    pst, psb = psA.next()
                    for j in range(NCH):
                        P.pe(lambda e, pst=pst, j=j, tt=tt, wb=wb: e.matmul(pst[:], hT[:, j, tt * 128:(tt + 1) * 128], wb[:, j, :], start=(j == 0), stop=(j == NCH - 1)),
                             reads=[hTb[0], hTb[1], wbb], writes=[psb])
                    go, gob = gor.next()
                    P.act(lambda e, go=go, pst=pst: e.activation(out=go[:], in_=pst[:], func=AF.Silu), reads=[psb], writes=[gob])
                    r0 = tok0 + tt * 128
                    P.dma(SG_d[r0:r0 + 128, (g - 4) * 512:(g - 3) * 512], go[:], reads=[gob])
            if nxt is not None:
                wb, wbb = nxt
    P.end_phase()


def fourier_tables():
    w = np.arange(256)
    aw = 2 * np.pi * np.outer(w, w) / 256
    csw = np.concatenate([np.cos(aw), np.sin(aw)], 1) / 16.0
    csw = csw.reshape(2, 128, 512).transpose(1, 0, 2)
    t1 = np.arange(128)
    a1 = 2 * np.pi * np.outer(t1, t1) / 128
    nrm = 1.0 / math.sqrt(8192.0)
    ra = np.concatenate([np.cos(a1), np.sin(a1)], 1) * nrm
    rb = np.concatenate([-np.sin(a1), np.cos(a1)], 1) * nrm
    m = np.arange(128)
    t2 = m % 64
    c2 = m // 64
    atw = 2 * np.pi * np.outer(t2, np.arange(128)) / 8192
    ct, st = np.cos(atw), np.sin(atw)
    n = np.arange(128)
    c2n, k2 = n // 64, n % 64
    a2 = 2 * np.pi * np.outer(t2, k2) / 64
    delta = (c2[:, None] == c2n[None, :]).astype(np.float64)
    re = delta * np.cos(a2)
    rf = -delta * np.sin(a2)
    return {
        "t_csw": np.ascontiguousarray(csw).astype(np.float32).astype(NPBF),
        "t_ra": ra.astype(np.float32).astype(NPBF),
        "t_rb": rb.astype(np.float32).astype(NPBF),
        "t_ct": ct.astype(np.float32).astype(NPBF),
        "t_st": st.astype(np.float32).astype(NPBF),
        "t_re": re.astype(np.float32).astype(NPBF),
        "t_rf": rf.astype(np.float32).astype(NPBF),
    }


def emit_E(P, C, T):
    P.begin_phase()
    csw_d, ra_d, rb_d, ct_d, st_d, re_d, rf_d = T["t_csw"], T["t_ra"], T["t_rb"], T["t_ct"], T["t_st"], T["t_re"], T["t_rf"]
    sel = P.sb("sel", [128, 2], F32)
    selb = Buf("sel")
    P.dma(sel[:], T["sel"], writes=[selb])
    csw = P.sb("csw", [128, 2, 512], BF16)
    ra = P.sb("ra", [128, 256], BF16)
    rb = P.sb("rb", [128, 256], BF16)
    ct = P.sb("ct", [128, 128], BF16)
    st = P.sb("st", [128, 128], BF16)
    re = P.sb("re", [128, 128], BF16)
    rf = P.sb("rf", [128, 128], BF16)
    tb = Buf("tables")
    for dst, src in ((csw, csw_d), (ra, ra_d), (rb, rb_d), (ct, ct_d), (st, st_d), (re, re_d), (rf, rf_d)):
        P.dma(dst[:], src, writes=[tb])
    ps1 = Ring(P, "ps1", [512], F32, 2, psum=True)
    ps2 = Ring(P, "ps2", [512], F32, 2, psum=True)
    ps3 = Ring(P, "ps3", [512], F32, 2, psum=True)
    AB = P.sb("AB", [128, 512, 64], BF16)
    ABa, ABd = Buf("ABa"), Buf("ABd")
    G2 = P.sb("G2", [128, 64, 256], BF16)
    G2b = Buf("G2")
    fS = P.sb("fS", [128, 64, 128], BF16)
    fSa, fSd = Buf("fSa"), Buf("fSd")
    ur = Ring(P, "u", [2, 8, 128], BF16, 3)
    upb_ = [Buf("up0"), Buf("up1"), Buf("up2")]
    b0r = Ring(P, "ub0", [2, 512], BF16, 2)
    b1r = Ring(P, "ub1", [2, 512], BF16, 2)
    efr = Ring(P, "ef", [512], BF16, 3)
    m1r = Ring(P, "m1", [2, 128], BF16, 3)
    m2r = Ring(P, "m2", [2, 128], BF16, 3)
    m3r = Ring(P, "m3", [2, 128], BF16, 3)
    m4r = Ring(P, "m4", [2, 128], BF16, 3)
    fov = T["f_own"].rearrange("(t a e) c -> e t a c", t=64, a=32, e=2)
    fsv = [T["fsend"][j].rearrange("(t a e) c -> e t a c", t=16, a=32, e=2) for j in range(4)]
    uk = 0
    ctb = ct[:].unsqueeze(1).to_broadcast([128, 2, 128])
    stb = st[:].unsqueeze(1).to_broadcast([128, 2, 128])
    k = 0
    for g in range(4):
        for t8 in range(8):
            u, ub = ur.next()
            upb = upb_[uk % 3]
            uk += 1
            b0, b0b = b0r.next()
            b1, b1b = b1r.next()
            c0 = t8 * 512
            for kc in range(2):
                P.dma(u[:, kc, :, 0:64], T["UTown"][2 * g + kc, :, c0:c0 + 512].rearrange("p (i t) -> p i t", t=64), writes=[ub])
            agv = T["UTag"][g].rearrange("(r k p) t -> p r k t", r=2, k=2)
            P.dma(b0[:], agv[:, 0, :, c0:c0 + 512], writes=[b0b])
            P.dma(b1[:], agv[:, 1, :, c0:c0 + 512], writes=[b1b])
            P.dve(lambda e, b1=b1: e.tensor_scalar(out=b1[:], in0=b1[:], scalar1=sel[:, 1:2], scalar2=None, op0=ALU.mult), reads=[b1b, selb], writes=[b1b])
            P.dve(lambda e, u=u, b0=b0, b1=b1: e.scalar_tensor_tensor(out=u[:, :, :, 64:128], in0=b0[:].rearrange("p k (i t) -> p k i t", t=64), scalar=sel[:, 0:1],
                                                                     in1=b1[:].rearrange("p k (i t) -> p k i t", t=64), op0=ALU.mult, op1=ALU.add),
                  reads=[b0b, b1b, selb], writes=[upb])
            for ti in range(8):
                t2 = t8 * 8 + ti
                ps, psb = ps1.next()
                for kc in range(2):
                    P.pe(lambda e, ps=ps, u=u, kc=kc, ti=ti: e.matmul(ps[:], u[:, kc, ti, :], csw[:, kc, :], start=(kc == 0), stop=(kc == 1)),
                         reads=[ub, upb, tb], writes=[psb])
                k += 1
                if k % 3 != 0:
                    P.act(lambda e, ps=ps, t2=t2: e.activation(out=AB[:, :, t2], in_=ps[:], func=AF.Copy), reads=[psb], writes=[ABa])
                else:
                    P.dve(lambda e, ps=ps, t2=t2: e.tensor_copy(AB[:, :, t2], ps[:]), reads=[psb], writes=[ABd])
        for half in range(2):
            for pb in range(32):
                ps, psb = ps2.next()
                for pi in range(2):
                    pl = 2 * pb + pi
                    ch = half * 128 + 2 * pl
                    P.pe(lambda e, ps=ps, pi=pi, ch=ch: e.matmul(ps[:, pi * 256:(pi + 1) * 256], AB[:, ch:ch + 2, :].rearrange("p a b -> p (a b)"), ra[:], start=True, stop=False),
                         reads=[ABa, ABd, tb], writes=[psb])
                    P.pe(lambda e, ps=ps, pi=pi, ch=ch: e.matmul(ps[:, pi * 256:(pi + 1) * 256], AB[:, 256 + ch:256 + ch + 2, :].rearrange("p a b -> p (a b)"), rb[:], start=False, stop=True),
                         reads=[ABa, ABd, tb], writes=[psb])
                ef, efb = efr.next()
                P.act(lambda e, ef=ef, ps=ps: e.activation(out=ef[:], in_=ps[:], func=AF.Copy), reads=[psb], writes=[efb])
                efv = ef[:].rearrange("p (a b) -> p a b", a=2)
                Ev, Fv = efv[:, :, 0:128], efv[:, :, 128:256]
                m1, m1b = m1r.next()
                m2, m2b = m2r.next()
                m3, m3b = m3r.next()
                m4, m4b = m4r.next()
                P.pool(lambda e, m1=m1, Ev=Ev: e.tensor_tensor(out=m1[:], in0=Ev, in1=ctb, op=ALU.mult), reads=[efb, tb], writes=[m1b])
                P.pool(lambda e, m2=m2, Fv=Fv: e.tensor_tensor(out=m2[:], in0=Fv, in1=stb, op=ALU.mult), reads=[efb, tb], writes=[m2b])
                P.dve(lambda e, m3=m3, Ev=Ev: e.tensor_tensor(out=m3[:], in0=Ev, in1=stb, op=ALU.mult), reads=[efb, tb], writes=[m3b])
                P.pool(lambda e, m4=m4, Fv=Fv: e.tensor_tensor(out=m4[:], in0=Fv, in1=ctb, op=ALU.mult), reads=[efb, tb], writes=[m4b])
                P.dve(lambda e, m1=m1, m2=m2, pb=pb: e.tensor_tensor(out=G2[:, 2 * pb:2 * pb + 2, 0:128], in0=m1[:], in1=m2[:], op=ALU.subtract),
                      reads=[m1b, m2b], writes=[G2b])
                P.dve(lambda e, m3=m3, m4=m4, pb=pb: e.tensor_tensor(out=G2[:, 2 * pb:2 * pb + 2, 128:256], in0=m3[:], in1=m4[:], op=ALU.add),
                      reads=[m3b, m4b], writes=[G2b])
            for qb in range(16):
                ps, psb = ps3.next()
                for pi in range(4):
                    pl = 4 * qb + pi
                    P.pe(lambda e, ps=ps, pi=pi, pl=pl: e.matmul(ps[:, pi * 128:(pi + 1) * 128], G2[:, pl, 0:128], re[:], start=True, stop=False),
                         reads=[G2b, tb], writes=[psb])
                    P.pe(lambda e, ps=ps, pi=pi, pl=pl: e.matmul(ps[:, pi * 128:(pi + 1) * 128], G2[:, pl, 128:256], rf[:], start=False, stop=True),
                         reads=[G2b, tb], writes=[psb])
                dst = fS[:, :, 8 * qb:8 * qb + 8].rearrange("p k (a c) -> p a c k", a=4, c=2)
                src_fn = lambda ps=ps: ps[:].rearrange("p (a c k) -> p a c k", a=4, c=2)
                if qb % 2 == 0:
                    P.act(lambda e, dst=dst, src_fn=src_fn: e.activation(out=dst, in_=src_fn(), func=AF.Copy), reads=[psb], writes=[fSa])
                else:
                    P.dve(lambda e, dst=dst, src_fn=src_fn: e.tensor_copy(dst, src_fn()), reads=[psb], writes=[fSd])
            c0 = g * 256 + half * 128
            for e_ in range(2):
                P.dma(fov[e_][:, :, c0:c0 + 128], fS[e_ * 64:(e_ + 1) * 64, 0:32, :], reads=[fSa, fSd])
                for j in range(4):
                    P.dma(fsv[j][e_][:, :, c0:c0 + 128], fS[e_ * 64 + 16 * j:e_ * 64 + 16 * j + 16, 32:64, :], reads=[fSa, fSd])
    P.barrier()
    agb = Buf("ag")
    for j in range(4):
        P.allgather(T["fsend"][j], T["fag"][j], writes=[agb])
    P.end_phase()


def assemble_UTf(ut0, ut1, hf):
    a = np.asarray(ut0)[8 * hf:8 * hf + 8].reshape(8, 128, 64, 64)
    b = np.asarray(ut1)[8 * hf:8 * hf + 8].reshape(8, 128, 64, 64)
    return np.ascontiguousarray(np.concatenate([a, b], axis=3).reshape(8, 128, S))


def build_fused():
    P = Prog()
    T = {}
    I32 = mybir.dt.int32
    T["x"] = P.din("x", [TOWN, D], F32)
    T["cvec"] = P.din("cvec", [128, NCH], F32)
    T["norm_g0"] = P.din("norm_g0", [1, D], F32)
    T["norm_g1"] = P.din("norm_g1", [1, D], F32)
    T["ada_w0"] = P.din("ada_w0", [D, 3 * D], F32)
    T["ada_w1"] = P.din("ada_w1", [D, 3 * D], F32)
    T["ada_b0"] = P.din("ada_b0", [1, 3 * D], F32)
    T["ada_b1"] = P.din("ada_b1", [1, 3 * D], F32)
    T["w_in0"] = P.din("w_in0", [D, 5120], F32)
    T["qg"] = P.din("qg", [128, 1], F32)
    T["kg"] = P.din("kg", [128, 1], F32)
    T["ropeC"] = P.din("ropeC", [128, TOWN], F32)
    T["ropeS"] = P.din("ropeS", [128, TOWN], F32)
    T["w_out0"] = P.din("w_out0", [D, D], F32)
    T["w_in1"] = P.din("w_in1", [D, 4096], F32)
    T["w_out1"] = P.din("w_out1", [D, D], F32)
    T["final_g"] = P.din("final_g", [1, D], F32)
    T["sel"] = P.din("sel", [128, 2], F32)
    T["t_csw"] = P.din("t_csw", [128, 2, 512], BF16)
    T["t_ra"] = P.din("t_ra", [128, 256], BF16)
    T["t_rb"] = P.din("t_rb", [128, 256], BF16)
    T["t_ct"] = P.din("t_ct", [128, 128], BF16)
    T["t_st"] = P.din("t_st", [128, 128], BF16)
    T["t_re"] = P.din("t_re", [128, 128], BF16)
    T["t_rf"] = P.din("t_rf", [128, 128], BF16)
    T["out"] = P.dout("out", [TOWN, D], F32)
    T["QT"] = P.dint("QT", [16, 128, TOWN], BF16)
    T["SG"] = P.dint("SG", [16, 128, TOWN], BF16)
    T["gate_bc0"] = P.dint("gate_bc0", [128, D], F32)
    T["gate_bc1"] = P.dint("gate_bc1", [128, D], F32)
    T["KTown"] = [P.dint(f"KTown{j}", [256, TOWN], BF16) for j in range(2)]
    T["KTag"] = [P.dint(f"KTag{j}", [512, TOWN], BF16) for j in range(2)]
    T["Vown"] = [P.dint(f"Vown{j}", [2048, 512], BF16) for j in range(2)]
    T["Vag"] = [P.dint(f"Vag{j}", [4096, 512], BF16) for j in range(2)]
    T["OGT"] = P.dint("OGT", [16, 128, TOWN], BF16)
    T["x1"] = P.dint("x1", [TOWN, D], F32)
    T["UTown"] = P.dint("UTown", [8, 128, TOWN], BF16)
    T["UTsend"] = [P.dint(f"UTsend{j}", [256, TOWN], BF16) for j in range(4)]
    T["UTag"] = [P.dint(f"UTag{j}", [512, TOWN], BF16) for j in range(4)]
    T["SG1"] = P.dint("SG1", [TOWN, D], BF16)
    T["f_own"] = P.dint("f_own", [TOWN, 1024], BF16)
    T["fsend"] = [P.dint(f"fsend{j}", [1024, 1024], BF16) for j in range(4)]
    T["fag"] = [P.dint(f"fag{j}", [2048, 1024], BF16) for j in range(4)]
    C = emit_consts(P)
    emit_A(P, C, T)
    emit_B(P, C, T)
    emit_CF(P, C, T, False)
    emit_D(P, C, T)
    emit_E(P, C, T)
    emit_CF(P, C, T, True)
    return P.finish()


def core_tables(hf):
    tb = fourier_tables()
    q = (np.arange(128) + 64 * hf) % 128
    tb["t_ra"] = np.ascontiguousarray(tb["t_ra"][q])
    tb["t_rb"] = np.ascontiguousarray(tb["t_rb"][q])
    n = np.arange(128)
    k2 = ((n % 64) + 32 * hf) % 64
    col = (n // 64) * 64 + k2
    tb["t_re"] = np.ascontiguousarray(tb["t_re"][:, col])
    tb["t_rf"] = np.ascontiguousarray(tb["t_rf"][:, col])
    return tb


def kernel(x, c, norm_g, ada_w, ada_b, attn_w_in, attn_q_gain, attn_k_gain,
           attn_w_out, fourier_w_in, fourier_w_out, final_g):
    x = np.asarray(x)
    c = np.asarray(c)
    norm_g = np.asarray(norm_g)
    ada_w = np.asarray(ada_w)
    ada_b = np.asarray(ada_b)
    w_in = np.asarray(attn_w_in)[0]
    fw_in = np.asarray(fourier_w_in)[0]
    fw_out = np.asarray(fourier_w_out)[0]
    ropeC, ropeS = rope_tables()
    colperm = np.arange(5120)
    for h in range(20):
        colperm[h * 128:(h + 1) * 128] = h * 128 + PERM
    w_in_p = np.ascontiguousarray(w_in[:, colperm])
    consts = host_consts()
    qg = np.ascontiguousarray(np.asarray(attn_q_gain)[0][PERM].reshape(128, 1))
    kg = np.ascontiguousarray(np.asarray(attn_k_gain)[0][PERM].reshape(128, 1))
    per_hf = []
    for hf in range(2):
        ch = np.concatenate([np.arange(1024 * hf, 1024 * hf + 1024), np.arange(1024 * (1 - hf), 1024 * (1 - hf) + 1024)])
        d = dict(core_tables(hf))
        d["w_in1"] = np.ascontiguousarray(np.concatenate([fw_in[:, ch], fw_in[:, 2048 + ch]], axis=1))
        d["w_out1"] = np.ascontiguousarray(fw_out[ch, :])
        d["sel"] = np.ascontiguousarray(np.broadcast_to(np.array([float(hf), float(1 - hf)], np.float32), (128, 2)))
        d["ropeC"] = np.ascontiguousarray(ropeC[:, hf * TOWN:(hf + 1) * TOWN])
        d["ropeS"] = np.ascontiguousarray(ropeS[:, hf * TOWN:(hf + 1) * TOWN])
        per_hf.append(d)
    maps = []
    for core in range(NCORES):
        b, hf = core % 4, core // 4
        m = dict(consts)
        m.update(per_hf[hf])
        m["x"] = np.ascontiguousarray(x[b, hf * TOWN:(hf + 1) * TOWN])
        m["cvec"] = np.ascontiguousarray(c[b].reshape(NCH, 128).T)
        m["norm_g0"] = np.ascontiguousarray(norm_g[0:1])
        m["norm_g1"] = np.ascontiguousarray(norm_g[1:2])
        m["ada_w0"] = ada_w[0]
        m["ada_w1"] = ada_w[1]
        m["ada_b0"] = np.ascontiguousarray(ada_b[0:1])
        m["ada_b1"] = np.ascontiguousarray(ada_b[1:2])
        m["w_in0"] = w_in_p
        m["qg"] = qg
        m["kg"] = kg
        m["w_out0"] = np.asarray(attn_w_out)[0]
        m["final_g"] = np.ascontiguousarray(np.asarray(final_g).reshape(1, D))
        maps.append(m)
    res = run_bass_kernel_spmd(build_fused(), maps, core_ids=list(range(NCORES))).results
    out = np.empty((NB, S, D), np.float32)
    for core in range(NCORES):
        b, hf = core % 4, core // 4
        out[b, hf * TOWN:(hf + 1) * TOWN] = res[core]["out"]
    return out
```
